# Optimizing a Trainium2 kernel written in Bass

```python
import math
import jax, jax.numpy as jnp
from jax import lax
import numpy as np

D_MODEL = 1024
BATCH = 8
SEQ = 2048
DEPTH = 4

CHUNK = 64
N_A_LAYERS = DEPTH // 2
N_B_LAYERS = DEPTH - N_A_LAYERS
A_HEADS = 4
A_DV = D_MODEL // A_HEADS
A_DQK = A_DV // 2
A_QK_W = A_HEADS * A_DQK
A_V_W = A_HEADS * A_DV
A_PROJ = 2 * A_QK_W + 2 * A_V_W + 2 * A_HEADS
B_HEADS = 16
B_DH = D_MODEL // B_HEADS
B_KV_PROJ = 2 * D_MODEL + B_HEADS
Q_BLOCK = 128
D_FF = 4 * D_MODEL
EPS = 1e-6

kernel_name = "yoco_mlstm_fox_adaln_encoder"


def rms_norm(x, gain=None):
    xf = x.astype(jnp.float32)
    y = xf * lax.rsqrt(jnp.mean(xf * xf, axis=-1, keepdims=True) + EPS)
    if gain is not None:
        y = y * gain.astype(jnp.float32)
    return y.astype(x.dtype)


def modulate(x, shift, scale):
    return rms_norm(x) * (1 + scale[:, None, :]) + shift[:, None, :]


def mlstm_chunkwise(q, k, v, i_pre, logf):
    B, H, S, DQK = q.shape
    DV = v.shape[-1]
    nc = S // CHUNK

    def chunks(t):
        return jnp.moveaxis(t.reshape(B, H, nc, CHUNK, *t.shape[3:]), 2, 0)

    causal = jnp.asarray(np.tril(np.ones((CHUNK, CHUNK), dtype=bool)))

    def step(carry, xs):
        C, n, m = carry
        qc, kc, vc, ic, fc = xs
        b = jnp.cumsum(fc, axis=-1)
        d = b[..., :, None] - b[..., None, :] + ic[..., None, :]
        d = jnp.where(causal, d, -jnp.inf)
        inter = b + m[..., None]
        m_t = jnp.maximum(inter, jnp.max(d, axis=-1))
        w = jnp.exp(d - m_t[..., None])
        w_inter = jnp.exp(inter - m_t)
        a = w * jnp.einsum('bhtd,bhsd->bhts', qc, kc)
        num = (w_inter[..., None] * jnp.einsum('bhtd,bhde->bhte', qc, C)
               + jnp.einsum('bhts,bhse->bhte', a, vc))
        den = w_inter * jnp.einsum('bhtd,bhd->bht', qc, n) + jnp.sum(a, axis=-1)
        h = num / jnp.maximum(jnp.abs(den), jnp.exp(-m_t))[..., None]
        bL = b[..., -1]
        dl = bL[..., None] - b + ic
        m_new = jnp.maximum(bL + m, jnp.max(dl, axis=-1))
        wl = jnp.exp(dl - m_new[..., None])
        decay = jnp.exp(bL + m - m_new)
        C_new = decay[..., None, None] * C + jnp.einsum('bhs,bhsd,bhse->bhde', wl, kc, vc)
        n_new = decay[..., None] * n + jnp.einsum('bhs,bhsd->bhd', wl, kc)
        return (C_new, n_new, m_new), h

    init = (jnp.zeros((B, H, DQK, DV), jnp.float32),
            jnp.zeros((B, H, DQK), jnp.float32),
            jnp.zeros((B, H), jnp.float32))
    _, hs = lax.scan(step, init, (chunks(q), chunks(k), chunks(v), chunks(i_pre), chunks(logf)))
    return jnp.moveaxis(hs, 0, 2).reshape(B, H, S, DV)


def mlstm_mixer(h, w_in, b_i, b_f, head_gain, w_out):
    B, S, _ = h.shape
    proj = h @ w_in
    q, k, v, o, ig, fg = jnp.split(
        proj, [A_QK_W, 2 * A_QK_W, 2 * A_QK_W + A_V_W, 2 * A_QK_W + 2 * A_V_W,
               2 * A_QK_W + 2 * A_V_W + A_HEADS], axis=-1)
    q = q.reshape(B, S, A_HEADS, A_DQK).transpose(0, 2, 1, 3).astype(jnp.float32) * (A_DQK ** -0.5)
    k = k.reshape(B, S, A_HEADS, A_DQK).transpose(0, 2, 1, 3).astype(jnp.float32)
    v = v.reshape(B, S, A_HEADS, A_DV).transpose(0, 2, 1, 3).astype(jnp.float32)
    i_pre = (ig + b_i).astype(jnp.float32).transpose(0, 2, 1)
    logf = jax.nn.log_sigmoid((fg + b_f).astype(jnp.float32)).transpose(0, 2, 1)
    ht = mlstm_chunkwise(q, k, v, i_pre, logf)
    ht = ht * lax.rsqrt(jnp.mean(ht * ht, axis=-1, keepdims=True) + EPS)
    ht = ht * head_gain.astype(jnp.float32)[None, :, None, :]
    ht = ht.transpose(0, 2, 1, 3).reshape(B, S, A_V_W).astype(h.dtype)
    return (jax.nn.sigmoid(o) * ht) @ w_out


def fox_shared_kv(x, kv_gain, w_kv, fg_bias):
    B, S, _ = x.shape
    proj = rms_norm(x, kv_gain) @ w_kv
    k, v, fg = jnp.split(proj, [D_MODEL, 2 * D_MODEL], axis=-1)
    k = k.reshape(B, S, B_HEADS, B_DH).transpose(0, 2, 1, 3)
    v = v.reshape(B, S, B_HEADS, B_DH).transpose(0, 2, 1, 3)
    logf = jax.nn.log_sigmoid((fg + fg_bias).astype(jnp.float32)).transpose(0, 2, 1)
    F = jnp.cumsum(logf, axis=-1)
    return k, v, F


def fox_attention(q, k, v, F):
    S = q.shape[2]
    scale = B_DH ** -0.5
    outs = []
    for blk in range(S // Q_BLOCK):
        lo, hi = blk * Q_BLOCK, (blk + 1) * Q_BLOCK
        kb, vb = k[:, :, :hi], v[:, :, :hi]
        s = (jnp.einsum('bhtd,bhsd->bhts', q[:, :, lo:hi], kb).astype(jnp.float32) * scale
             + F[:, :, lo:hi, None] - F[:, :, None, :hi])
        mask = (lo + np.arange(Q_BLOCK))[:, None] >= np.arange(hi)[None, :]
        p = jax.nn.softmax(jnp.where(mask, s, -jnp.inf), axis=-1)
        outs.append(jnp.einsum('bhts,bhsd->bhtd', p.astype(vb.dtype), vb))
    return jnp.concatenate(outs, axis=2)


def fox_mixer(h, w_q, w_out, k, v, F):
    B, S, _ = h.shape
    q = (h @ w_q).reshape(B, S, B_HEADS, B_DH).transpose(0, 2, 1, 3)
    o = fox_attention(q, k, v, F)
    return o.transpose(0, 2, 1, 3).reshape(B, S, D_MODEL) @ w_out


def squared_relu_mlp(h, w1, w2):
    return jnp.square(jax.nn.relu(h @ w1)) @ w2


def setup_inputs(seed: int = 0) -> dict:
    key = jax.random.key(seed)
    ks = jax.random.split(key, 17)
    f32 = jnp.float32
    nrm = lambda k, shape: jax.random.normal(k, shape, f32)
    d = D_MODEL
    return {
        "x": nrm(ks[0], (BATCH, SEQ, d)),
        "c": nrm(ks[1], (BATCH, d)),
        "ada_w": nrm(ks[2], (DEPTH, d, 6 * d)) * (0.5 * d ** -0.5),
        "ada_b": nrm(ks[3], (DEPTH, 6 * d)) * 0.02,
        "a_w_in": nrm(ks[4], (N_A_LAYERS, d, A_PROJ)) * d ** -0.5,
        "a_b_i": nrm(ks[5], (N_A_LAYERS, A_HEADS)) * 0.1,
        "a_b_f": jnp.linspace(3.0, 6.0, A_HEADS, dtype=f32)[None, :] + 0.1 * nrm(ks[6], (N_A_LAYERS, A_HEADS)),
        "a_head_gain": 1.0 + 0.02 * nrm(ks[7], (N_A_LAYERS, A_HEADS, A_DV)),
        "a_w_out": nrm(ks[8], (N_A_LAYERS, A_V_W, d)) * A_V_W ** -0.5,
        "kv_gain": 1.0 + 0.02 * nrm(ks[9], (d,)),
        "b_w_kv": nrm(ks[10], (d, B_KV_PROJ)) * d ** -0.5,
        "b_fg_bias": jnp.linspace(1.0, 5.0, B_HEADS, dtype=f32) + 0.1 * nrm(ks[11], (B_HEADS,)),
        "b_w_q": nrm(ks[12], (N_B_LAYERS, d, d)) * d ** -0.5,
        "b_w_out": nrm(ks[13], (N_B_LAYERS, d, d)) * d ** -0.5,
        "mlp_w1": nrm(ks[14], (DEPTH, d, D_FF)) * d ** -0.5,
        "mlp_w2": nrm(ks[15], (DEPTH, D_FF, d)) * D_FF ** -0.5,
        "final_gain": 1.0 + 0.02 * nrm(ks[16], (d,)),
    }


def reference(x, c, ada_w, ada_b, a_w_in, a_b_i, a_b_f, a_head_gain, a_w_out,
              kv_gain, b_w_kv, b_fg_bias, b_w_q, b_w_out, mlp_w1, mlp_w2, final_gain):
    cond = jax.nn.silu(c)
    shared = None
    for l in range(DEPTH):
        mod = cond @ ada_w[l] + ada_b[l]
        sh1, sc1, g1, sh2, sc2, g2 = jnp.split(mod, 6, axis=-1)
        h = modulate(x, sh1, sc1)
        if l < N_A_LAYERS:
            y = mlstm_mixer(h, a_w_in[l], a_b_i[l], a_b_f[l], a_head_gain[l], a_w_out[l])
        else:
            if shared is None:
                shared = fox_shared_kv(x, kv_gain, b_w_kv, b_fg_bias)
            j = l - N_A_LAYERS
            y = fox_mixer(h, b_w_q[j], b_w_out[j], *shared)
        x = x + g1[:, None, :] * y
        h = modulate(x, sh2, sc2)
        x = x + g2[:, None, :] * squared_relu_mlp(h, mlp_w1[l], mlp_w2[l])
    return rms_norm(x, final_gain)
```

```python
import numpy as np
from contextlib import ExitStack
import concourse.bass as bass
import concourse.mybir as mybir
from concourse.bass_utils import run_bass_kernel_spmd

F32 = mybir.dt.float32
BF16 = mybir.dt.bfloat16
AF = mybir.ActivationFunctionType
ALU = mybir.AluOpType

S = 2048
D = 1024
KC = 8
NTB = 4
NSB = 16
DEPTH = 4
N_A = 2
A_H = 4
B_H = 16
EPS = 1e-6


class DSem:
    def __init__(self, sem, name):
        self.sem = sem
        self.val = 0
        self.name = name


class Sched:
    def __init__(self, nc, es):
        self.nc = nc
        self.es = es
        self.engs = dict(pe=nc.tensor, act=nc.scalar, dve=nc.vector, pool=nc.gpsimd, sp=nc.sync)
        self.prog = {e: [] for e in self.engs}
        self.pos = {e: 0 for e in self.engs}
        self.waited = {}
        self.last_w = {}
        self.readers = {}
        self.milestones = {e: set() for e in self.engs}
        self.esem = {}
        for e in ("pe", "act", "dve", "pool"):
            self.esem[e] = es.enter_context(nc.semaphore("es_" + e))
        self.dsems = {}
        self.final_tokens = []

    def dsem(self, name):
        if name not in self.dsems:
            self.dsems[name] = DSem(self.es.enter_context(self.nc.semaphore("ds_" + name)), name)
        return self.dsems[name]

    def _deps(self, eng, reads, writes, pe_accum):
        deps = []
        for r in reads:
            t = self.last_w.get(r)
            if t is not None:
                deps.append(t)
        for w in writes:
            t = self.last_w.get(w)
            if t is not None:
                if not (pe_accum and t[0] == "E" and t[1] == "pe"):
                    deps.append(t)
            deps.extend(self.readers.get(w, []))
        out = []
        for t in deps:
            key = (eng, t[0], t[1])
            if self.waited.get(key, 0) >= t[2]:
                continue
            self.waited[key] = t[2]
            out.append(t)
            if t[0] == "E":
                self.milestones[t[1]].add(t[2])
        return out

    def _commit(self, tok, reads, writes):
        for r in reads:
            self.readers.setdefault(r, []).append(tok)
        for w in writes:
            self.last_w[w] = tok
            self.readers[w] = []

    def op(self, eng, meth, kw, reads=(), writes=(), pe_accum=False, args=()):
        fn = (meth, args, kw)
        reads = list(reads)
        writes = list(writes)
        deps = self._deps(eng, reads, writes, pe_accum)
        self.pos[eng] += 1
        p = self.pos[eng]
        tok = ("E", eng, p)
        self.prog[eng].append(("op", deps, fn, p))
        self._commit(tok, reads, writes)
        return tok

    def dma(self, q, out, in_, ds, reads=(), writes=()):
        fn = ("dma_start", (), dict(out=out, in_=in_))
        reads = list(reads)
        writes = list(writes)
        deps = self._deps(q, reads, writes, False)
        ds.val += 16
        tok = ("D", ds.name, ds.val)
        self.prog[q].append(("dma", deps, fn, ds))
        self._commit(tok, reads, writes)
        return tok

    def fence(self, pool=False):
        toks = [("E", e, self.pos[e]) for e in ("pe", "act", "dve", "pool") if self.pos[e] > 0 and any(k == "op" for k, _, _, _ in self.prog[e])]
        for name, ds in self.dsems.items():
            if ds.val > 0 and not name.startswith("W"):
                toks.append(("D", name, ds.val))
        for e in (("pe", "act", "dve", "sp", "pool") if pool else ("pe", "act", "dve", "sp")):
            deps = []
            for t in toks:
                if t[0] == "E" and t[1] == e:
                    continue
                key = (e, t[0], t[1])
                if self.waited.get(key, 0) >= t[2]:
                    continue
                self.waited[key] = t[2]
                deps.append(t)
                if t[0] == "E":
                    self.milestones[t[1]].add(t[2])
            self.prog[e].append(("wait", deps, None, None))

    def wait_all(self, eng, toks):
        self.prog[eng].append(("wait", [t for t in toks], None, None))
        for t in toks:
            if t[0] == "E":
                self.milestones[t[1]].add(t[2])

    def emit(self):
        ms_rank = {}
        for e, ms in self.milestones.items():
            for i, p in enumerate(sorted(ms)):
                ms_rank[(e, p)] = i + 1
        engs = self.engs
        esem = self.esem
        dsems = self.dsems
        milestones = self.milestones

        def replay(ename, eobj):
            for kind, deps, fn, extra in self.prog[ename]:
                for t in deps:
                    if t[0] == "E":
                        eobj.wait_ge(esem[t[1]], ms_rank[(t[1], t[2])])
                    else:
                        eobj.wait_ge(dsems[t[1]].sem, t[2])
                if kind == "op":
                    ins = getattr(eobj, fn[0])(*fn[1], **fn[2])
                    if extra in milestones[ename]:
                        ins.then_inc(esem[ename], 1)
                elif kind == "dma":
                    getattr(eobj, fn[0])(*fn[1], **fn[2]).then_inc(extra.sem, 16)

        with self.nc.Block() as block:
            @block.tensor
            def _(e):
                replay("pe", e)

            @block.scalar
            def _(e):
                replay("act", e)

            @block.vector
            def _(e):
                replay("dve", e)

            @block.gpsimd
            def _(e):
                replay("pool", e)

            @block.sync
            def _(e):
                replay("sp", e)


def build_program(cfg):
    nc = bass.Bass("TRN2", target_bir_lowering=False)
    dt = nc.dram_tensor
    xT_d = dt("xT", [D, S], F32, kind="ExternalInput").ap()
    cT_d = dt("cT", [128, KC], F32, kind="ExternalInput").ap()
    adaw_d = dt("ada_w", [DEPTH, D, 6 * D], F32, kind="ExternalInput").ap()
    adab_d = dt("ada_bT", [128, DEPTH * 48], F32, kind="ExternalInput").ap()
    awin_d = dt("a_w_in", [N_A, D, 3080], F32, kind="ExternalInput").ap()
    abi_d = dt("a_b_iT", [A_H, N_A], F32, kind="ExternalInput").ap()
    abf_d = dt("a_b_fT", [A_H, N_A], F32, kind="ExternalInput").ap()
    ahg_d = dt("a_hgT", [128, N_A * 8], F32, kind="ExternalInput").ap()
    awout_d = dt("a_w_out", [N_A, D, D], F32, kind="ExternalInput").ap()
    kvg_d = dt("kv_gT", [128, KC], F32, kind="ExternalInput").ap()
    bwkv_d = dt("b_w_kv", [D, 2064], F32, kind="ExternalInput").ap()
    bfgb_d = dt("b_fgbT", [B_H, 1], F32, kind="ExternalInput").ap()
    bwq_d = dt("b_w_q", [2, D, D], F32, kind="ExternalInput").ap()
    bwout_d = dt("b_w_out", [2, D, D], F32, kind="ExternalInput").ap()
    w1_d = dt("mlp_w1", [DEPTH, D, 4 * D], F32, kind="ExternalInput").ap()
    w2_d = dt("mlp_w2", [DEPTH, 4 * D, D], F32, kind="ExternalInput").ap()
    fg_d = dt("fin_gT", [128, KC], F32, kind="ExternalInput").ap()
    cstf_d = dt("cstf", [128, 1024], F32, kind="ExternalInput").ap()
    cstb_d = dt("cstb", [128, 512], F32, kind="ExternalInput").ap()
    yT_d = dt("yT", [D, S], F32, kind="ExternalOutput").ap()
    kt_d = dt("kt_scr", [128, KC * S], BF16, kind="Internal").ap()
    v_d = dt("v_scr", [128, NSB * D], BF16, kind="Internal").ap()
    gk_d = dt("gk_scr", [B_H, 3, S], BF16, kind="Internal").ap()
    gq_d = dt("gq_scr", [B_H, 3, S], BF16, kind="Internal").ap()

    es = ExitStack()
    with es:
        def sb(name, shape, dtype):
            return es.enter_context(nc.sbuf_tensor(name, shape, dtype))

        XT = sb("XT", [128, KC, S], F32)
        HT = sb("HT", [128, KC, S], BF16)
        CSTF = sb("CSTF", [128, 1024], F32)
        CSTB = sb("CSTB", [128, 512], BF16)
        IDENT = CSTF[:, 0:128]
        E127 = CSTF[:, 128:256]
        ONESF = CSTF[:, 256:384]
        NEGMASK = CSTF[:, 896:1024]
        MASKB = CSTB[:, 0:128]
        ONESB = CSTB[:, 128:256]
        HALFB = [CSTB[:, 256:384], CSTB[:, 384:512]]
        MODC = sb("MODC", [128, DEPTH * 48], F32)
        ADAB = sb("ADAB", [128, DEPTH * 48], F32)
        CONDF = sb("CONDF", [128, KC], F32)
        CONDS = sb("CONDS", [128, KC], F32)
        CONDB = sb("CONDB", [128, KC], BF16)
        SMALL = sb("SMALL", [128, 64], F32)
        ABI = SMALL[0:4, 0:2]
        ABF = SMALL[0:4, 2:4]
        NABF = SMALL[0:4, 4:6]
        AHG = SMALL[:, 8:24]
        KVG = SMALL[:, 24:32]
        FING = SMALL[:, 32:40]
        FGB = SMALL[0:16, 40:41]
        NFGB = SMALL[0:16, 41:42]
        EPSC = SMALL[:, 48:49]
        ONEC = SMALL[:, 49:50]
        WA = [sb("WA%d" % i, [128, 4096], BF16) for i in range(2)]
        WB = [sb("WB%d" % i, [128, 4096], BF16) for i in range(2)]
        MIXW = 12800
        MIX = sb("MIX", [128, MIXW], F32)
        MIXB = MIX.bitcast(BF16)
        RS = [sb("RS%d" % i, [128, 512], F32) for i in range(2)]
        NTF = 6
        TF = [sb("TF%d" % i, [128, 512], F32) for i in range(NTF)]
        NTBT = 4
        TB = [sb("TB%d" % i, [128, 512], BF16) for i in range(NTBT)]
        CC = sb("CC", [128, 64], F32)
        PS = [es.enter_context(nc.psum_tensor("PS%d" % i, [128, 512], F32)) for i in range(8)]

        sc = Sched(nc, es)
        ADD_ENG = cfg.get('add_eng', 'pool')
        rot = {}

        def nxt(name, n):
            rot[name] = (rot.get(name, -1) + 1) % n
            return rot[name]

        def MM(out, lhsT, rhs, start, stop, reads, writes):
            return sc.op("pe", "matmul", dict(out=out, lhsT=lhsT, rhs=rhs, start=start, stop=stop), reads, writes, pe_accum=True)

        def TR(out, in_, ident, reads, writes):
            return sc.op("pe", "transpose", dict(out=out, in_=in_, identity=ident), reads, writes, pe_accum=True)

        def ACT(out, in_, func, reads, writes, bias=None, scale=None):
            kw = dict(out=out, in_=in_, func=func)
            if bias is not None:
                kw["bias"] = bias
            if scale is not None:
                kw["scale"] = scale
            return sc.op("act", "activation", kw, reads, writes)

        def TT(out, in0, in1, op, reads, writes):
            return sc.op("dve", "tensor_tensor", dict(out=out, in0=in0, in1=in1, op=op), reads, writes)

        def STT(out, in0, scalar, in1, op0, op1, reads, writes):
            return sc.op("dve", "scalar_tensor_tensor", dict(out=out, in0=in0, scalar=scalar, in1=in1, op0=op0, op1=op1), reads, writes)

        def TS(out, in0, s1, op0, reads, writes):
            return sc.op("dve", "tensor_scalar", dict(out=out, in0=in0, scalar1=s1, scalar2=None, op0=op0), reads, writes)

        def RECIP(out, in_, reads, writes):
            return sc.op("dve", "reciprocal", dict(out=out, in_=in_), reads, writes)

        def VCOPY(out, in_, reads, writes):
            return sc.op("dve", "tensor_copy", dict(out=out, in_=in_), reads, writes)

        def SCAN(out, d0, d1, op0, op1, reads, writes):
            return sc.op("dve", "tensor_tensor_scan", dict(out=out, data0=d0, data1=d1, initial=0.0, op0=op0, op1=op1), reads, writes)

        def wkeys(name, i):
            return ["%s%d_%d" % (name, i, q) for q in range(4)]

        def wload(slots, name, parts):
            i = nxt(name, len(slots))
            t = slots[i]
            keys = wkeys(name, i)
            ds = sc.dsem("%s%d" % (name, i))
            tok = None
            for n, (dv, src) in enumerate(parts):
                tok = sc.dma("pool", dv(t), src, ds, reads=[], writes=(keys if n == 0 else []))
            for k in keys:
                sc.last_w[k] = tok
            return t, keys

        def w_rows(src2d, r0, nr, c0, ncw):
            return src2d[r0:r0 + nr, c0:c0 + ncw].rearrange("(kc p) n -> p kc n", p=128)

        def sview(t, off, nk, ncw):
            return t[:, off:off + nk * ncw].rearrange("p (kc n) -> p kc n", kc=nk)

        def small_dma(out, in_, name, writes):
            sc.dma("sp", out, in_, sc.dsem(name), writes=writes)

        small_dma(CSTF[:], cstf_d, "cstf", ["CSTF"])
        sc.dma("pool", CSTB[:], cstb_d, sc.dsem("cstb"), writes=["CSTB"])
        small_dma(ADAB[:], adab_d, "adab", ["ADAB"])
        small_dma(CONDF[:], cT_d, "condf", ["CONDF"])
        small_dma(ABI, abi_d, "abi", ["ABI"])
        small_dma(ABF, abf_d, "abf", ["ABF"])
        small_dma(AHG, ahg_d, "ahg", ["AHG"])
        small_dma(KVG, kvg_d, "kvg", ["KVG"])
        small_dma(FING, fg_d, "fing", ["FING"])
        small_dma(FGB, bfgb_d, "fgb", ["FGB"])
        xkeys = lambda kc, tb: "XT%d_%d" % (kc, tb)
        hkeys = lambda kc, tb: "HT%d_%d" % (kc, tb)
        for kc in range(KC):
            sc.dma("sp", XT[:, kc, :], xT_d[kc * 128:(kc + 1) * 128, :], sc.dsem("x%d" % kc),
                   writes=[xkeys(kc, tb) for tb in range(NTB)])
        TS(NABF, ABF, -1.0, ALU.mult, ["ABF"], ["NABF"])
        TS(NFGB, FGB, -1.0, ALU.mult, ["FGB"], ["NFGB"])
        sc.op("dve", "memset", {}, [], ["EPSC"], args=(EPSC, EPS))
        sc.op("dve", "memset", {}, [], ["ONEC"], args=(ONEC, 1.0))
        ACT(CONDS[:], CONDF[:], AF.Sigmoid, ["CONDF"], ["CONDS"])
        TT(CONDB[:], CONDF[:], CONDS[:], ALU.mult, ["CONDF", "CONDS"], ["CONDB"])

        layers_used = sorted(set(l for (l, _, _) in cfg["layers"]))

        def mod_chunk(l, cg, wv, keys):
            for j in range(4):
                col = l * 48 + cg * 4 + j
                for kc in range(KC):
                    MM(PS[7][:, col:col + 1], wv[:, kc, j * 128:(j + 1) * 128], CONDB[:, kc:kc + 1],
                       kc == 0, kc == KC - 1, keys + ["CONDB"], ["PS7"])

        def mkeys(l, vs):
            return ["MODC%dv%d" % (l, v) for v in vs]

        def mod_finish_vec(l, v):
            a0 = l * 48 + v * 8
            TT(MODC[:, a0:a0 + 8], PS[7][:, a0:a0 + 8], ADAB[:, a0:a0 + 8], ALU.add, ["PS7", "ADAB"], mkeys(l, [v]))
            if v in (1, 4):
                TS(MODC[:, a0:a0 + 8], MODC[:, a0:a0 + 8], 1.0, ALU.add, mkeys(l, [v]), mkeys(l, [v]))

        def mod_finish(l):
            allk = mkeys(l, range(6))
            TT(MODC[:, l * 48:(l + 1) * 48], PS[7][:, l * 48:(l + 1) * 48], ADAB[:, l * 48:(l + 1) * 48], ALU.add,
               ["PS7", "ADAB"], allk)
            for v in (1, 4):
                a0 = l * 48 + v * 8
                TS(MODC[:, a0:a0 + 8], MODC[:, a0:a0 + 8], 1.0, ALU.add, mkeys(l, [v]), mkeys(l, [v]))

        mod_done = set()
        overlap_mod = cfg.get("overlap_mod", True)
        for l in (layers_used[:1] if overlap_mod else layers_used):
            for cg in range(12):
                t, keys = wload(WA, "WA", [(lambda t: sview(t, 0, 8, 512), w_rows(adaw_d[l], 0, D, cg * 512, 512))])
                mod_chunk(l, cg, sview(t, 0, 8, 512), keys)
                if cg % 2 == 1:
                    mod_finish_vec(l, cg // 2)
            mod_done.add(l)

        def modcol(l, v, kc):
            c = l * 48 + v * 8 + kc
            return MODC[:, c:c + 1]

        def tbs(tb):
            return slice(tb * 512, (tb + 1) * 512)

        def rstd_block(tb):
            for kc in range(KC):
                i = nxt("TB", NTBT)
                ACT(TB[i][:], XT[:, kc, tbs(tb)], AF.Square, [xkeys(kc, tb)], ["TB%d" % i])
                MM(PS[6][:], ONESB, TB[i][:], kc == 0, kc == KC - 1, ["TB%d" % i, "CSTB"], ["PS6"])
            r = nxt("RS", 2)
            ACT(RS[r][:], PS[6][:], AF.Ln, ["PS6", "EPSC"], ["RS%d" % r], bias=EPSC, scale=1.0 / D)
            ACT(RS[r][:], RS[r][:], AF.Exp, ["RS%d" % r], ["RS%d" % r], scale=-0.5)
            return r

        def make_norm(scale_col, shift_col, pkeys):
            def modulate(tb, r):
                for kc in range(KC):
                    i = nxt("TF", NTF)
                    STT(TF[i][:], XT[:, kc, tbs(tb)], scale_col(kc), RS[r][:], ALU.mult, ALU.mult,
                        [xkeys(kc, tb), "RS%d" % r] + pkeys, ["TF%d" % i])
                    if shift_col is not None:
                        ACT(HT[:, kc, tbs(tb)], TF[i][:], AF.Identity, ["TF%d" % i] + pkeys, [hkeys(kc, tb)],
                            bias=shift_col(kc), scale=1.0)
                    else:
                        ACT(HT[:, kc, tbs(tb)], TF[i][:], AF.Copy, ["TF%d" % i], [hkeys(kc, tb)])
            return rstd_block, modulate, ("std", scale_col, shift_col, pkeys)

        out_toks = []

        def final_norm_emitters():
            def modulate(tb, r):
                for kc in range(KC):
                    STT(XT[:, kc, tbs(tb)], XT[:, kc, tbs(tb)], FING[:, kc:kc + 1], RS[r][:], ALU.mult, ALU.mult,
                        [xkeys(kc, tb), "RS%d" % r, "FING"], [xkeys(kc, tb)])
                    out_toks.append(sc.dma("sp", yT_d[kc * 128:(kc + 1) * 128, tbs(tb)], XT[:, kc, tbs(tb)], sc.dsem("out"),
                                           reads=[xkeys(kc, tb)]))
            return rstd_block, modulate, ("final", None, None, [])

        def norm_pieces(em, tb, bank_fn, tmp_fn):
            kind, scale_col, shift_col, pkeys = em[2]
            st = {}
            pcs = []

            def stats_all():
                b = bank_fn()
                for kc in range(KC):
                    tq, tk_ = tmp_fn()
                    if kc % 2 == 0:
                        ACT(tq, XT[:, kc, tbs(tb)], AF.Square, [xkeys(kc, tb)], [tk_])
                    else:
                        TT(tq, XT[:, kc, tbs(tb)], XT[:, kc, tbs(tb)], ALU.mult, [xkeys(kc, tb)], [tk_])
                    MM(PS[b][:], ONESB, tq, kc == 0, kc == KC - 1, [tk_, "CSTB"], ["PS%d" % b])
                r = nxt("RS", 2)
                st["r"] = r
                ACT(RS[r][:], PS[b][:], AF.Ln, ["PS%d" % b, "EPSC"], ["RS%d" % r], bias=EPSC, scale=1.0 / D)
                ACT(RS[r][:], RS[r][:], AF.Exp, ["RS%d" % r], ["RS%d" % r], scale=-0.5)

            def mod(kc):
                r = st["r"]
                if kind == "final":
                    STT(XT[:, kc, tbs(tb)], XT[:, kc, tbs(tb)], FING[:, kc:kc + 1], RS[r][:], ALU.mult, ALU.mult,
                        [xkeys(kc, tb), "RS%d" % r, "FING"], [xkeys(kc, tb)])
                    out_toks.append(sc.dma("sp", yT_d[kc * 128:(kc + 1) * 128, tbs(tb)], XT[:, kc, tbs(tb)], sc.dsem("out"),
                                           reads=[xkeys(kc, tb)]))
                    return
                i = nxt("TF", NTF)
                STT(TF[i][:], XT[:, kc, tbs(tb)], scale_col(kc), RS[r][:], ALU.mult, ALU.mult,
                    [xkeys(kc, tb), "RS%d" % r] + pkeys, ["TF%d" % i])
                if shift_col is not None:
                    ACT(HT[:, kc, tbs(tb)], TF[i][:], AF.Identity, ["TF%d" % i] + pkeys, [hkeys(kc, tb)],
                        bias=shift_col(kc), scale=1.0)
                else:
                    ACT(HT[:, kc, tbs(tb)], TF[i][:], AF.Copy, ["TF%d" % i], [hkeys(kc, tb)])

            pcs.append(stats_all)
            for kc in range(KC):
                pcs.append(lambda kc=kc: mod(kc))
            return pcs

        def run_norm(em):
            stats, modulate = em[0], em[1]
            rr = {0: stats(0)}
            for tb in range(NTB):
                if tb + 1 < NTB:
                    rr[tb + 1] = stats(tb + 1)
                modulate(tb, rr[tb])

        def norm_to_HT(scale_col, shift_col, pkeys):
            run_norm(make_norm(scale_col, shift_col, pkeys))

        def norm_spec(l, which):
            if which == "mix":
                return make_norm(lambda kc: modcol(l, 1, kc), lambda kc: modcol(l, 0, kc), mkeys(l, [1, 0]))
            return make_norm(lambda kc: modcol(l, 4, kc), lambda kc: modcol(l, 3, kc), mkeys(l, [4, 3]))

        def acc_bank():
            return nxt("ACC", 2)

        def resid_update(b, l, gvec, dc, tb, mk):
            STT(XT[:, dc, tbs(tb)], PS[b][:], modcol(l, gvec, dc), XT[:, dc, tbs(tb)], ALU.mult, ALU.add,
                ["PS%d" % b, xkeys(dc, tb)] + mkeys(l, [gvec]), [xkeys(dc, tb)])

        def mlp(l, skip_norm=False, next_em=None):
            mk = ["MODC%d" % l]
            later = [x for x in layers_used if x > l and x not in mod_done]
            nl = later[0] if later else None
            if not skip_norm:
                run_norm(norm_spec(l, "mlp"))
            sc.fence(pool=(nl is not None))
            H1 = MIXB[:, 0:4 * S].rearrange("p (f t) -> p f t", f=4)
            MS = [MIXB[:, 8192 + i * 4096:8192 + (i + 1) * 4096] for i in range(2)]

            def ada_chunk(cg):
                i = nxt("MS", 2)
                keys = ["MS%d" % i]
                wv = MS[i].rearrange("p (kc n) -> p kc n", kc=8)
                sc.dma("pool", wv, w_rows(adaw_d[nl], 0, D, cg * 512, 512), sc.dsem("WMS%d" % i), reads=[], writes=keys)
                mod_chunk(nl, cg, wv, keys)

            for g in range(8):
                t1, k1 = wload(WA, "WA", [(lambda t: sview(t, 0, 8, 512), w_rows(w1_d[l], 0, D, g * 512, 512))])
                t2, k2 = wload(WB, "WB", [(lambda t: sview(t, 0, 4, 1024), w_rows(w2_d[l], g * 512, 512, 0, 1024))])
                w1v = sview(t1, 0, 8, 512)
                w2v = sview(t2, 0, 4, 1024)
                cgs = [c_ for c_ in (2 * g, 2 * g + 1) if c_ < 12] if nl is not None else []
                for tb in range(NTB):
                    for f in range(4):
                        b = nxt("ACC4", 4)
                        for kc in range(KC):
                            MM(PS[b][:], w1v[:, kc, f * 128:(f + 1) * 128], HT[:, kc, tbs(tb)], kc == 0, kc == KC - 1,
                               k1 + [hkeys(kc, tb)], ["PS%d" % b])
                        i = nxt("TF", NTF)
                        ACT(TF[i][:], PS[b][:], AF.Relu, ["PS%d" % b], ["TF%d" % i])
                        TT(H1[:, f, tbs(tb)], TF[i][:], TF[i][:], ALU.mult, ["TF%d" % i], ["H1_%d_%d" % (f, tb)])
                    if tb == 1 and cgs:
                        ada_chunk(cgs[0])
                for tb in range(NTB):
                    for dc in range(KC):
                        b = nxt("ACC4", 4)
                        for f in range(4):
                            MM(PS[b][:], w2v[:, f, dc * 128:(dc + 1) * 128], H1[:, f, tbs(tb)], f == 0, f == 3,
                               k2 + ["H1_%d_%d" % (f, tb)], ["PS%d" % b])
                        resid_update(b, l, 5, dc, tb, mk)
                    if tb == 1 and len(cgs) > 1:
                        ada_chunk(cgs[1])
                    if g == 7 and next_em is not None:
                        next_em[1](tb, next_em[0](tb))
                if g == 6 and nl is not None:
                    mod_finish(nl)
                    mod_done.add(nl)

        def mlstm(l, skip_norm=False, next_em=None):
            mk = ["MODC%d" % l]
            if not skip_norm:
                run_norm(norm_spec(l, "mix"))
            sc.fence()
            T_i = MIX[0:4, 0:2048]
            T_l = MIX[0:4, 2048:4096]
            T_F = MIX[0:4, 4096:6144]
            T_m = MIX[0:4, 6144:8192]
            EBt = MIX[:, 0:2048]
            ABt = MIX[:, 2048:4096]
            QH = MIXB[:, 16384:18432]
            KH = MIXB[:, 18432:20480]
            VH = MIXB[:, 20480:24576].rearrange("p (s e) -> p s e", s=16)
            AHt = MIXB[:, 24576:25600].rearrange("p (a t) -> p a t", a=2)
            rk = lambda r, tb: "R%d_%d" % (r, tb)
            allr = lambda r: [rk(r, tb) for tb in range(NTB)]
            tg, kg = wload(WA, "WA", [(lambda t: sview(t, 0, 8, 8), w_rows(awin_d[l], 0, D, 3072, 8))])
            wg = sview(tg, 0, 8, 8)
            for tb in range(NTB):
                b = acc_bank()
                for kc in range(KC):
                    MM(PS[b][0:4, :], wg[:, kc, 0:4], HT[:, kc, tbs(tb)], kc == 0, kc == KC - 1, kg + [hkeys(kc, tb)], ["PS%d" % b])
                ACT(T_i[:, tbs(tb)], PS[b][0:4, :], AF.Identity, ["PS%d" % b, "ABI"], [rk(0, tb)], bias=ABI[:, l:l + 1], scale=1.0)
                b = acc_bank()
                for kc in range(KC):
                    MM(PS[b][0:4, :], wg[:, kc, 4:8], HT[:, kc, tbs(tb)], kc == 0, kc == KC - 1, kg + [hkeys(kc, tb)], ["PS%d" % b])
                ACT(T_l[:, tbs(tb)], PS[b][0:4, :], AF.Exp, ["PS%d" % b, "NABF"], [rk(1, tb)], bias=NABF[:, l:l + 1], scale=-1.0)
            ACT(T_l, T_l, AF.Ln, allr(1) + ["ONEC"], allr(1), bias=ONEC[0:4, :], scale=1.0)
            TS(T_l, T_l, -1.0, ALU.mult, allr(1), allr(1))
            SCAN(T_F, T_l, T_l, ALU.add, ALU.min, allr(1), allr(2))
            SCAN(T_m, T_l, T_i, ALU.add, ALU.max, allr(1) + allr(0), allr(3))
            TT(T_i, T_i, T_F, ALU.subtract, allr(0) + allr(2), allr(0))
            TT(T_F, T_F, T_m, ALU.subtract, allr(2) + allr(3), allr(2))
            TS(T_m, T_m, -1.0, ALU.mult, allr(3), allr(3))
            for J in range(NSB):
                TR(PS[7][:, J * 4:J * 4 + 4], T_i[:, J * 128:(J + 1) * 128], CSTF[0:4, 0:4], [rk(0, J // 4), "CSTF"], ["PS7"])
            ACT(CC[:, 0:64], PS[7][:, 0:64], AF.Copy, ["PS7"], ["CC"])
            X8 = [MIX[:, k * 512:(k + 1) * 512] for k in range(8)]
            ABd, EBd = X8[0:2], X8[2:4]
            N0t, N1t, DAt, RSt = X8[4], X8[5], X8[6], X8[7]
            x8k = ["X8_%d" % k for k in range(8)]
            sc.fence()
            PFm = cfg.get("ml_pf", 3)
            NPTm = 4

            def acc2():
                return (0, 7)[nxt("ACC2", 2)]

            def head_weights(h):
                tA, kA = wload(WA, "WA", [
                    (lambda t: sview(t, 0, 8, 512)[:, :, 0:128], w_rows(awin_d[l], 0, D, h * 128, 128)),
                    (lambda t: sview(t, 0, 8, 512)[:, :, 128:256], w_rows(awin_d[l], 0, D, 512 + h * 128, 128)),
                    (lambda t: sview(t, 0, 8, 512)[:, :, 256:512], w_rows(awin_d[l], 0, D, 1024 + h * 256, 256))])
                tB, kB = wload(WB, "WB", [
                    (lambda t: sview(t, 0, 8, 256), w_rows(awin_d[l], 0, D, 2048 + h * 256, 256)),
                    (lambda t: sview(t, 2048, 2, 1024), w_rows(awout_d[l], h * 256, 256, 0, 1024))])
                return tA, kA, tB, kB

            hw = {0: head_weights(0)}
            bgm = []

            def sq_tmp_m():
                i = nxt("TF", NTF)
                return TF[i].bitcast(BF16)[:, 0:512], "TF%d" % i
            for h in range(A_H):
                tA, kA, tB, kB = hw[h]
                if h + 1 < A_H:
                    hw[h + 1] = head_weights(h + 1)
                wA = sview(tA, 0, 8, 512)
                wo = sview(tB, 0, 8, 256)
                wout = sview(tB, 2048, 2, 1024)
                oh = CSTF[0:4, 384 + h * 128:384 + (h + 1) * 128]

                def gen_ab(tb, h=h, oh=oh):
                    d = tb % 2
                    b = acc2()
                    MM(PS[b][:], oh, T_F[:, tbs(tb)], True, True, ["CSTF", rk(2, tb)], ["PS%d" % b])
                    ACT(ABd[d], PS[b][:], AF.Copy, ["PS%d" % b], [x8k[d]])
                    b = acc2()
                    MM(PS[b][:], oh, T_m[:, tbs(tb)], True, True, ["CSTF", rk(3, tb)], ["PS%d" % b])
                    ACT(EBd[d], PS[b][:], AF.Exp, ["PS%d" % b], [x8k[2 + d]])

                for tb in range(NTB):
                    b = acc2()
                    for kc in range(KC):
                        MM(PS[b][:], wA[:, kc, 0:128], HT[:, kc, tbs(tb)], kc == 0, kc == KC - 1, kA + [hkeys(kc, tb)], ["PS%d" % b])
                    ACT(QH[:, tbs(tb)], PS[b][:], AF.Identity, ["PS%d" % b], ["QH%d" % tb], scale=float(128 ** -0.5))
                    b = acc2()
                    for kc in range(KC):
                        MM(PS[b][:], wA[:, kc, 128:256], HT[:, kc, tbs(tb)], kc == 0, kc == KC - 1, kA + [hkeys(kc, tb)], ["PS%d" % b])
                    VCOPY(KH[:, tbs(tb)], PS[b][:], ["PS%d" % b], ["KH%d" % tb])
                for sbk in range(NSB):
                    b = acc2()
                    for kc in range(KC):
                        MM(PS[b][:, 0:256], HT[:, kc, sbk * 128:(sbk + 1) * 128], wA[:, kc, 256:512], kc == 0, kc == KC - 1,
                           kA + [hkeys(kc, sbk // 4)], ["PS%d" % b])
                    if sbk % 2 == 0:
                        VCOPY(VH[:, sbk, :], PS[b][:, 0:256], ["PS%d" % b], ["VH%d" % sbk])
                    else:
                        ACT(VH[:, sbk, :], PS[b][:, 0:256], AF.Copy, ["PS%d" % b], ["VH%d" % sbk])
                gen_ab(0)

                iters = [(tb, J) for tb in range(NTB) for J in range(4 * tb + 4)]
                state = {}
                pending = []

                def stageA(k, h=h):
                    tb, J = iters[k]
                    for _ in range(2):
                        if bgm:
                            bgm.pop(0)()
                    if J == PFm and tb + 1 < NTB:
                        gen_ab(tb + 1)
                    d = tb % 2
                    n0 = max(0, J - 4 * tb)
                    c0 = n0 * 128
                    st = 1 + nxt("ST", 3)
                    MM(PS[st][:, c0:512], KH[:, J * 128:(J + 1) * 128], QH[:, tb * 512 + c0:(tb + 1) * 512], True, True,
                       ["KH%d" % (J // 4), "QH%d" % tb], ["PS%d" % st])
                    wi = nxt("TF", NTF)
                    ACT(TF[wi][:, c0:512], ABd[d][:, c0:512], AF.Exp, [x8k[d], "CC"], ["TF%d" % wi],
                        bias=CC[:, J * 4 + h:J * 4 + h + 1], scale=1.0)
                    ai = nxt("TB", NPTm)
                    TT(TB[ai][:, c0:512], PS[st][:, c0:512], TF[wi][:, c0:512], ALU.mult, ["PS%d" % st, "TF%d" % wi], ["TB%d" % ai])
                    if J >= 4 * tb:
                        TT(TB[ai][:, c0:c0 + 128], TB[ai][:, c0:c0 + 128], MASKB, ALU.mult, ["TB%d" % ai, "CSTB"], ["TB%d" % ai])
                    state[k] = (ai, c0)

                def post2(tb, h=h, wo=wo, wout=wout, kB=kB):
                    ACT(DAt, DAt, AF.Ln, [x8k[6]], [x8k[6]])
                    ACT(DAt, DAt, AF.Exp, [x8k[6]], [x8k[6]], scale=-1.0)
                    sq = []
                    for e2, Nt, nk in ((0, N0t, x8k[4]), (1, N1t, x8k[5])):
                        sc.op("pool", "tensor_tensor", dict(out=Nt, in0=Nt, in1=DAt, op=ALU.mult), [nk, x8k[6]], [nk])
                        si = nxt("TF", NTF)
                        sc.op("pool", "tensor_tensor", dict(out=TF[si][:], in0=Nt, in1=Nt, op=ALU.mult), [nk], ["TF%d" % si])
                        sq.append(si)
                    b = acc2()
                    for e2 in range(2):
                        MM(PS[b][:], ONESF, TF[sq[e2]][:], e2 == 0, e2 == 1, ["CSTF", "TF%d" % sq[e2]], ["PS%d" % b])
                    ACT(RSt, PS[b][:], AF.Ln, ["PS%d" % b, "EPSC"], [x8k[7]], bias=EPSC, scale=1.0 / 256)
                    ACT(RSt, RSt, AF.Exp, [x8k[7]], [x8k[7]], scale=-0.5)
                    for e2, Nt, nk in ((0, N0t, x8k[4]), (1, N1t, x8k[5])):
                        b = acc2()
                        for kc in range(KC):
                            MM(PS[b][:], wo[:, kc, e2 * 128:(e2 + 1) * 128], HT[:, kc, tbs(tb)], kc == 0, kc == KC - 1,
                               kB + [hkeys(kc, tb)], ["PS%d" % b])
                        gi = nxt("TF", NTF)
                        ACT(TF[gi][:], PS[b][:], AF.Sigmoid, ["PS%d" % b], ["TF%d" % gi])
                        gc = l * 8 + h * 2 + e2
                        STT(Nt, Nt, AHG[:, gc:gc + 1], RSt, ALU.mult, ALU.mult, [nk, "AHG", x8k[7]], [nk])
                        sc.op("pool", "tensor_tensor", dict(out=AHt[:, e2, :], in0=Nt, in1=TF[gi][:], op=ALU.mult),
                              [nk, "TF%d" % gi], ["AH%d" % e2])
                    for dc in range(KC):
                        b = acc2()
                        MM(PS[b][:], wout[:, 0, dc * 128:(dc + 1) * 128], AHt[:, 0, :], True, False, kB + ["AH0"], ["PS%d" % b])
                        MM(PS[b][:], wout[:, 1, dc * 128:(dc + 1) * 128], AHt[:, 1, :], False, True, kB + ["AH1"], ["PS%d" % b])
                        resid_update(b, l, 2, dc, tb, mk)
                    if h == A_H - 1 and next_em is not None:
                        bgm.extend(norm_pieces(next_em, tb, acc2, sq_tmp_m))

                def stageB(k):
                    tb, J = iters[k]
                    ai, c0 = state.pop(k)
                    last = 4 * tb + 3
                    MM(PS[4][:, c0:512], VH[:, J, 0:128], TB[ai][:, c0:512], J == 0, J == last, ["VH%d" % J, "TB%d" % ai], ["PS4"])
                    MM(PS[5][:, c0:512], VH[:, J, 128:256], TB[ai][:, c0:512], J == 0, J == last, ["VH%d" % J, "TB%d" % ai], ["PS5"])
                    MM(PS[6][:, c0:512], ONESB, TB[ai][:, c0:512], J == 0, J == last, ["CSTB", "TB%d" % ai], ["PS6"])
                    if J == last:
                        ACT(DAt, PS[6][:], AF.Abs, ["PS6"], [x8k[6]])
                        VCOPY(N0t, PS[4][:], ["PS4"], [x8k[4]])
                        ACT(N1t, PS[5][:], AF.Copy, ["PS5"], [x8k[5]])
                        TT(DAt, DAt, EBd[tb % 2], ALU.max, [x8k[6], x8k[2 + tb % 2]], [x8k[6]])
                        pending.append([3, tb])
                    for p in pending:
                        p[0] -= 1
                    while pending and pending[0][0] <= 0:
                        _, a = pending.pop(0)
                        post2(a)

                n = len(iters)
                for k in range(min(PFm, n)):
                    stageA(k)
                for k in range(n):
                    if k + PFm < n:
                        stageA(k + PFm)
                    stageB(k)
                while pending:
                    _, a = pending.pop(0)
                    post2(a)
                while bgm:
                    bgm.pop(0)()

        def kv_norm_em():
            return make_norm(lambda kc: KVG[:, kc:kc + 1], None, ["KVG"])

        def fox_kv(skip_norm=False):
            if not skip_norm:
                run_norm(kv_norm_em())
            sc.fence()
            FGr = MIX[0:16, 0:2048]
            Gr = MIX[0:16, 2048:4096]
            KTs = [MIXB[:, 8192:10240], MIXB[:, 10240:12288]]
            tf_, kf = wload(WA, "WA", [(lambda t: sview(t, 0, 8, 16), w_rows(bwkv_d, 0, D, 2048, 16))])
            wf = sview(tf_, 0, 8, 16)
            for tb in range(NTB):
                b = acc_bank()
                for kc in range(KC):
                    MM(PS[b][0:16, :], wf[:, kc, 0:16], HT[:, kc, tbs(tb)], kc == 0, kc == KC - 1, kf + [hkeys(kc, tb)], ["PS%d" % b])
                ACT(FGr[:, tbs(tb)], PS[b][0:16, :], AF.Exp, ["PS%d" % b, "NFGB"], ["FGr"], bias=NFGB, scale=-1.0)
            ACT(FGr, FGr, AF.Ln, ["FGr", "ONEC"], ["FGr"], bias=ONEC[0:16, :], scale=1.0)
            SCAN(Gr, FGr, FGr, ALU.add, ALU.max, ["FGr"], ["Gr"])
            GHt = [MIXB[0:16, 12288 + k * 2048:12288 + (k + 1) * 2048] for k in range(3)]
            NGt = [MIXB[0:16, 18432 + k * 2048:18432 + (k + 1) * 2048] for k in range(3)]
            R1 = FGr
            VCOPY(GHt[0], Gr, ["Gr"], ["GH0"])
            TT(R1, Gr, GHt[0], ALU.subtract, ["Gr", "GH0", "FGr"], ["FGr"])
            VCOPY(GHt[1], R1, ["FGr"], ["GH1"])
            TT(R1, R1, GHt[1], ALU.subtract, ["FGr", "GH1"], ["FGr"])
            VCOPY(GHt[2], R1, ["FGr"], ["GH2"])
            for k in range(3):
                TS(NGt[k], GHt[k], -1.0, ALU.mult, ["GH%d" % k], ["NG%d" % k])
                sc.dma("sp", gk_d[:, k, :], GHt[k], sc.dsem("gh%d" % k), reads=["GH%d" % k], writes=["GKD"])
                sc.dma("sp", gq_d[:, k, :], NGt[k], sc.dsem("ng%d" % k), reads=["NG%d" % k], writes=["GQD"])
            for c in range(KC):
                tk, kk = wload(WA, "WA", [(lambda t: sview(t, 0, 8, 128), w_rows(bwkv_d, 0, D, c * 128, 128))])
                wk = sview(tk, 0, 8, 128)
                kt = KTs[c % 2]
                for tb in range(NTB):
                    b = acc_bank()
                    for kc in range(KC):
                        MM(PS[b][:], wk[:, kc, :], HT[:, kc, tbs(tb)], kc == 0, kc == KC - 1, kk + [hkeys(kc, tb)], ["PS%d" % b])
                    ACT(kt[:, tbs(tb)], PS[b][:], AF.Identity, ["PS%d" % b], ["KTs%d" % (c % 2)], scale=0.125)
                sc.dma("sp", kt_d[:, c * S:(c + 1) * S], kt, sc.dsem("kts%d" % (c % 2)), reads=["KTs%d" % (c % 2)], writes=["KTD%d" % c])
            vdv = v_d.rearrange("p (c s n) -> p c s n", c=8, s=16)
            for half in range(2):
                tv, kv = wload(WB, "WB", [(lambda t: sview(t, 0, 8, 512), w_rows(bwkv_d, 0, D, 1024 + half * 512, 512))])
                wv = sview(tv, 0, 8, 512)
                for sbk in range(NSB):
                    b = acc_bank()
                    for kc in range(KC):
                        MM(PS[b][:], HT[:, kc, sbk * 128:(sbk + 1) * 128], wv[:, kc, :], kc == 0, kc == KC - 1,
                           kv + [hkeys(kc, sbk // 4)], ["PS%d" % b])
                    vi = nxt("TB", NTBT)
                    VCOPY(TB[vi][:], PS[b][:], ["PS%d" % b], ["TB%d" % vi])
                    sc.dma("sp", vdv[:, half * 4:(half + 1) * 4, sbk, :], TB[vi][:].rearrange("p (c n) -> p c n", c=4),
                           sc.dsem("tb%d" % vi), reads=["TB%d" % vi], writes=["VD%d_%d" % (half, sbk)])

        def fox(l, j, skip_norm=False, next_em=None):
            mk = ["MODC%d" % l]
            if not skip_norm:
                run_norm(norm_spec(l, "mix"))
            sc.fence()
            KX = [[MIXB[:, (2 * ks + hh) * 2048:(2 * ks + hh + 1) * 2048] for hh in range(2)] for ks in range(2)]
            VX = [[MIXB[:, 8192 + (2 * ks + hh) * 2048:8192 + (2 * ks + hh + 1) * 2048].rearrange("p (s n) -> p s n", s=16)
                   for hh in range(2)] for ks in range(2)]
            QX = [[MIXB[:, 16384 + (2 * ks + hh) * 512:16384 + (2 * ks + hh + 1) * 512] for hh in range(2)] for ks in range(2)]
            OTS = [MIXB[:, 18432 + c * 512:18432 + (c + 1) * 512] for c in range(KC)]
            NPT = 8
            PT = [TB[k][:] for k in range(4)] + [MIXB[:, 22528 + k * 512:22528 + (k + 1) * 512] for k in range(4)]
            AUG = [64, 0]
            ptk = ["TB%d" % k for k in range(4)] + ["PTm%d" % k for k in range(4)]
            NSF = 5
            SF = [TF[k][:] for k in range(5)]
            LT = TF[5]
            PF = cfg.get("fox_pf", 3)
            vdkeys = ["VD%d_%d" % (hf, s_) for hf in range(2) for s_ in range(NSB)]
            for ks in range(2):
                for hh in range(2):
                    o = 64 * (1 - hh)
                    sc.op("dve", "memset", {}, [], ["VX%d_%d" % (ks, hh)], args=(VX[ks][hh][:, :, o:o + 64], 0.0))
                    sc.op("dve", "memset", {}, [], ["QXz%d_%d" % (ks, hh)], args=(QX[ks][hh][o:o + 64, :], 0.0))
                    sc.op("dve", "memset", {}, [], ["QXz%d_%d" % (ks, hh)], args=(QX[ks][hh][AUG[hh]:AUG[hh] + 6, :], 1.0))
                    sc.op("dve", "memset", {}, [], ["KXz%d_%d" % (ks, hh)], args=(KX[ks][hh][:, :], 0.0))
                    sc.op("dve", "memset", {}, [], ["KXz%d_%d" % (ks, hh)], args=(KX[ks][hh][AUG[hh]:AUG[hh] + 6, :], 1.0))
            wq_s, wo_s = [], []
            for i in range(2):
                t, k = wload(WA, "WA", [(lambda t: sview(t, 0, 8, 512), w_rows(bwq_d[j], 0, D, i * 512, 512))])
                wq_s.append((sview(t, 0, 8, 512), k))
            for i in range(2):
                t, k = wload(WB, "WB", [(lambda t: sview(t, 0, 4, 1024), w_rows(bwout_d[j], i * 512, 512, 0, 1024))])
                wo_s.append((sview(t, 0, 4, 1024), k))

            units = [(tb, c) for tb in range(NTB) for c in range(KC)]

            def loads(u):
                tb, c = units[u]
                ks = u % 2
                nJ = 4 * tb + 4
                for hh in range(2):
                    h = 2 * c + hh
                    pp = slice(hh * 64, (hh + 1) * 64)
                    sc.dma("sp", KX[ks][hh][pp, 0:nJ * 128], kt_d[pp, c * S:c * S + nJ * 128], sc.dsem("kx%d_%d" % (ks, hh)),
                           reads=["KTD%d" % c, "KXz%d_%d" % (ks, hh)], writes=["KX%d_%d" % (ks, hh)])
                    a0 = AUG[hh]
                    sc.dma("sp", KX[ks][hh][a0:a0 + 3, 0:nJ * 128], gk_d[h, :, 0:nJ * 128], sc.dsem("kxa%d_%d" % (ks, hh)),
                           reads=["GKD", "KXz%d_%d" % (ks, hh)], writes=["KXa%d_%d" % (ks, hh)])
                    sc.dma("sp", QX[ks][hh][a0 + 3:a0 + 6, :], gq_d[h, :, tb * 512:(tb + 1) * 512], sc.dsem("qxa%d_%d" % (ks, hh)),
                           reads=["GQD", "QXz%d_%d" % (ks, hh)], writes=["QXa%d_%d" % (ks, hh)])
                vsrc = v_d[:, c * 2048:(c + 1) * 2048].rearrange("p (s n) -> p s n", s=16)
                for hh in range(2):
                    sc.dma("sp", VX[ks][hh][:, 0:nJ, hh * 64:(hh + 1) * 64], vsrc[:, 0:nJ, hh * 64:(hh + 1) * 64],
                           sc.dsem("vx%d_%d" % (ks, hh)), reads=vdkeys, writes=["VX%d_%d" % (ks, hh)])

            def qproj(u):
                tb, c = units[u]
                ks = u % 2
                wq, kq = wq_s[c // 4]
                cc = (c % 4) * 128
                for kc in range(KC):
                    MM(PS[0][:], wq[:, kc, cc:cc + 128], HT[:, kc, tbs(tb)], kc == 0, kc == KC - 1, kq + [hkeys(kc, tb)], ["PS0"])
                ACT(QX[ks][0][0:64, :], PS[0][0:64, :], AF.Copy, ["PS0"], ["QX%d_0" % ks])
                ACT(QX[ks][1][64:128, :], PS[0][64:128, :], AF.Copy, ["PS0"], ["QX%d_1" % ks])

            iters = []
            for u, (tb, c) in enumerate(units):
                n_u = 2 * (4 * tb + 4)
                li = 0
                for J in range(4 * tb + 4):
                    for hh in range(2):
                        iters.append((u, J, hh, li))
                        li += 1
            state = {}
            pending = []

            bg = []

            def sq_tmp():
                i = nxt("SF", NSF)
                return TF[i].bitcast(BF16)[:, 0:512], "TF%d" % i

            def stageA(k):
                u, J, hh, li = iters[k]
                tb, c = units[u]
                ks = u % 2
                if bg:
                    bg.pop(0)()
                if li == 1 and u + 1 < len(units):
                    qproj(u + 1)
                if li == PF + 2 and u + 1 < len(units):
                    loads(u + 1)
                n0 = max(0, J - 4 * tb)
                c0 = n0 * 128
                nb = 4 - n0
                st = 1 + nxt("ST", 3)
                MM(PS[st][:, c0:512], KX[ks][hh][:, J * 128:(J + 1) * 128], QX[ks][hh][:, c0:512], True, True,
                   ["KX%d_%d" % (ks, hh), "KXa%d_%d" % (ks, hh), "KXz%d_%d" % (ks, hh),
                    "QX%d_%d" % (ks, hh), "QXa%d_%d" % (ks, hh), "QXz%d_%d" % (ks, hh)], ["PS%d" % st])
                pi = nxt("PT", NPT)
                if J >= 4 * tb:
                    sf = nxt("SF", NSF)
                    TT(SF[sf][:, 0:128], PS[st][:, c0:c0 + 128], NEGMASK, ALU.add, ["PS%d" % st, "CSTF"], ["TF%d" % sf])
                    ACT(PT[pi][:, c0:c0 + 128], SF[sf][:, 0:128], AF.Exp, ["TF%d" % sf], [ptk[pi]])
                    if nb > 1:
                        ACT(PT[pi][:, c0 + 128:512], PS[st][:, c0 + 128:512], AF.Exp, ["PS%d" % st], [ptk[pi]])
                else:
                    ACT(PT[pi][:, c0:512], PS[st][:, c0:512], AF.Exp, ["PS%d" % st], [ptk[pi]])
                state[k] = (pi, c0)

            def outproj(tb):
                for dc in range(KC):
                    for c in range(KC):
                        wo, ko = wo_s[c // 4]
                        MM(PS[0][:], wo[:, c % 4, dc * 128:(dc + 1) * 128], OTS[c], c == 0, c == KC - 1, ko + ["OTS%d" % c], ["PS0"])
                    resid_update(0, l, 2, dc, tb, mk)
                if next_em is not None:
                    bg.extend(norm_pieces(next_em, tb, lambda: 1 + nxt("ST", 3), sq_tmp))

            def stageB(k):
                u, J, hh, li = iters[k]
                tb, c = units[u]
                ks = u % 2
                pi, c0 = state.pop(k)
                last = 4 * tb + 3
                first = (J == 0 and hh == 0)
                final = (J == last and hh == 1)
                bo = 4 + 2 * (u % 2)
                MM(PS[bo][:, c0:512], VX[ks][hh][:, J, :], PT[pi][:, c0:512], first, final,
                   ["VX%d_%d" % (ks, hh), ptk[pi]], ["PS%d" % bo])
                MM(PS[bo + 1][:, c0:512], HALFB[hh], PT[pi][:, c0:512], first, final, ["CSTB", ptk[pi]], ["PS%d" % (bo + 1)])
                if final:
                    RECIP(LT[:], PS[bo + 1][:], ["PS%d" % (bo + 1)], ["TF5"])
                    TT(OTS[c], PS[bo][:], LT[:], ALU.mult, ["PS%d" % bo, "TF5"], ["OTS%d" % c])
                    if c == KC - 1:
                        pending.append([3, tb])
                for p in pending:
                    p[0] -= 1
                while pending and pending[0][0] <= 0:
                    _, a = pending.pop(0)
                    outproj(a)

            loads(0)
            qproj(0)
            n = len(iters)
            for k in range(min(PF, n)):
                stageA(k)
            for k in range(n):
                if k + PF < n:
                    stageA(k + PF)
                stageB(k)
            while pending:
                _, a = pending.pop(0)
                outproj(a)
            while bg:
                bg.pop(0)()

        sc.fence()
        phases = []
        kv_done = False
        for (l, do_mixer, do_mlp) in cfg["layers"]:
            if do_mixer:
                if l < N_A:
                    phases.append(("mlstm", l))
                else:
                    if not kv_done:
                        phases.append(("kv", l))
                        kv_done = True
                    phases.append(("fox", l))
            if do_mlp:
                phases.append(("mlp", l))
        if cfg.get("final_norm", True):
            phases.append(("final", None))
        fuse_norm = cfg.get("fuse_norm", True)
        prenormed = False
        for pi_, (kind, l) in enumerate(phases):
            if l is not None and l not in mod_done:
                for cg in range(12):
                    t, keys = wload(WA, "WA", [(lambda t: sview(t, 0, 8, 512), w_rows(adaw_d[l], 0, D, cg * 512, 512))])
                    mod_chunk(l, cg, sview(t, 0, 8, 512), keys)
                mod_finish(l)
                mod_done.add(l)
            skip = prenormed
            prenormed = False
            mix_next = None
            if fuse_norm and cfg.get("fuse_" + kind, kind == "fox") and kind in ("mlstm", "fox") and pi_ + 1 < len(phases):
                nk, nl_ = phases[pi_ + 1]
                if nk == "mlp" and nl_ in mod_done:
                    mix_next = norm_spec(nl_, "mlp")
                elif nk == "final":
                    mix_next = final_norm_emitters()
            if kind == "mlstm":
                mlstm(l, skip_norm=skip, next_em=mix_next)
                prenormed = mix_next is not None
            elif kind == "kv":
                fox_kv(skip_norm=skip)
            elif kind == "fox":
                fox(l, l - N_A, skip_norm=skip, next_em=mix_next)
                prenormed = mix_next is not None
            elif kind == "mlp":
                next_em = None
                if fuse_norm and pi_ + 1 < len(phases):
                    nk, nl_ = phases[pi_ + 1]
                    if nk in ("mlstm", "fox"):
                        next_em = norm_spec(nl_, "mix")
                    elif nk == "kv":
                        next_em = kv_norm_em()
                    elif nk == "mlp":
                        next_em = norm_spec(nl_, "mlp")
                    elif nk == "final":
                        next_em = final_norm_emitters()
                    if nl_ is not None and nl_ not in mod_done and not (nl_ > l and all(x <= l or x >= nl_ for x in layers_used)):
                        next_em = None
                mlp(l, skip_norm=skip, next_em=next_em)
                prenormed = next_em is not None
            elif kind == "final":
                if not skip:
                    run_norm(final_norm_emitters())

        if not cfg.get("final_norm", True):
            for kc in range(KC):
                out_toks.append(sc.dma("sp", yT_d[kc * 128:(kc + 1) * 128, :], XT[:, kc, :], sc.dsem("out"),
                                       reads=[xkeys(kc, tb) for tb in range(NTB)]))
        sc.wait_all("sp", [out_toks[-1]])
        sc.emit()
    return nc


def _prep_inputs(inputs):
    f = lambda a: np.ascontiguousarray(np.asarray(a, dtype=np.float32))
    x = f(inputs["x"])
    c = f(inputs["c"])
    B = x.shape[0]
    shared = {}
    shared["ada_w"] = f(inputs["ada_w"])
    ab = f(inputs["ada_b"])
    shared["ada_bT"] = f(ab.reshape(DEPTH, 48, 128).transpose(2, 0, 1).reshape(128, DEPTH * 48))
    shared["a_w_in"] = f(inputs["a_w_in"])
    shared["a_b_iT"] = f(f(inputs["a_b_i"]).T)
    shared["a_b_fT"] = f(f(inputs["a_b_f"]).T)
    hg = f(inputs["a_head_gain"])
    shared["a_hgT"] = f(hg.reshape(N_A, A_H, 2, 128).transpose(3, 0, 1, 2).reshape(128, N_A * 8))
    shared["a_w_out"] = f(inputs["a_w_out"])
    shared["kv_gT"] = f(f(inputs["kv_gain"]).reshape(KC, 128).T)
    shared["b_w_kv"] = f(inputs["b_w_kv"])
    shared["b_fgbT"] = f(f(inputs["b_fg_bias"]).reshape(B_H, 1))
    shared["b_w_q"] = f(inputs["b_w_q"])
    shared["b_w_out"] = f(inputs["b_w_out"])
    shared["mlp_w1"] = f(inputs["mlp_w1"])
    shared["mlp_w2"] = f(inputs["mlp_w2"])
    shared["fin_gT"] = f(f(inputs["final_gain"]).reshape(KC, 128).T)
    cstf = np.zeros((128, 1024), np.float32)
    cstf[:, 896:1024] = -30000.0 * np.tril(np.ones((128, 128)), -1)
    cstf[:, 0:128] = np.eye(128)
    cstf[127, 128:256] = 1.0
    cstf[:, 256:384] = 1.0
    for h in range(4):
        cstf[h, 384 + h * 128:384 + (h + 1) * 128] = 1.0
    shared["cstf"] = cstf
    cstb = np.zeros((128, 512), np.float32)
    cstb[:, 0:128] = np.triu(np.ones((128, 128)))
    cstb[:, 128:256] = 1.0
    cstb[:, 256:320] = 1.0
    cstb[:, 448:512] = 1.0
    shared["cstb"] = cstb
    in_maps = []
    for b in range(B):
        m = dict(shared)
        m["xT"] = f(x[b].T)
        m["cT"] = f(c[b].reshape(KC, 128).T)
        in_maps.append(m)
    return in_maps


FULL_CFG = dict(layers=[(l, True, True) for l in range(DEPTH)], final_norm=True)


def run_cfg(inputs, cfg, cores=None, trace=False):
    in_maps = _prep_inputs(inputs)
    if cores is not None:
        in_maps = [in_maps[i] for i in cores]
    nc = build_program(cfg)
    res = run_bass_kernel_spmd(nc, in_maps, core_ids=list(range(len(in_maps))), trace=trace)
    outs = [np.ascontiguousarray(r["yT"].T) for r in res.results]
    out = np.stack(outs, axis=0).astype(np.float32)
    if trace:
        return out, res
    return out


def kernel(**inputs):
    return run_cfg(inputs, FULL_CFG)
```

```python
import numpy as np
from contextlib import ExitStack
import concourse.bass as bass
import concourse.mybir as mybir
from concourse.bass_utils import run_bass_kernel_spmd

F32 = mybir.dt.float32
BF16 = mybir.dt.bfloat16
AF = mybir.ActivationFunctionType
ALU = mybir.AluOpType

S = 2048
D = 1024
KC = 8
NTB = 4
NSB = 16
DEPTH = 4
N_A = 2
A_H = 4
B_H = 16
EPS = 1e-6


class DSem:
    def __init__(self, sem, name):
        self.sem = sem
        self.val = 0
        self.name = name


class Sched:
    def __init__(self, nc, es):
        self.nc = nc
        self.es = es
        self.engs = dict(pe=nc.tensor, act=nc.scalar, dve=nc.vector, pool=nc.gpsimd, sp=nc.sync)
        self.prog = {e: [] for e in self.engs}
        self.pos = {e: 0 for e in self.engs}
        self.waited = {}
        self.last_w = {}
        self.readers = {}
        self.milestones = {e: set() for e in self.engs}
        self.esem = {}
        for e in ("pe", "act", "dve", "pool"):
            self.esem[e] = es.enter_context(nc.semaphore("es_" + e))
        self.dsems = {}
        self.final_tokens = []

    def dsem(self, name):
        if name not in self.dsems:
            self.dsems[name] = DSem(self.es.enter_context(self.nc.semaphore("ds_" + name)), name)
        return self.dsems[name]

    def _deps(self, eng, reads, writes, pe_accum):
        deps = []
        for r in reads:
            t = self.last_w.get(r)
            if t is not None:
                deps.append(t)
        for w in writes:
            t = self.last_w.get(w)
            if t is not None:
                if not (pe_accum and t[0] == "E" and t[1] == "pe"):
                    deps.append(t)
            deps.extend(self.readers.get(w, []))
        out = []
        for t in deps:
            key = (eng, t[0], t[1])
            if self.waited.get(key, 0) >= t[2]:
                continue
            self.waited[key] = t[2]
            out.append(t)
            if t[0] == "E":
                self.milestones[t[1]].add(t[2])
        return out

    def _commit(self, tok, reads, writes):
        for r in reads:
            self.readers.setdefault(r, []).append(tok)
        for w in writes:
            self.last_w[w] = tok
            self.readers[w] = []

    def op(self, eng, meth, kw, reads=(), writes=(), pe_accum=False, args=()):
        fn = (meth, args, kw)
        reads = list(reads)
        writes = list(writes)
        deps = self._deps(eng, reads, writes, pe_accum)
        self.pos[eng] += 1
        p = self.pos[eng]
        tok = ("E", eng, p)
        self.prog[eng].append(("op", deps, fn, p))
        self._commit(tok, reads, writes)
        return tok

    def dma(self, q, out, in_, ds, reads=(), writes=()):
        fn = ("dma_start", (), dict(out=out, in_=in_))
        reads = list(reads)
        writes = list(writes)
        deps = self._deps(q, reads, writes, False)
        ds.val += 16
        tok = ("D", ds.name, ds.val)
        self.prog[q].append(("dma", deps, fn, ds))
        self._commit(tok, reads, writes)
        return tok

    def fence(self, pool=False):
        toks = [("E", e, self.pos[e]) for e in ("pe", "act", "dve", "pool") if self.pos[e] > 0 and any(k == "op" for k, _, _, _ in self.prog[e])]
        for name, ds in self.dsems.items():
            if ds.val > 0 and not name.startswith("W"):
                toks.append(("D", name, ds.val))
        for e in (("pe", "act", "dve", "sp", "pool") if pool else ("pe", "act", "dve", "sp")):
            deps = []
            for t in toks:
                if t[0] == "E" and t[1] == e:
                    continue
                key = (e, t[0], t[1])
                if self.waited.get(key, 0) >= t[2]:
                    continue
                self.waited[key] = t[2]
                deps.append(t)
                if t[0] == "E":
                    self.milestones[t[1]].add(t[2])
            self.prog[e].append(("wait", deps, None, None))

    def wait_all(self, eng, toks):
        self.prog[eng].append(("wait", [t for t in toks], None, None))
        for t in toks:
            if t[0] == "E":
                self.milestones[t[1]].add(t[2])

    def emit(self):
        ms_rank = {}
        for e, ms in self.milestones.items():
            for i, p in enumerate(sorted(ms)):
                ms_rank[(e, p)] = i + 1
        engs = self.engs
        esem = self.esem
        dsems = self.dsems
        milestones = self.milestones

        def replay(ename, eobj):
            for kind, deps, fn, extra in self.prog[ename]:
                for t in deps:
                    if t[0] == "E":
                        eobj.wait_ge(esem[t[1]], ms_rank[(t[1], t[2])])
                    else:
                        eobj.wait_ge(dsems[t[1]].sem, t[2])
                if kind == "op":
                    ins = getattr(eobj, fn[0])(*fn[1], **fn[2])
                    if extra in milestones[ename]:
                        ins.then_inc(esem[ename], 1)
                elif kind == "dma":
                    getattr(eobj, fn[0])(*fn[1], **fn[2]).then_inc(extra.sem, 16)

        with self.nc.Block() as block:
            @block.tensor
            def _(e):
                replay("pe", e)

            @block.scalar
            def _(e):
                replay("act", e)

            @block.vector
            def _(e):
                replay("dve", e)

            @block.gpsimd
            def _(e):
                replay("pool", e)

            @block.sync
            def _(e):
                replay("sp", e)


def build_program(cfg):
    nc = bass.Bass("TRN2", target_bir_lowering=False)
    dt = nc.dram_tensor
    xT_d = dt("xT", [D, S], F32, kind="ExternalInput").ap()
    cT_d = dt("cT", [128, KC], F32, kind="ExternalInput").ap()
    adaw_d = dt("ada_w", [DEPTH, D, 6 * D], F32, kind="ExternalInput").ap()
    adab_d = dt("ada_bT", [128, DEPTH * 48], F32, kind="ExternalInput").ap()
    awin_d = dt("a_w_in", [N_A, D, 3080], F32, kind="ExternalInput").ap()
    abi_d = dt("a_b_iT", [A_H, N_A], F32, kind="ExternalInput").ap()
    abf_d = dt("a_b_fT", [A_H, N_A], F32, kind="ExternalInput").ap()
    ahg_d = dt("a_hgT", [128, N_A * 8], F32, kind="ExternalInput").ap()
    awout_d = dt("a_w_out", [N_A, D, D], F32, kind="ExternalInput").ap()
    kvg_d = dt("kv_gT", [128, KC], F32, kind="ExternalInput").ap()
    bwkv_d = dt("b_w_kv", [D, 2064], F32, kind="ExternalInput").ap()
    bfgb_d = dt("b_fgbT", [B_H, 1], F32, kind="ExternalInput").ap()
    bwq_d = dt("b_w_q", [2, D, D], F32, kind="ExternalInput").ap()
    bwout_d = dt("b_w_out", [2, D, D], F32, kind="ExternalInput").ap()
    w1_d = dt("mlp_w1", [DEPTH, D, 4 * D], F32, kind="ExternalInput").ap()
    w2_d = dt("mlp_w2", [DEPTH, 4 * D, D], F32, kind="ExternalInput").ap()
    fg_d = dt("fin_gT", [128, KC], F32, kind="ExternalInput").ap()
    cstf_d = dt("cstf", [128, 1024], F32, kind="ExternalInput").ap()
    cstb_d = dt("cstb", [128, 512], F32, kind="ExternalInput").ap()
    yT_d = dt("yT", [D, S], F32, kind="ExternalOutput").ap()
    kt_d = dt("kt_scr", [128, KC * S], BF16, kind="Internal").ap()
    v_d = dt("v_scr", [128, NSB * D], BF16, kind="Internal").ap()
    gk_d = dt("gk_scr", [B_H, 3, S], BF16, kind="Internal").ap()
    gq_d = dt("gq_scr", [B_H, 3, S], BF16, kind="Internal").ap()

    es = ExitStack()
    with es:
        def sb(name, shape, dtype):
            return es.enter_context(nc.sbuf_tensor(name, shape, dtype))

        XT = sb("XT", [128, KC, S], F32)
        HT = sb("HT", [128, KC, S], BF16)
        CSTF = sb("CSTF", [128, 1024], F32)
        CSTB = sb("CSTB", [128, 512], BF16)
        IDENT = CSTF[:, 0:128]
        E127 = CSTF[:, 128:256]
        ONESF = CSTF[:, 256:384]
        NEGMASK = CSTF[:, 896:1024]
        MASKB = CSTB[:, 0:128]
        ONESB = CSTB[:, 128:256]
        HALFB = [CSTB[:, 256:384], CSTB[:, 384:512]]
        MODC = sb("MODC", [128, DEPTH * 48], F32)
        ADAB = sb("ADAB", [128, DEPTH * 48], F32)
        CONDF = sb("CONDF", [128, KC], F32)
        CONDS = sb("CONDS", [128, KC], F32)
        CONDB = sb("CONDB", [128, KC], BF16)
        SMALL = sb("SMALL", [128, 64], F32)
        ABI = SMALL[0:4, 0:2]
        ABF = SMALL[0:4, 2:4]
        NABF = SMALL[0:4, 4:6]
        AHG = SMALL[:, 8:24]
        KVG = SMALL[:, 24:32]
        FING = SMALL[:, 32:40]
        FGB = SMALL[0:16, 40:41]
        NFGB = SMALL[0:16, 41:42]
        EPSC = SMALL[:, 48:49]
        ONEC = SMALL[:, 49:50]
        WA = [sb("WA%d" % i, [128, 4096], BF16) for i in range(2)]
        WB = [sb("WB%d" % i, [128, 4096], BF16) for i in range(2)]
        MIXW = 12800
        MIX = sb("MIX", [128, MIXW], F32)
        MIXB = MIX.bitcast(BF16)
        RS = [sb("RS%d" % i, [128, 512], F32) for i in range(2)]
        NTF = 6
        TF = [sb("TF%d" % i, [128, 512], F32) for i in range(NTF)]
        NTBT = 4
        TB = [sb("TB%d" % i, [128, 512], BF16) for i in range(NTBT)]
        CC = sb("CC", [128, 64], F32)
        PS = [es.enter_context(nc.psum_tensor("PS%d" % i, [128, 512], F32)) for i in range(8)]

        sc = Sched(nc, es)
        ADD_ENG = cfg.get('add_eng', 'pool')
        rot = {}

        def nxt(name, n):
            rot[name] = (rot.get(name, -1) + 1) % n
            return rot[name]

        def MM(out, lhsT, rhs, start, stop, reads, writes):
            return sc.op("pe", "matmul", dict(out=out, lhsT=lhsT, rhs=rhs, start=start, stop=stop), reads, writes, pe_accum=True)

        def TR(out, in_, ident, reads, writes):
            return sc.op("pe", "transpose", dict(out=out, in_=in_, identity=ident), reads, writes, pe_accum=True)

        def ACT(out, in_, func, reads, writes, bias=None, scale=None):
            kw = dict(out=out, in_=in_, func=func)
            if bias is not None:
                kw["bias"] = bias
            if scale is not None:
                kw["scale"] = scale
            return sc.op("act", "activation", kw, reads, writes)

        def TT(out, in0, in1, op, reads, writes):
            return sc.op("dve", "tensor_tensor", dict(out=out, in0=in0, in1=in1, op=op), reads, writes)

        def STT(out, in0, scalar, in1, op0, op1, reads, writes):
            return sc.op("dve", "scalar_tensor_tensor", dict(out=out, in0=in0, scalar=scalar, in1=in1, op0=op0, op1=op1), reads, writes)

        def TS(out, in0, s1, op0, reads, writes):
            return sc.op("dve", "tensor_scalar", dict(out=out, in0=in0, scalar1=s1, scalar2=None, op0=op0), reads, writes)

        def RECIP(out, in_, reads, writes):
            return sc.op("dve", "reciprocal", dict(out=out, in_=in_), reads, writes)

        def VCOPY(out, in_, reads, writes):
            return sc.op("dve", "tensor_copy", dict(out=out, in_=in_), reads, writes)

        def SCAN(out, d0, d1, op0, op1, reads, writes):
            return sc.op("dve", "tensor_tensor_scan", dict(out=out, data0=d0, data1=d1, initial=0.0, op0=op0, op1=op1), reads, writes)

        def wkeys(name, i):
            return ["%s%d_%d" % (name, i, q) for q in range(4)]

        def wload(slots, name, parts):
            i = nxt(name, len(slots))
            t = slots[i]
            keys = wkeys(name, i)
            ds = sc.dsem("%s%d" % (name, i))
            tok = None
            for n, (dv, src) in enumerate(parts):
                tok = sc.dma("pool", dv(t), src, ds, reads=[], writes=(keys if n == 0 else []))
            for k in keys:
                sc.last_w[k] = tok
            return t, keys

        def w_rows(src2d, r0, nr, c0, ncw):
            return src2d[r0:r0 + nr, c0:c0 + ncw].rearrange("(kc p) n -> p kc n", p=128)

        def sview(t, off, nk, ncw):
            return t[:, off:off + nk * ncw].rearrange("p (kc n) -> p kc n", kc=nk)

        def small_dma(out, in_, name, writes):
            sc.dma("sp", out, in_, sc.dsem(name), writes=writes)

        small_dma(CSTF[:], cstf_d, "cstf", ["CSTF"])
        sc.dma("pool", CSTB[:], cstb_d, sc.dsem("cstb"), writes=["CSTB"])
        small_dma(ADAB[:], adab_d, "adab", ["ADAB"])
        small_dma(CONDF[:], cT_d, "condf", ["CONDF"])
        small_dma(ABI, abi_d, "abi", ["ABI"])
        small_dma(ABF, abf_d, "abf", ["ABF"])
        small_dma(AHG, ahg_d, "ahg", ["AHG"])
        small_dma(KVG, kvg_d, "kvg", ["KVG"])
        small_dma(FING, fg_d, "fing", ["FING"])
        small_dma(FGB, bfgb_d, "fgb", ["FGB"])
        xkeys = lambda kc, tb: "XT%d_%d" % (kc, tb)
        hkeys = lambda kc, tb: "HT%d_%d" % (kc, tb)
        for kc in range(KC):
            sc.dma("sp", XT[:, kc, :], xT_d[kc * 128:(kc + 1) * 128, :], sc.dsem("x%d" % kc),
                   writes=[xkeys(kc, tb) for tb in range(NTB)])
        TS(NABF, ABF, -1.0, ALU.mult, ["ABF"], ["NABF"])
        TS(NFGB, FGB, -1.0, ALU.mult, ["FGB"], ["NFGB"])
        sc.op("dve", "memset", {}, [], ["EPSC"], args=(EPSC, EPS))
        sc.op("dve", "memset", {}, [], ["ONEC"], args=(ONEC, 1.0))
        ACT(CONDS[:], CONDF[:], AF.Sigmoid, ["CONDF"], ["CONDS"])
        TT(CONDB[:], CONDF[:], CONDS[:], ALU.mult, ["CONDF", "CONDS"], ["CONDB"])

        layers_used = sorted(set(l for (l, _, _) in cfg["layers"]))

        def mod_chunk(l, cg, wv, keys):
            for j in range(4):
                col = l * 48 + cg * 4 + j
                for kc in range(KC):
                    MM(PS[7][:, col:col + 1], wv[:, kc, j * 128:(j + 1) * 128], CONDB[:, kc:kc + 1],
                       kc == 0, kc == KC - 1, keys + ["CONDB"], ["PS7"])

        def mkeys(l, vs):
            return ["MODC%dv%d" % (l, v) for v in vs]

        def mod_finish_vec(l, v):
            a0 = l * 48 + v * 8
            TT(MODC[:, a0:a0 + 8], PS[7][:, a0:a0 + 8], ADAB[:, a0:a0 + 8], ALU.add, ["PS7", "ADAB"], mkeys(l, [v]))
            if v in (1, 4):
                TS(MODC[:, a0:a0 + 8], MODC[:, a0:a0 + 8], 1.0, ALU.add, mkeys(l, [v]), mkeys(l, [v]))

        def mod_finish(l):
            allk = mkeys(l, range(6))
            TT(MODC[:, l * 48:(l + 1) * 48], PS[7][:, l * 48:(l + 1) * 48], ADAB[:, l * 48:(l + 1) * 48], ALU.add,
               ["PS7", "ADAB"], allk)
            for v in (1, 4):
                a0 = l * 48 + v * 8
                TS(MODC[:, a0:a0 + 8], MODC[:, a0:a0 + 8], 1.0, ALU.add, mkeys(l, [v]), mkeys(l, [v]))

        mod_done = set()
        overlap_mod = cfg.get("overlap_mod", True)
        for l in (layers_used[:1] if overlap_mod else layers_used):
            for cg in range(12):
                t, keys = wload(WA, "WA", [(lambda t: sview(t, 0, 8, 512), w_rows(adaw_d[l], 0, D, cg * 512, 512))])
                mod_chunk(l, cg, sview(t, 0, 8, 512), keys)
                if cg % 2 == 1:
                    mod_finish_vec(l, cg // 2)
            mod_done.add(l)

        def modcol(l, v, kc):
            c = l * 48 + v * 8 + kc
            return MODC[:, c:c + 1]

        def tbs(tb):
            return slice(tb * 512, (tb + 1) * 512)

        def rstd_block(tb):
            for kc in range(KC):
                i = nxt("TB", NTBT)
                ACT(TB[i][:], XT[:, kc, tbs(tb)], AF.Square, [xkeys(kc, tb)], ["TB%d" % i])
                MM(PS[6][:], ONESB, TB[i][:], kc == 0, kc == KC - 1, ["TB%d" % i, "CSTB"], ["PS6"])
            r = nxt("RS", 2)
            ACT(RS[r][:], PS[6][:], AF.Ln, ["PS6", "EPSC"], ["RS%d" % r], bias=EPSC, scale=1.0 / D)
            ACT(RS[r][:], RS[r][:], AF.Exp, ["RS%d" % r], ["RS%d" % r], scale=-0.5)
            return r

        def make_norm(scale_col, shift_col, pkeys):
            def modulate(tb, r):
                for kc in range(KC):
                    i = nxt("TF", NTF)
                    STT(TF[i][:], XT[:, kc, tbs(tb)], scale_col(kc), RS[r][:], ALU.mult, ALU.mult,
                        [xkeys(kc, tb), "RS%d" % r] + pkeys, ["TF%d" % i])
                    if shift_col is not None:
                        ACT(HT[:, kc, tbs(tb)], TF[i][:], AF.Identity, ["TF%d" % i] + pkeys, [hkeys(kc, tb)],
                            bias=shift_col(kc), scale=1.0)
                    else:
                        ACT(HT[:, kc, tbs(tb)], TF[i][:], AF.Copy, ["TF%d" % i], [hkeys(kc, tb)])
            return rstd_block, modulate, ("std", scale_col, shift_col, pkeys)

        out_toks = []

        def final_norm_emitters():
            def modulate(tb, r):
                for kc in range(KC):
                    STT(XT[:, kc, tbs(tb)], XT[:, kc, tbs(tb)], FING[:, kc:kc + 1], RS[r][:], ALU.mult, ALU.mult,
                        [xkeys(kc, tb), "RS%d" % r, "FING"], [xkeys(kc, tb)])
                    out_toks.append(sc.dma("sp", yT_d[kc * 128:(kc + 1) * 128, tbs(tb)], XT[:, kc, tbs(tb)], sc.dsem("out"),
                                           reads=[xkeys(kc, tb)]))
            return rstd_block, modulate, ("final", None, None, [])

        def norm_pieces(em, tb, bank_fn, tmp_fn):
            kind, scale_col, shift_col, pkeys = em[2]
            st = {}
            pcs = []

            def stats_all():
                b = bank_fn()
                for kc in range(KC):
                    tq, tk_ = tmp_fn()
                    if kc % 2 == 0:
                        ACT(tq, XT[:, kc, tbs(tb)], AF.Square, [xkeys(kc, tb)], [tk_])
                    else:
                        TT(tq, XT[:, kc, tbs(tb)], XT[:, kc, tbs(tb)], ALU.mult, [xkeys(kc, tb)], [tk_])
                    MM(PS[b][:], ONESB, tq, kc == 0, kc == KC - 1, [tk_, "CSTB"], ["PS%d" % b])
                r = nxt("RS", 2)
                st["r"] = r
                ACT(RS[r][:], PS[b][:], AF.Ln, ["PS%d" % b, "EPSC"], ["RS%d" % r], bias=EPSC, scale=1.0 / D)
                ACT(RS[r][:], RS[r][:], AF.Exp, ["RS%d" % r], ["RS%d" % r], scale=-0.5)

            def mod(kc):
                r = st["r"]
                if kind == "final":
                    STT(XT[:, kc, tbs(tb)], XT[:, kc, tbs(tb)], FING[:, kc:kc + 1], RS[r][:], ALU.mult, ALU.mult,
                        [xkeys(kc, tb), "RS%d" % r, "FING"], [xkeys(kc, tb)])
                    out_toks.append(sc.dma("sp", yT_d[kc * 128:(kc + 1) * 128, tbs(tb)], XT[:, kc, tbs(tb)], sc.dsem("out"),
                                           reads=[xkeys(kc, tb)]))
                    return
                i = nxt("TF", NTF)
                STT(TF[i][:], XT[:, kc, tbs(tb)], scale_col(kc), RS[r][:], ALU.mult, ALU.mult,
                    [xkeys(kc, tb), "RS%d" % r] + pkeys, ["TF%d" % i])
                if shift_col is not None:
                    ACT(HT[:, kc, tbs(tb)], TF[i][:], AF.Identity, ["TF%d" % i] + pkeys, [hkeys(kc, tb)],
                        bias=shift_col(kc), scale=1.0)
                else:
                    ACT(HT[:, kc, tbs(tb)], TF[i][:], AF.Copy, ["TF%d" % i], [hkeys(kc, tb)])

            pcs.append(stats_all)
            for kc in range(KC):
                pcs.append(lambda kc=kc: mod(kc))
            return pcs

        def run_norm(em):
            stats, modulate = em[0], em[1]
            rr = {0: stats(0)}
            for tb in range(NTB):
                if tb + 1 < NTB:
                    rr[tb + 1] = stats(tb + 1)
                modulate(tb, rr[tb])

        def norm_to_HT(scale_col, shift_col, pkeys):
            run_norm(make_norm(scale_col, shift_col, pkeys))

        def norm_spec(l, which):
            if which == "mix":
                return make_norm(lambda kc: modcol(l, 1, kc), lambda kc: modcol(l, 0, kc), mkeys(l, [1, 0]))
            return make_norm(lambda kc: modcol(l, 4, kc), lambda kc: modcol(l, 3, kc), mkeys(l, [4, 3]))

        def acc_bank():
            return nxt("ACC", 2)

        def resid_update(b, l, gvec, dc, tb, mk):
            STT(XT[:, dc, tbs(tb)], PS[b][:], modcol(l, gvec, dc), XT[:, dc, tbs(tb)], ALU.mult, ALU.add,
                ["PS%d" % b, xkeys(dc, tb)] + mkeys(l, [gvec]), [xkeys(dc, tb)])

        def mlp(l, skip_norm=False, next_em=None):
            mk = ["MODC%d" % l]
            later = [x for x in layers_used if x > l and x not in mod_done]
            nl = later[0] if later else None
            if not skip_norm:
                run_norm(norm_spec(l, "mlp"))
            sc.fence(pool=(nl is not None))
            H1 = MIXB[:, 0:4 * S].rearrange("p (f t) -> p f t", f=4)
            MS = [MIXB[:, 8192 + i * 4096:8192 + (i + 1) * 4096] for i in range(2)]

            def ada_chunk(cg):
                i = nxt("MS", 2)
                keys = ["MS%d" % i]
                wv = MS[i].rearrange("p (kc n) -> p kc n", kc=8)
                sc.dma("pool", wv, w_rows(adaw_d[nl], 0, D, cg * 512, 512), sc.dsem("WMS%d" % i), reads=[], writes=keys)
                mod_chunk(nl, cg, wv, keys)

            for g in range(8):
                t1, k1 = wload(WA, "WA", [(lambda t: sview(t, 0, 8, 512), w_rows(w1_d[l], 0, D, g * 512, 512))])
                t2, k2 = wload(WB, "WB", [(lambda t: sview(t, 0, 4, 1024), w_rows(w2_d[l], g * 512, 512, 0, 1024))])
                w1v = sview(t1, 0, 8, 512)
                w2v = sview(t2, 0, 4, 1024)
                cgs = [c_ for c_ in (2 * g, 2 * g + 1) if c_ < 12] if nl is not None else []
                for tb in range(NTB):
                    for f in range(4):
                        b = nxt("ACC4", 4)
                        for kc in range(KC):
                            MM(PS[b][:], w1v[:, kc, f * 128:(f + 1) * 128], HT[:, kc, tbs(tb)], kc == 0, kc == KC - 1,
                               k1 + [hkeys(kc, tb)], ["PS%d" % b])
                        i = nxt("TF", NTF)
                        ACT(TF[i][:], PS[b][:], AF.Relu, ["PS%d" % b], ["TF%d" % i])
                        TT(H1[:, f, tbs(tb)], TF[i][:], TF[i][:], ALU.mult, ["TF%d" % i], ["H1_%d_%d" % (f, tb)])
                    if tb == 1 and cgs:
                        ada_chunk(cgs[0])
                for tb in range(NTB):
                    for dc in range(KC):
                        b = nxt("ACC4", 4)
                        for f in range(4):
                            MM(PS[b][:], w2v[:, f, dc * 128:(dc + 1) * 128], H1[:, f, tbs(tb)], f == 0, f == 3,
                               k2 + ["H1_%d_%d" % (f, tb)], ["PS%d" % b])
                        resid_update(b, l, 5, dc, tb, mk)
                    if tb == 1 and len(cgs) > 1:
                        ada_chunk(cgs[1])
                    if g == 7 and next_em is not None:
                        next_em[1](tb, next_em[0](tb))
                if g == 6 and nl is not None:
                    mod_finish(nl)
                    mod_done.add(nl)

        def mlstm(l, skip_norm=False, next_em=None):
            mk = ["MODC%d" % l]
            if not skip_norm:
                run_norm(norm_spec(l, "mix"))
            sc.fence()
            T_i = MIX[0:4, 0:2048]
            T_l = MIX[0:4, 2048:4096]
            T_F = MIX[0:4, 4096:6144]
            T_m = MIX[0:4, 6144:8192]
            EBt = MIX[:, 0:2048]
            ABt = MIX[:, 2048:4096]
            QH = MIXB[:, 16384:18432]
            KH = MIXB[:, 18432:20480]
            VH = MIXB[:, 20480:24576].rearrange("p (s e) -> p s e", s=16)
            AHt = MIXB[:, 24576:25600].rearrange("p (a t) -> p a t", a=2)
            rk = lambda r, tb: "R%d_%d" % (r, tb)
            allr = lambda r: [rk(r, tb) for tb in range(NTB)]
            tg, kg = wload(WA, "WA", [(lambda t: sview(t, 0, 8, 8), w_rows(awin_d[l], 0, D, 3072, 8))])
            wg = sview(tg, 0, 8, 8)
            for tb in range(NTB):
                b = acc_bank()
                for kc in range(KC):
                    MM(PS[b][0:4, :], wg[:, kc, 0:4], HT[:, kc, tbs(tb)], kc == 0, kc == KC - 1, kg + [hkeys(kc, tb)], ["PS%d" % b])
                ACT(T_i[:, tbs(tb)], PS[b][0:4, :], AF.Identity, ["PS%d" % b, "ABI"], [rk(0, tb)], bias=ABI[:, l:l + 1], scale=1.0)
                b = acc_bank()
                for kc in range(KC):
                    MM(PS[b][0:4, :], wg[:, kc, 4:8], HT[:, kc, tbs(tb)], kc == 0, kc == KC - 1, kg + [hkeys(kc, tb)], ["PS%d" % b])
                ACT(T_l[:, tbs(tb)], PS[b][0:4, :], AF.Exp, ["PS%d" % b, "NABF"], [rk(1, tb)], bias=NABF[:, l:l + 1], scale=-1.0)
            ACT(T_l, T_l, AF.Ln, allr(1) + ["ONEC"], allr(1), bias=ONEC[0:4, :], scale=1.0)
            TS(T_l, T_l, -1.0, ALU.mult, allr(1), allr(1))
            SCAN(T_F, T_l, T_l, ALU.add, ALU.min, allr(1), allr(2))
            SCAN(T_m, T_l, T_i, ALU.add, ALU.max, allr(1) + allr(0), allr(3))
            TT(T_i, T_i, T_F, ALU.subtract, allr(0) + allr(2), allr(0))
            TT(T_F, T_F, T_m, ALU.subtract, allr(2) + allr(3), allr(2))
            TS(T_m, T_m, -1.0, ALU.mult, allr(3), allr(3))
            for J in range(NSB):
                TR(PS[7][:, J * 4:J * 4 + 4], T_i[:, J * 128:(J + 1) * 128], CSTF[0:4, 0:4], [rk(0, J // 4), "CSTF"], ["PS7"])
            ACT(CC[:, 0:64], PS[7][:, 0:64], AF.Copy, ["PS7"], ["CC"])
            X8 = [MIX[:, k * 512:(k + 1) * 512] for k in range(8)]
            ABd, EBd = X8[0:2], X8[2:4]
            N0t, N1t, DAt, RSt = X8[4], X8[5], X8[6], X8[7]
            x8k = ["X8_%d" % k for k in range(8)]
            sc.fence()
            PFm = cfg.get("ml_pf", 3)
            NPTm = 4

            def acc2():
                return (0, 7)[nxt("ACC2", 2)]

            def head_weights(h):
                tA, kA = wload(WA, "WA", [
                    (lambda t: sview(t, 0, 8, 512)[:, :, 0:128], w_rows(awin_d[l], 0, D, h * 128, 128)),
                    (lambda t: sview(t, 0, 8, 512)[:, :, 128:256], w_rows(awin_d[l], 0, D, 512 + h * 128, 128)),
                    (lambda t: sview(t, 0, 8, 512)[:, :, 256:512], w_rows(awin_d[l], 0, D, 1024 + h * 256, 256))])
                tB, kB = wload(WB, "WB", [
                    (lambda t: sview(t, 0, 8, 256), w_rows(awin_d[l], 0, D, 2048 + h * 256, 256)),
                    (lambda t: sview(t, 2048, 2, 1024), w_rows(awout_d[l], h * 256, 256, 0, 1024))])
                return tA, kA, tB, kB

            hw = {0: head_weights(0)}
            bgm = []

            def sq_tmp_m():
                i = nxt("TF", NTF)
                return TF[i].bitcast(BF16)[:, 0:512], "TF%d" % i
            for h in range(A_H):
                tA, kA, tB, kB = hw[h]
                if h + 1 < A_H:
                    hw[h + 1] = head_weights(h + 1)
                wA = sview(tA, 0, 8, 512)
                wo = sview(tB, 0, 8, 256)
                wout = sview(tB, 2048, 2, 1024)
                oh = CSTF[0:4, 384 + h * 128:384 + (h + 1) * 128]

                def gen_ab(tb, h=h, oh=oh):
                    d = tb % 2
                    b = acc2()
                    MM(PS[b][:], oh, T_F[:, tbs(tb)], True, True, ["CSTF", rk(2, tb)], ["PS%d" % b])
                    ACT(ABd[d], PS[b][:], AF.Copy, ["PS%d" % b], [x8k[d]])
                    b = acc2()
                    MM(PS[b][:], oh, T_m[:, tbs(tb)], True, True, ["CSTF", rk(3, tb)], ["PS%d" % b])
                    ACT(EBd[d], PS[b][:], AF.Exp, ["PS%d" % b], [x8k[2 + d]])

                for tb in range(NTB):
                    b = acc2()
                    for kc in range(KC):
                        MM(PS[b][:], wA[:, kc, 0:128], HT[:, kc, tbs(tb)], kc == 0, kc == KC - 1, kA + [hkeys(kc, tb)], ["PS%d" % b])
                    ACT(QH[:, tbs(tb)], PS[b][:], AF.Identity, ["PS%d" % b], ["QH%d" % tb], scale=float(128 ** -0.5))
                    b = acc2()
                    for kc in range(KC):
                        MM(PS[b][:], wA[:, kc, 128:256], HT[:, kc, tbs(tb)], kc == 0, kc == KC - 1, kA + [hkeys(kc, tb)], ["PS%d" % b])
                    VCOPY(KH[:, tbs(tb)], PS[b][:], ["PS%d" % b], ["KH%d" % tb])
                for sbk in range(NSB):
                    b = acc2()
                    for kc in range(KC):
                        MM(PS[b][:, 0:256], HT[:, kc, sbk * 128:(sbk + 1) * 128], wA[:, kc, 256:512], kc == 0, kc == KC - 1,
                           kA + [hkeys(kc, sbk // 4)], ["PS%d" % b])
                    if sbk % 2 == 0:
                        VCOPY(VH[:, sbk, :], PS[b][:, 0:256], ["PS%d" % b], ["VH%d" % sbk])
                    else:
                        ACT(VH[:, sbk, :], PS[b][:, 0:256], AF.Copy, ["PS%d" % b], ["VH%d" % sbk])
                gen_ab(0)

                iters = [(tb, J) for tb in range(NTB) for J in range(4 * tb + 4)]
                state = {}
                pending = []

                def stageA(k, h=h):
                    tb, J = iters[k]
                    for _ in range(2):
                        if bgm:
                            bgm.pop(0)()
                    if J == PFm and tb + 1 < NTB:
                        gen_ab(tb + 1)
                    d = tb % 2
                    n0 = max(0, J - 4 * tb)
                    c0 = n0 * 128
                    st = 1 + nxt("ST", 3)
                    MM(PS[st][:, c0:512], KH[:, J * 128:(J + 1) * 128], QH[:, tb * 512 + c0:(tb + 1) * 512], True, True,
                       ["KH%d" % (J // 4), "QH%d" % tb], ["PS%d" % st])
                    wi = nxt("TF", NTF)
                    ACT(TF[wi][:, c0:512], ABd[d][:, c0:512], AF.Exp, [x8k[d], "CC"], ["TF%d" % wi],
                        bias=CC[:, J * 4 + h:J * 4 + h + 1], scale=1.0)
                    ai = nxt("TB", NPTm)
                    TT(TB[ai][:, c0:512], PS[st][:, c0:512], TF[wi][:, c0:512], ALU.mult, ["PS%d" % st, "TF%d" % wi], ["TB%d" % ai])
                    if J >= 4 * tb:
                        TT(TB[ai][:, c0:c0 + 128], TB[ai][:, c0:c0 + 128], MASKB, ALU.mult, ["TB%d" % ai, "CSTB"], ["TB%d" % ai])
                    state[k] = (ai, c0)

                def post2(tb, h=h, wo=wo, wout=wout, kB=kB):
                    ACT(DAt, DAt, AF.Ln, [x8k[6]], [x8k[6]])
                    ACT(DAt, DAt, AF.Exp, [x8k[6]], [x8k[6]], scale=-1.0)
                    sq = []
                    for e2, Nt, nk in ((0, N0t, x8k[4]), (1, N1t, x8k[5])):
                        sc.op("pool", "tensor_tensor", dict(out=Nt, in0=Nt, in1=DAt, op=ALU.mult), [nk, x8k[6]], [nk])
                        si = nxt("TF", NTF)
                        sc.op("pool", "tensor_tensor", dict(out=TF[si][:], in0=Nt, in1=Nt, op=ALU.mult), [nk], ["TF%d" % si])
                        sq.append(si)
                    gis = []
                    for e2 in range(2):
                        b = acc2()
                        for kc in range(KC):
                            MM(PS[b][:], wo[:, kc, e2 * 128:(e2 + 1) * 128], HT[:, kc, tbs(tb)], kc == 0, kc == KC - 1,
                               kB + [hkeys(kc, tb)], ["PS%d" % b])
                        gi = nxt("TF", NTF)
                        ACT(TF[gi][:], PS[b][:], AF.Sigmoid, ["PS%d" % b], ["TF%d" % gi])
                        gis.append(gi)
                    b = acc2()
                    for e2 in range(2):
                        MM(PS[b][:], ONESF, TF[sq[e2]][:], e2 == 0, e2 == 1, ["CSTF", "TF%d" % sq[e2]], ["PS%d" % b])
                    ACT(RSt, PS[b][:], AF.Ln, ["PS%d" % b, "EPSC"], [x8k[7]], bias=EPSC, scale=1.0 / 256)
                    ACT(RSt, RSt, AF.Exp, [x8k[7]], [x8k[7]], scale=-0.5)
                    for e2, Nt, nk in ((0, N0t, x8k[4]), (1, N1t, x8k[5])):
                        gi = gis[e2]
                        gc = l * 8 + h * 2 + e2
                        STT(Nt, Nt, AHG[:, gc:gc + 1], RSt, ALU.mult, ALU.mult, [nk, "AHG", x8k[7]], [nk])
                        sc.op("pool", "tensor_tensor", dict(out=AHt[:, e2, :], in0=Nt, in1=TF[gi][:], op=ALU.mult),
                              [nk, "TF%d" % gi], ["AH%d" % e2])
                    for dc in range(KC):
                        b = acc2()
                        MM(PS[b][:], wout[:, 0, dc * 128:(dc + 1) * 128], AHt[:, 0, :], True, False, kB + ["AH0"], ["PS%d" % b])
                        MM(PS[b][:], wout[:, 1, dc * 128:(dc + 1) * 128], AHt[:, 1, :], False, True, kB + ["AH1"], ["PS%d" % b])
                        if dc % 2 == 0 or not cfg.get("split_evac", False):
                            resid_update(b, l, 2, dc, tb, mk)
                        else:
                            ui = nxt("TF", NTF)
                            ACT(TF[ui][:], PS[b][:], AF.Identity, ["PS%d" % b] + mkeys(l, [2]), ["TF%d" % ui], scale=modcol(l, 2, dc))
                            sc.op("pool", "tensor_tensor", dict(out=XT[:, dc, tbs(tb)], in0=XT[:, dc, tbs(tb)], in1=TF[ui][:], op=ALU.add),
                                  ["TF%d" % ui, xkeys(dc, tb)], [xkeys(dc, tb)])
                    if h == A_H - 1 and next_em is not None:
                        bgm.extend(norm_pieces(next_em, tb, acc2, sq_tmp_m))

                def stageB(k):
                    tb, J = iters[k]
                    ai, c0 = state.pop(k)
                    last = 4 * tb + 3
                    MM(PS[4][:, c0:512], VH[:, J, 0:128], TB[ai][:, c0:512], J == 0, J == last, ["VH%d" % J, "TB%d" % ai], ["PS4"])
                    MM(PS[5][:, c0:512], VH[:, J, 128:256], TB[ai][:, c0:512], J == 0, J == last, ["VH%d" % J, "TB%d" % ai], ["PS5"])
                    MM(PS[6][:, c0:512], ONESB, TB[ai][:, c0:512], J == 0, J == last, ["CSTB", "TB%d" % ai], ["PS6"])
                    if J == last:
                        ACT(DAt, PS[6][:], AF.Abs, ["PS6"], [x8k[6]])
                        VCOPY(N0t, PS[4][:], ["PS4"], [x8k[4]])
                        ACT(N1t, PS[5][:], AF.Copy, ["PS5"], [x8k[5]])
                        TT(DAt, DAt, EBd[tb % 2], ALU.max, [x8k[6], x8k[2 + tb % 2]], [x8k[6]])
                        pending.append([3, tb])
                    for p in pending:
                        p[0] -= 1
                    while pending and pending[0][0] <= 0:
                        _, a = pending.pop(0)
                        post2(a)

                n = len(iters)
                for k in range(min(PFm, n)):
                    stageA(k)
                for k in range(n):
                    if k + PFm < n:
                        stageA(k + PFm)
                    stageB(k)
                while pending:
                    _, a = pending.pop(0)
                    post2(a)
                while bgm:
                    bgm.pop(0)()

        def kv_norm_em():
            return make_norm(lambda kc: KVG[:, kc:kc + 1], None, ["KVG"])

        def fox_kv(skip_norm=False):
            if not skip_norm:
                run_norm(kv_norm_em())
            sc.fence()
            FGr = MIX[0:16, 0:2048]
            Gr = MIX[0:16, 2048:4096]
            KTs = [MIXB[:, 8192:10240], MIXB[:, 10240:12288]]
            tf_, kf = wload(WA, "WA", [(lambda t: sview(t, 0, 8, 16), w_rows(bwkv_d, 0, D, 2048, 16))])
            wf = sview(tf_, 0, 8, 16)
            for tb in range(NTB):
                b = acc_bank()
                for kc in range(KC):
                    MM(PS[b][0:16, :], wf[:, kc, 0:16], HT[:, kc, tbs(tb)], kc == 0, kc == KC - 1, kf + [hkeys(kc, tb)], ["PS%d" % b])
                ACT(FGr[:, tbs(tb)], PS[b][0:16, :], AF.Exp, ["PS%d" % b, "NFGB"], ["FGr"], bias=NFGB, scale=-1.0)
            ACT(FGr, FGr, AF.Ln, ["FGr", "ONEC"], ["FGr"], bias=ONEC[0:16, :], scale=1.0)
            SCAN(Gr, FGr, FGr, ALU.add, ALU.max, ["FGr"], ["Gr"])
            GHt = [MIXB[0:16, 12288 + k * 2048:12288 + (k + 1) * 2048] for k in range(3)]
            NGt = [MIXB[0:16, 18432 + k * 2048:18432 + (k + 1) * 2048] for k in range(3)]
            R1 = FGr
            VCOPY(GHt[0], Gr, ["Gr"], ["GH0"])
            TT(R1, Gr, GHt[0], ALU.subtract, ["Gr", "GH0", "FGr"], ["FGr"])
            VCOPY(GHt[1], R1, ["FGr"], ["GH1"])
            TT(R1, R1, GHt[1], ALU.subtract, ["FGr", "GH1"], ["FGr"])
            VCOPY(GHt[2], R1, ["FGr"], ["GH2"])
            for k in range(3):
                TS(NGt[k], GHt[k], -1.0, ALU.mult, ["GH%d" % k], ["NG%d" % k])
                sc.dma("sp", gk_d[:, k, :], GHt[k], sc.dsem("gh%d" % k), reads=["GH%d" % k], writes=["GKD"])
                sc.dma("sp", gq_d[:, k, :], NGt[k], sc.dsem("ng%d" % k), reads=["NG%d" % k], writes=["GQD"])
            for c in range(KC):
                tk, kk = wload(WA, "WA", [(lambda t: sview(t, 0, 8, 128), w_rows(bwkv_d, 0, D, c * 128, 128))])
                wk = sview(tk, 0, 8, 128)
                kt = KTs[c % 2]
                for tb in range(NTB):
                    b = acc_bank()
                    for kc in range(KC):
                        MM(PS[b][:], wk[:, kc, :], HT[:, kc, tbs(tb)], kc == 0, kc == KC - 1, kk + [hkeys(kc, tb)], ["PS%d" % b])
                    ACT(kt[:, tbs(tb)], PS[b][:], AF.Identity, ["PS%d" % b], ["KTs%d" % (c % 2)], scale=0.125)
                sc.dma("sp", kt_d[:, c * S:(c + 1) * S], kt, sc.dsem("kts%d" % (c % 2)), reads=["KTs%d" % (c % 2)], writes=["KTD%d" % c])
            vdv = v_d.rearrange("p (c s n) -> p c s n", c=8, s=16)
            for half in range(2):
                tv, kv = wload(WB, "WB", [(lambda t: sview(t, 0, 8, 512), w_rows(bwkv_d, 0, D, 1024 + half * 512, 512))])
                wv = sview(tv, 0, 8, 512)
                for sbk in range(NSB):
                    b = acc_bank()
                    for kc in range(KC):
                        MM(PS[b][:], HT[:, kc, sbk * 128:(sbk + 1) * 128], wv[:, kc, :], kc == 0, kc == KC - 1,
                           kv + [hkeys(kc, sbk // 4)], ["PS%d" % b])
                    vi = nxt("TB", NTBT)
                    VCOPY(TB[vi][:], PS[b][:], ["PS%d" % b], ["TB%d" % vi])
                    sc.dma("sp", vdv[:, half * 4:(half + 1) * 4, sbk, :], TB[vi][:].rearrange("p (c n) -> p c n", c=4),
                           sc.dsem("tb%d" % vi), reads=["TB%d" % vi], writes=["VD%d_%d" % (half, sbk)])

        def fox(l, j, skip_norm=False, next_em=None):
            mk = ["MODC%d" % l]
            if not skip_norm:
                run_norm(norm_spec(l, "mix"))
            sc.fence()
            KX = [[MIXB[:, (2 * ks + hh) * 2048:(2 * ks + hh + 1) * 2048] for hh in range(2)] for ks in range(2)]
            VX = [[MIXB[:, 8192 + (2 * ks + hh) * 2048:8192 + (2 * ks + hh + 1) * 2048].rearrange("p (s n) -> p s n", s=16)
                   for hh in range(2)] for ks in range(2)]
            QX = [[MIXB[:, 16384 + (2 * ks + hh) * 512:16384 + (2 * ks + hh + 1) * 512] for hh in range(2)] for ks in range(2)]
            OTS = [MIXB[:, 18432 + c * 512:18432 + (c + 1) * 512] for c in range(KC)]
            NPT = 8
            PT = [TB[k][:] for k in range(4)] + [MIXB[:, 22528 + k * 512:22528 + (k + 1) * 512] for k in range(4)]
            AUG = [64, 0]
            ptk = ["TB%d" % k for k in range(4)] + ["PTm%d" % k for k in range(4)]
            NSF = 5
            SF = [TF[k][:] for k in range(5)]
            LT = TF[5]
            PF = cfg.get("fox_pf", 3)
            vdkeys = ["VD%d_%d" % (hf, s_) for hf in range(2) for s_ in range(NSB)]
            for ks in range(2):
                for hh in range(2):
                    o = 64 * (1 - hh)
                    sc.op("dve", "memset", {}, [], ["VX%d_%d" % (ks, hh)], args=(VX[ks][hh][:, :, o:o + 64], 0.0))
                    sc.op("dve", "memset", {}, [], ["QXz%d_%d" % (ks, hh)], args=(QX[ks][hh][o:o + 64, :], 0.0))
                    sc.op("dve", "memset", {}, [], ["QXz%d_%d" % (ks, hh)], args=(QX[ks][hh][AUG[hh]:AUG[hh] + 6, :], 1.0))
                    sc.op("dve", "memset", {}, [], ["KXz%d_%d" % (ks, hh)], args=(KX[ks][hh][:, :], 0.0))
                    sc.op("dve", "memset", {}, [], ["KXz%d_%d" % (ks, hh)], args=(KX[ks][hh][AUG[hh]:AUG[hh] + 6, :], 1.0))
            wq_s, wo_s = [], []
            for i in range(2):
                t, k = wload(WA, "WA", [(lambda t: sview(t, 0, 8, 512), w_rows(bwq_d[j], 0, D, i * 512, 512))])
                wq_s.append((sview(t, 0, 8, 512), k))
            for i in range(2):
                t, k = wload(WB, "WB", [(lambda t: sview(t, 0, 4, 1024), w_rows(bwout_d[j], i * 512, 512, 0, 1024))])
                wo_s.append((sview(t, 0, 4, 1024), k))

            units = [(tb, c) for tb in range(NTB) for c in range(KC)]

            def loads(u, part="all"):
                tb, c = units[u]
                ks = u % 2
                nJ = 4 * tb + 4
                for hh in (range(2) if part in ("all", "k") else []):
                    h = 2 * c + hh
                    pp = slice(hh * 64, (hh + 1) * 64)
                    sc.dma("sp", KX[ks][hh][pp, 0:nJ * 128], kt_d[pp, c * S:c * S + nJ * 128], sc.dsem("kx%d_%d" % (ks, hh)),
                           reads=["KTD%d" % c, "KXz%d_%d" % (ks, hh)], writes=["KX%d_%d" % (ks, hh)])
                    a0 = AUG[hh]
                    sc.dma("sp", KX[ks][hh][a0:a0 + 3, 0:nJ * 128], gk_d[h, :, 0:nJ * 128], sc.dsem("kxa%d_%d" % (ks, hh)),
                           reads=["GKD", "KXz%d_%d" % (ks, hh)], writes=["KXa%d_%d" % (ks, hh)])
                    sc.dma("sp", QX[ks][hh][a0 + 3:a0 + 6, :], gq_d[h, :, tb * 512:(tb + 1) * 512], sc.dsem("qxa%d_%d" % (ks, hh)),
                           reads=["GQD", "QXz%d_%d" % (ks, hh)], writes=["QXa%d_%d" % (ks, hh)])
                vsrc = v_d[:, c * 2048:(c + 1) * 2048].rearrange("p (s n) -> p s n", s=16)
                for hh in (range(2) if part in ("all", "v") else []):
                    sc.dma("sp", VX[ks][hh][:, 0:nJ, hh * 64:(hh + 1) * 64], vsrc[:, 0:nJ, hh * 64:(hh + 1) * 64],
                           sc.dsem("vx%d_%d" % (ks, hh)), reads=vdkeys, writes=["VX%d_%d" % (ks, hh)])

            def qproj(u):
                tb, c = units[u]
                ks = u % 2
                wq, kq = wq_s[c // 4]
                cc = (c % 4) * 128
                for kc in range(KC):
                    MM(PS[0][:], wq[:, kc, cc:cc + 128], HT[:, kc, tbs(tb)], kc == 0, kc == KC - 1, kq + [hkeys(kc, tb)], ["PS0"])
                ACT(QX[ks][0][0:64, :], PS[0][0:64, :], AF.Copy, ["PS0"], ["QX%d_0" % ks])
                VCOPY(QX[ks][1][64:128, :], PS[0][64:128, :], ["PS0"], ["QX%d_1" % ks])

            iters = []
            for u, (tb, c) in enumerate(units):
                n_u = 2 * (4 * tb + 4)
                li = 0
                for J in range(4 * tb + 4):
                    for hh in range(2):
                        iters.append((u, J, hh, li))
                        li += 1
            state = {}
            pending = []

            bg = []

            def sq_tmp():
                i = nxt("SF", NSF)
                return TF[i].bitcast(BF16)[:, 0:512], "TF%d" % i

            def stageA(k):
                u, J, hh, li = iters[k]
                tb, c = units[u]
                ks = u % 2
                if bg:
                    bg.pop(0)()
                if li == 1 and u + 1 < len(units):
                    qproj(u + 1)
                if li == 0 and u + 1 < len(units):
                    loads(u + 1, "k")
                if li == PF + 2 and u + 1 < len(units):
                    loads(u + 1, "v")
                n0 = max(0, J - 4 * tb)
                c0 = n0 * 128
                nb = 4 - n0
                st = 1 + nxt("ST", 3)
                MM(PS[st][:, c0:512], KX[ks][hh][:, J * 128:(J + 1) * 128], QX[ks][hh][:, c0:512], True, True,
                   ["KX%d_%d" % (ks, hh), "KXa%d_%d" % (ks, hh), "KXz%d_%d" % (ks, hh),
                    "QX%d_%d" % (ks, hh), "QXa%d_%d" % (ks, hh), "QXz%d_%d" % (ks, hh)], ["PS%d" % st])
                pi = nxt("PT", NPT)
                if J >= 4 * tb:
                    sf = nxt("SF", NSF)
                    TT(SF[sf][:, 0:128], PS[st][:, c0:c0 + 128], NEGMASK, ALU.add, ["PS%d" % st, "CSTF"], ["TF%d" % sf])
                    ACT(PT[pi][:, c0:c0 + 128], SF[sf][:, 0:128], AF.Exp, ["TF%d" % sf], [ptk[pi]])
                    if nb > 1:
                        ACT(PT[pi][:, c0 + 128:512], PS[st][:, c0 + 128:512], AF.Exp, ["PS%d" % st], [ptk[pi]])
                else:
                    ACT(PT[pi][:, c0:512], PS[st][:, c0:512], AF.Exp, ["PS%d" % st], [ptk[pi]])
                state[k] = (pi, c0)

            def outproj(tb):
                for dc in range(KC):
                    for c in range(KC):
                        wo, ko = wo_s[c // 4]
                        MM(PS[0][:], wo[:, c % 4, dc * 128:(dc + 1) * 128], OTS[c], c == 0, c == KC - 1, ko + ["OTS%d" % c], ["PS0"])
                    resid_update(0, l, 2, dc, tb, mk)
                if next_em is not None:
                    bg.extend(norm_pieces(next_em, tb, lambda: 1 + nxt("ST", 3), sq_tmp))

            def stageB(k):
                u, J, hh, li = iters[k]
                tb, c = units[u]
                ks = u % 2
                pi, c0 = state.pop(k)
                last = 4 * tb + 3
                first = (J == 0 and hh == 0)
                final = (J == last and hh == 1)
                bo = 4 + 2 * (u % 2)
                MM(PS[bo][:, c0:512], VX[ks][hh][:, J, :], PT[pi][:, c0:512], first, final,
                   ["VX%d_%d" % (ks, hh), ptk[pi]], ["PS%d" % bo])
                MM(PS[bo + 1][:, c0:512], HALFB[hh], PT[pi][:, c0:512], first, final, ["CSTB", ptk[pi]], ["PS%d" % (bo + 1)])
                if final:
                    RECIP(LT[:], PS[bo + 1][:], ["PS%d" % (bo + 1)], ["TF5"])
                    TT(OTS[c], PS[bo][:], LT[:], ALU.mult, ["PS%d" % bo, "TF5"], ["OTS%d" % c])
                    if c == KC - 1:
                        pending.append([3, tb])
                for p in pending:
                    p[0] -= 1
                while pending and pending[0][0] <= 0:
                    _, a = pending.pop(0)
                    outproj(a)

            loads(0)
            qproj(0)
            n = len(iters)
            for k in range(min(PF, n)):
                stageA(k)
            for k in range(n):
                if k + PF < n:
                    stageA(k + PF)
                stageB(k)
            while pending:
                _, a = pending.pop(0)
                outproj(a)
            while bg:
                bg.pop(0)()

        sc.fence()
        phases = []
        kv_done = False
        for (l, do_mixer, do_mlp) in cfg["layers"]:
            if do_mixer:
                if l < N_A:
                    phases.append(("mlstm", l))
                else:
                    if not kv_done:
                        phases.append(("kv", l))
                        kv_done = True
                    phases.append(("fox", l))
            if do_mlp:
                phases.append(("mlp", l))
        if cfg.get("final_norm", True):
            phases.append(("final", None))
        fuse_norm = cfg.get("fuse_norm", True)
        prenormed = False
        for pi_, (kind, l) in enumerate(phases):
            if l is not None and l not in mod_done:
                for cg in range(12):
                    t, keys = wload(WA, "WA", [(lambda t: sview(t, 0, 8, 512), w_rows(adaw_d[l], 0, D, cg * 512, 512))])
                    mod_chunk(l, cg, sview(t, 0, 8, 512), keys)
                mod_finish(l)
                mod_done.add(l)
            skip = prenormed
            prenormed = False
            mix_next = None
            if fuse_norm and cfg.get("fuse_" + kind, kind == "fox") and kind in ("mlstm", "fox") and pi_ + 1 < len(phases):
                nk, nl_ = phases[pi_ + 1]
                if nk == "mlp" and nl_ in mod_done:
                    mix_next = norm_spec(nl_, "mlp")
                elif nk == "final":
                    mix_next = final_norm_emitters()
            if kind == "mlstm":
                mlstm(l, skip_norm=skip, next_em=mix_next)
                prenormed = mix_next is not None
            elif kind == "kv":
                fox_kv(skip_norm=skip)
            elif kind == "fox":
                fox(l, l - N_A, skip_norm=skip, next_em=mix_next)
                prenormed = mix_next is not None
            elif kind == "mlp":
                next_em = None
                if fuse_norm and pi_ + 1 < len(phases):
                    nk, nl_ = phases[pi_ + 1]
                    if nk in ("mlstm", "fox"):
                        next_em = norm_spec(nl_, "mix")
                    elif nk == "kv":
                        next_em = kv_norm_em()
                    elif nk == "mlp":
                        next_em = norm_spec(nl_, "mlp")
                    elif nk == "final":
                        next_em = final_norm_emitters()
                    if nl_ is not None and nl_ not in mod_done and not (nl_ > l and all(x <= l or x >= nl_ for x in layers_used)):
                        next_em = None
                mlp(l, skip_norm=skip, next_em=next_em)
                prenormed = next_em is not None
            elif kind == "final":
                if not skip:
                    run_norm(final_norm_emitters())

        if not cfg.get("final_norm", True):
            for kc in range(KC):
                out_toks.append(sc.dma("sp", yT_d[kc * 128:(kc + 1) * 128, :], XT[:, kc, :], sc.dsem("out"),
                                       reads=[xkeys(kc, tb) for tb in range(NTB)]))
        sc.wait_all("sp", [out_toks[-1]])
        sc.emit()
    return nc


def _prep_inputs(inputs):
    f = lambda a: np.ascontiguousarray(np.asarray(a, dtype=np.float32))
    x = f(inputs["x"])
    c = f(inputs["c"])
    B = x.shape[0]
    shared = {}
    shared["ada_w"] = f(inputs["ada_w"])
    ab = f(inputs["ada_b"])
    shared["ada_bT"] = f(ab.reshape(DEPTH, 48, 128).transpose(2, 0, 1).reshape(128, DEPTH * 48))
    shared["a_w_in"] = f(inputs["a_w_in"])
    shared["a_b_iT"] = f(f(inputs["a_b_i"]).T)
    shared["a_b_fT"] = f(f(inputs["a_b_f"]).T)
    hg = f(inputs["a_head_gain"])
    shared["a_hgT"] = f(hg.reshape(N_A, A_H, 2, 128).transpose(3, 0, 1, 2).reshape(128, N_A * 8))
    shared["a_w_out"] = f(inputs["a_w_out"])
    shared["kv_gT"] = f(f(inputs["kv_gain"]).reshape(KC, 128).T)
    shared["b_w_kv"] = f(inputs["b_w_kv"])
    shared["b_fgbT"] = f(f(inputs["b_fg_bias"]).reshape(B_H, 1))
    shared["b_w_q"] = f(inputs["b_w_q"])
    shared["b_w_out"] = f(inputs["b_w_out"])
    shared["mlp_w1"] = f(inputs["mlp_w1"])
    shared["mlp_w2"] = f(inputs["mlp_w2"])
    shared["fin_gT"] = f(f(inputs["final_gain"]).reshape(KC, 128).T)
    cstf = np.zeros((128, 1024), np.float32)
    cstf[:, 896:1024] = -30000.0 * np.tril(np.ones((128, 128)), -1)
    cstf[:, 0:128] = np.eye(128)
    cstf[127, 128:256] = 1.0
    cstf[:, 256:384] = 1.0
    for h in range(4):
        cstf[h, 384 + h * 128:384 + (h + 1) * 128] = 1.0
    shared["cstf"] = cstf
    cstb = np.zeros((128, 512), np.float32)
    cstb[:, 0:128] = np.triu(np.ones((128, 128)))
    cstb[:, 128:256] = 1.0
    cstb[:, 256:320] = 1.0
    cstb[:, 448:512] = 1.0
    shared["cstb"] = cstb
    in_maps = []
    for b in range(B):
        m = dict(shared)
        m["xT"] = f(x[b].T)
        m["cT"] = f(c[b].reshape(KC, 128).T)
        in_maps.append(m)
    return in_maps


FULL_CFG = dict(layers=[(l, True, True) for l in range(DEPTH)], final_norm=True)


def run_cfg(inputs, cfg, cores=None, trace=False):
    in_maps = _prep_inputs(inputs)
    if cores is not None:
        in_maps = [in_maps[i] for i in cores]
    nc = build_program(cfg)
    res = run_bass_kernel_spmd(nc, in_maps, core_ids=list(range(len(in_maps))), trace=trace)
    outs = [np.ascontiguousarray(r["yT"].T) for r in res.results]
    out = np.stack(outs, axis=0).astype(np.float32)
    if trace:
        return out, res
    return out


def kernel(**inputs):
    return run_cfg(inputs, FULL_CFG)
```

```python
import numpy as np
from contextlib import ExitStack
import concourse.bass as bass
import concourse.mybir as mybir
from concourse.bass_utils import run_bass_kernel_spmd

F32 = mybir.dt.float32
BF16 = mybir.dt.bfloat16
AF = mybir.ActivationFunctionType
ALU = mybir.AluOpType

S = 2048
D = 1024
KC = 8
NTB = 4
NSB = 16
DEPTH = 4
N_A = 2
A_H = 4
B_H = 16
EPS = 1e-6


class DSem:
    def __init__(self, sem, name):
        self.sem = sem
        self.val = 0
        self.name = name


class Sched:
    def __init__(self, nc, es):
        self.nc = nc
        self.es = es
        self.engs = dict(pe=nc.tensor, act=nc.scalar, dve=nc.vector, pool=nc.gpsimd, sp=nc.sync)
        self.prog = {e: [] for e in self.engs}
        self.pos = {e: 0 for e in self.engs}
        self.waited = {}
        self.last_w = {}
        self.readers = {}
        self.milestones = {e: set() for e in self.engs}
        self.esem = {}
        for e in ("pe", "act", "dve", "pool"):
            self.esem[e] = es.enter_context(nc.semaphore("es_" + e))
        self.dsems = {}
        self.final_tokens = []

    def dsem(self, name):
        if name not in self.dsems:
            self.dsems[name] = DSem(self.es.enter_context(self.nc.semaphore("ds_" + name)), name)
        return self.dsems[name]

    def _deps(self, eng, reads, writes, pe_accum):
        deps = []
        for r in reads:
            t = self.last_w.get(r)
            if t is not None:
                deps.append(t)
        for w in writes:
            t = self.last_w.get(w)
            if t is not None:
                if not (pe_accum and t[0] == "E" and t[1] == "pe"):
                    deps.append(t)
            deps.extend(self.readers.get(w, []))
        out = []
        for t in deps:
            key = (eng, t[0], t[1])
            if self.waited.get(key, 0) >= t[2]:
                continue
            self.waited[key] = t[2]
            out.append(t)
            if t[0] == "E":
                self.milestones[t[1]].add(t[2])
        return out

    def _commit(self, tok, reads, writes):
        for r in reads:
            self.readers.setdefault(r, []).append(tok)
        for w in writes:
            self.last_w[w] = tok
            self.readers[w] = []

    def op(self, eng, meth, kw, reads=(), writes=(), pe_accum=False, args=()):
        fn = (meth, args, kw)
        reads = list(reads)
        writes = list(writes)
        deps = self._deps(eng, reads, writes, pe_accum)
        self.pos[eng] += 1
        p = self.pos[eng]
        tok = ("E", eng, p)
        self.prog[eng].append(("op", deps, fn, p))
        self._commit(tok, reads, writes)
        return tok

    def dma(self, q, out, in_, ds, reads=(), writes=()):
        fn = ("dma_start", (), dict(out=out, in_=in_))
        reads = list(reads)
        writes = list(writes)
        deps = self._deps(q, reads, writes, False)
        ds.val += 16
        tok = ("D", ds.name, ds.val)
        self.prog[q].append(("dma", deps, fn, ds))
        self._commit(tok, reads, writes)
        return tok

    def fence(self, pool=False):
        toks = [("E", e, self.pos[e]) for e in ("pe", "act", "dve", "pool") if self.pos[e] > 0 and any(k == "op" for k, _, _, _ in self.prog[e])]
        for name, ds in self.dsems.items():
            if ds.val > 0 and not name.startswith("W"):
                toks.append(("D", name, ds.val))
        for e in (("pe", "act", "dve", "sp", "pool") if pool else ("pe", "act", "dve", "sp")):
            deps = []
            for t in toks:
                if t[0] == "E" and t[1] == e:
                    continue
                key = (e, t[0], t[1])
                if self.waited.get(key, 0) >= t[2]:
                    continue
                self.waited[key] = t[2]
                deps.append(t)
                if t[0] == "E":
                    self.milestones[t[1]].add(t[2])
            self.prog[e].append(("wait", deps, None, None))

    def wait_all(self, eng, toks):
        self.prog[eng].append(("wait", [t for t in toks], None, None))
        for t in toks:
            if t[0] == "E":
                self.milestones[t[1]].add(t[2])

    def emit(self):
        ms_rank = {}
        for e, ms in self.milestones.items():
            for i, p in enumerate(sorted(ms)):
                ms_rank[(e, p)] = i + 1
        engs = self.engs
        esem = self.esem
        dsems = self.dsems
        milestones = self.milestones

        def replay(ename, eobj):
            for kind, deps, fn, extra in self.prog[ename]:
                for t in deps:
                    if t[0] == "E":
                        eobj.wait_ge(esem[t[1]], ms_rank[(t[1], t[2])])
                    else:
                        eobj.wait_ge(dsems[t[1]].sem, t[2])
                if kind == "op":
                    ins = getattr(eobj, fn[0])(*fn[1], **fn[2])
                    if extra in milestones[ename]:
                        ins.then_inc(esem[ename], 1)
                elif kind == "dma":
                    getattr(eobj, fn[0])(*fn[1], **fn[2]).then_inc(extra.sem, 16)

        with self.nc.Block() as block:
            @block.tensor
            def _(e):
                replay("pe", e)

            @block.scalar
            def _(e):
                replay("act", e)

            @block.vector
            def _(e):
                replay("dve", e)

            @block.gpsimd
            def _(e):
                replay("pool", e)

            @block.sync
            def _(e):
                replay("sp", e)


def build_program(cfg):
    nc = bass.Bass("TRN2", target_bir_lowering=False)
    dt = nc.dram_tensor
    xT_d = dt("xT", [D, S], F32, kind="ExternalInput").ap()
    cT_d = dt("cT", [128, KC], F32, kind="ExternalInput").ap()
    adaw_d = dt("ada_w", [DEPTH, D, 6 * D], F32, kind="ExternalInput").ap()
    adab_d = dt("ada_bT", [128, DEPTH * 48], F32, kind="ExternalInput").ap()
    awin_d = dt("a_w_in", [N_A, D, 3080], F32, kind="ExternalInput").ap()
    abi_d = dt("a_b_iT", [A_H, N_A], F32, kind="ExternalInput").ap()
    abf_d = dt("a_b_fT", [A_H, N_A], F32, kind="ExternalInput").ap()
    ahg_d = dt("a_hgT", [128, N_A * 8], F32, kind="ExternalInput").ap()
    awout_d = dt("a_w_out", [N_A, D, D], F32, kind="ExternalInput").ap()
    kvg_d = dt("kv_gT", [128, KC], F32, kind="ExternalInput").ap()
    bwkv_d = dt("b_w_kv", [D, 2064], F32, kind="ExternalInput").ap()
    bfgb_d = dt("b_fgbT", [B_H, 1], F32, kind="ExternalInput").ap()
    bwq_d = dt("b_w_q", [2, D, D], F32, kind="ExternalInput").ap()
    bwout_d = dt("b_w_out", [2, D, D], F32, kind="ExternalInput").ap()
    w1_d = dt("mlp_w1", [DEPTH, D, 4 * D], F32, kind="ExternalInput").ap()
    w2_d = dt("mlp_w2", [DEPTH, 4 * D, D], F32, kind="ExternalInput").ap()
    fg_d = dt("fin_gT", [128, KC], F32, kind="ExternalInput").ap()
    cstf_d = dt("cstf", [128, 1024], F32, kind="ExternalInput").ap()
    cstb_d = dt("cstb", [128, 512], F32, kind="ExternalInput").ap()
    yT_d = dt("yT", [D, S], F32, kind="ExternalOutput").ap()
    kt_d = dt("kt_scr", [128, KC * S], BF16, kind="Internal").ap()
    v_d = dt("v_scr", [128, NSB * D], BF16, kind="Internal").ap()
    gk_d = dt("gk_scr", [B_H, 3, S], BF16, kind="Internal").ap()
    gq_d = dt("gq_scr", [B_H, 3, S], BF16, kind="Internal").ap()

    es = ExitStack()
    with es:
        def sb(name, shape, dtype):
            return es.enter_context(nc.sbuf_tensor(name, shape, dtype))

        XT = sb("XT", [128, KC, S], F32)
        HT = sb("HT", [128, KC, S], BF16)
        CSTF = sb("CSTF", [128, 1024], F32)
        CSTB = sb("CSTB", [128, 512], BF16)
        IDENT = CSTF[:, 0:128]
        E127 = CSTF[:, 128:256]
        ONESF = CSTF[:, 256:384]
        NEGMASK = CSTF[:, 896:1024]
        MASKB = CSTB[:, 0:128]
        ONESB = CSTB[:, 128:256]
        HALFB = [CSTB[:, 256:384], CSTB[:, 384:512]]
        MODC = sb("MODC", [128, DEPTH * 48], F32)
        ADAB = sb("ADAB", [128, DEPTH * 48], F32)
        CONDF = sb("CONDF", [128, KC], F32)
        CONDS = sb("CONDS", [128, KC], F32)
        CONDB = sb("CONDB", [128, KC], BF16)
        SMALL = sb("SMALL", [128, 64], F32)
        ABI = SMALL[0:4, 0:2]
        ABF = SMALL[0:4, 2:4]
        NABF = SMALL[0:4, 4:6]
        AHG = SMALL[:, 8:24]
        KVG = SMALL[:, 24:32]
        FING = SMALL[:, 32:40]
        FGB = SMALL[0:16, 40:41]
        NFGB = SMALL[0:16, 41:42]
        EPSC = SMALL[:, 48:49]
        ONEC = SMALL[:, 49:50]
        WA = [sb("WA%d" % i, [128, 4096], BF16) for i in range(2)]
        WB = [sb("WB%d" % i, [128, 4096], BF16) for i in range(2)]
        MIXW = 12800
        MIX = sb("MIX", [128, MIXW], F32)
        MIXB = MIX.bitcast(BF16)
        RS = [sb("RS%d" % i, [128, 512], F32) for i in range(2)]
        NTF = 6
        TF = [sb("TF%d" % i, [128, 512], F32) for i in range(NTF)]
        NTBT = 4
        TB = [sb("TB%d" % i, [128, 512], BF16) for i in range(NTBT)]
        CC = sb("CC", [128, 64], F32)
        PS = [es.enter_context(nc.psum_tensor("PS%d" % i, [128, 512], F32)) for i in range(8)]

        sc = Sched(nc, es)
        ADD_ENG = cfg.get('add_eng', 'pool')
        rot = {}

        def nxt(name, n):
            rot[name] = (rot.get(name, -1) + 1) % n
            return rot[name]

        def MM(out, lhsT, rhs, start, stop, reads, writes):
            return sc.op("pe", "matmul", dict(out=out, lhsT=lhsT, rhs=rhs, start=start, stop=stop), reads, writes, pe_accum=True)

        def TR(out, in_, ident, reads, writes):
            return sc.op("pe", "transpose", dict(out=out, in_=in_, identity=ident), reads, writes, pe_accum=True)

        def ACT(out, in_, func, reads, writes, bias=None, scale=None):
            kw = dict(out=out, in_=in_, func=func)
            if bias is not None:
                kw["bias"] = bias
            if scale is not None:
                kw["scale"] = scale
            return sc.op("act", "activation", kw, reads, writes)

        def TT(out, in0, in1, op, reads, writes):
            return sc.op("dve", "tensor_tensor", dict(out=out, in0=in0, in1=in1, op=op), reads, writes)

        def STT(out, in0, scalar, in1, op0, op1, reads, writes):
            return sc.op("dve", "scalar_tensor_tensor", dict(out=out, in0=in0, scalar=scalar, in1=in1, op0=op0, op1=op1), reads, writes)

        def TS(out, in0, s1, op0, reads, writes):
            return sc.op("dve", "tensor_scalar", dict(out=out, in0=in0, scalar1=s1, scalar2=None, op0=op0), reads, writes)

        def RECIP(out, in_, reads, writes):
            return sc.op("dve", "reciprocal", dict(out=out, in_=in_), reads, writes)

        def VCOPY(out, in_, reads, writes):
            return sc.op("dve", "tensor_copy", dict(out=out, in_=in_), reads, writes)

        def SCAN(out, d0, d1, op0, op1, reads, writes):
            return sc.op("dve", "tensor_tensor_scan", dict(out=out, data0=d0, data1=d1, initial=0.0, op0=op0, op1=op1), reads, writes)

        def wkeys(name, i):
            return ["%s%d_%d" % (name, i, q) for q in range(4)]

        def wload(slots, name, parts):
            i = nxt(name, len(slots))
            t = slots[i]
            keys = wkeys(name, i)
            ds = sc.dsem("%s%d" % (name, i))
            tok = None
            for n, (dv, src) in enumerate(parts):
                tok = sc.dma("pool", dv(t), src, ds, reads=[], writes=(keys if n == 0 else []))
            for k in keys:
                sc.last_w[k] = tok
            return t, keys

        def w_rows(src2d, r0, nr, c0, ncw):
            return src2d[r0:r0 + nr, c0:c0 + ncw].rearrange("(kc p) n -> p kc n", p=128)

        def sview(t, off, nk, ncw):
            return t[:, off:off + nk * ncw].rearrange("p (kc n) -> p kc n", kc=nk)

        def small_dma(out, in_, name, writes):
            sc.dma("sp", out, in_, sc.dsem(name), writes=writes)

        small_dma(CSTF[:], cstf_d, "cstf", ["CSTF"])
        sc.dma("pool", CSTB[:], cstb_d, sc.dsem("cstb"), writes=["CSTB"])
        small_dma(ADAB[:], adab_d, "adab", ["ADAB"])
        small_dma(CONDF[:], cT_d, "condf", ["CONDF"])
        small_dma(ABI, abi_d, "abi", ["ABI"])
        small_dma(ABF, abf_d, "abf", ["ABF"])
        small_dma(AHG, ahg_d, "ahg", ["AHG"])
        small_dma(KVG, kvg_d, "kvg", ["KVG"])
        small_dma(FING, fg_d, "fing", ["FING"])
        small_dma(FGB, bfgb_d, "fgb", ["FGB"])
        xkeys = lambda kc, tb: "XT%d_%d" % (kc, tb)
        hkeys = lambda kc, tb: "HT%d_%d" % (kc, tb)
        for kc in range(KC):
            sc.dma("sp", XT[:, kc, :], xT_d[kc * 128:(kc + 1) * 128, :], sc.dsem("x%d" % kc),
                   writes=[xkeys(kc, tb) for tb in range(NTB)])
        TS(NABF, ABF, -1.0, ALU.mult, ["ABF"], ["NABF"])
        TS(NFGB, FGB, -1.0, ALU.mult, ["FGB"], ["NFGB"])
        sc.op("dve", "memset", {}, [], ["EPSC"], args=(EPSC, EPS))
        sc.op("dve", "memset", {}, [], ["ONEC"], args=(ONEC, 1.0))
        ACT(CONDS[:], CONDF[:], AF.Sigmoid, ["CONDF"], ["CONDS"])
        TT(CONDB[:], CONDF[:], CONDS[:], ALU.mult, ["CONDF", "CONDS"], ["CONDB"])

        layers_used = sorted(set(l for (l, _, _) in cfg["layers"]))

        def mod_chunk(l, cg, wv, keys):
            for j in range(4):
                col = l * 48 + cg * 4 + j
                for kc in range(KC):
                    MM(PS[7][:, col:col + 1], wv[:, kc, j * 128:(j + 1) * 128], CONDB[:, kc:kc + 1],
                       kc == 0, kc == KC - 1, keys + ["CONDB"], ["PS7"])

        def mkeys(l, vs):
            return ["MODC%dv%d" % (l, v) for v in vs]

        def mod_finish_vec(l, v):
            a0 = l * 48 + v * 8
            TT(MODC[:, a0:a0 + 8], PS[7][:, a0:a0 + 8], ADAB[:, a0:a0 + 8], ALU.add, ["PS7", "ADAB"], mkeys(l, [v]))
            if v in (1, 4):
                TS(MODC[:, a0:a0 + 8], MODC[:, a0:a0 + 8], 1.0, ALU.add, mkeys(l, [v]), mkeys(l, [v]))

        def mod_finish(l):
            allk = mkeys(l, range(6))
            TT(MODC[:, l * 48:(l + 1) * 48], PS[7][:, l * 48:(l + 1) * 48], ADAB[:, l * 48:(l + 1) * 48], ALU.add,
               ["PS7", "ADAB"], allk)
            for v in (1, 4):
                a0 = l * 48 + v * 8
                TS(MODC[:, a0:a0 + 8], MODC[:, a0:a0 + 8], 1.0, ALU.add, mkeys(l, [v]), mkeys(l, [v]))

        mod_done = set()
        overlap_mod = cfg.get("overlap_mod", True)
        for l in (layers_used[:1] if overlap_mod else layers_used):
            for cg in range(12):
                t, keys = wload(WA, "WA", [(lambda t: sview(t, 0, 8, 512), w_rows(adaw_d[l], 0, D, cg * 512, 512))])
                mod_chunk(l, cg, sview(t, 0, 8, 512), keys)
                if cg % 2 == 1:
                    mod_finish_vec(l, cg // 2)
            mod_done.add(l)

        def modcol(l, v, kc):
            c = l * 48 + v * 8 + kc
            return MODC[:, c:c + 1]

        def tbs(tb):
            return slice(tb * 512, (tb + 1) * 512)

        def rstd_block(tb):
            for kc in range(KC):
                i = nxt("TB", NTBT)
                ACT(TB[i][:], XT[:, kc, tbs(tb)], AF.Square, [xkeys(kc, tb)], ["TB%d" % i])
                MM(PS[6][:], ONESB, TB[i][:], kc == 0, kc == KC - 1, ["TB%d" % i, "CSTB"], ["PS6"])
            r = nxt("RS", 2)
            ACT(RS[r][:], PS[6][:], AF.Ln, ["PS6", "EPSC"], ["RS%d" % r], bias=EPSC, scale=1.0 / D)
            ACT(RS[r][:], RS[r][:], AF.Exp, ["RS%d" % r], ["RS%d" % r], scale=-0.5)
            return r

        def make_norm(scale_col, shift_col, pkeys):
            def modulate(tb, r):
                for kc in range(KC):
                    i = nxt("TF", NTF)
                    STT(TF[i][:], XT[:, kc, tbs(tb)], scale_col(kc), RS[r][:], ALU.mult, ALU.mult,
                        [xkeys(kc, tb), "RS%d" % r] + pkeys, ["TF%d" % i])
                    if shift_col is not None:
                        ACT(HT[:, kc, tbs(tb)], TF[i][:], AF.Identity, ["TF%d" % i] + pkeys, [hkeys(kc, tb)],
                            bias=shift_col(kc), scale=1.0)
                    else:
                        ACT(HT[:, kc, tbs(tb)], TF[i][:], AF.Copy, ["TF%d" % i], [hkeys(kc, tb)])
            return rstd_block, modulate, ("std", scale_col, shift_col, pkeys)

        out_toks = []

        def final_norm_emitters():
            def modulate(tb, r):
                for kc in range(KC):
                    STT(XT[:, kc, tbs(tb)], XT[:, kc, tbs(tb)], FING[:, kc:kc + 1], RS[r][:], ALU.mult, ALU.mult,
                        [xkeys(kc, tb), "RS%d" % r, "FING"], [xkeys(kc, tb)])
                    out_toks.append(sc.dma("sp", yT_d[kc * 128:(kc + 1) * 128, tbs(tb)], XT[:, kc, tbs(tb)], sc.dsem("out"),
                                           reads=[xkeys(kc, tb)]))
            return rstd_block, modulate, ("final", None, None, [])

        def norm_pieces(em, tb, bank_fn, tmp_fn):
            kind, scale_col, shift_col, pkeys = em[2]
            st = {}
            pcs = []

            def stats_all():
                b = bank_fn()
                for kc in range(KC):
                    tq, tk_ = tmp_fn()
                    if kc % 2 == 0:
                        ACT(tq, XT[:, kc, tbs(tb)], AF.Square, [xkeys(kc, tb)], [tk_])
                    else:
                        TT(tq, XT[:, kc, tbs(tb)], XT[:, kc, tbs(tb)], ALU.mult, [xkeys(kc, tb)], [tk_])
                    MM(PS[b][:], ONESB, tq, kc == 0, kc == KC - 1, [tk_, "CSTB"], ["PS%d" % b])
                r = nxt("RS", 2)
                st["r"] = r
                ACT(RS[r][:], PS[b][:], AF.Ln, ["PS%d" % b, "EPSC"], ["RS%d" % r], bias=EPSC, scale=1.0 / D)
                ACT(RS[r][:], RS[r][:], AF.Exp, ["RS%d" % r], ["RS%d" % r], scale=-0.5)

            def mod(kc):
                r = st["r"]
                if kind == "final":
                    STT(XT[:, kc, tbs(tb)], XT[:, kc, tbs(tb)], FING[:, kc:kc + 1], RS[r][:], ALU.mult, ALU.mult,
                        [xkeys(kc, tb), "RS%d" % r, "FING"], [xkeys(kc, tb)])
                    out_toks.append(sc.dma("sp", yT_d[kc * 128:(kc + 1) * 128, tbs(tb)], XT[:, kc, tbs(tb)], sc.dsem("out"),
                                           reads=[xkeys(kc, tb)]))
                    return
                i = nxt("TF", NTF)
                STT(TF[i][:], XT[:, kc, tbs(tb)], scale_col(kc), RS[r][:], ALU.mult, ALU.mult,
                    [xkeys(kc, tb), "RS%d" % r] + pkeys, ["TF%d" % i])
                if shift_col is not None:
                    ACT(HT[:, kc, tbs(tb)], TF[i][:], AF.Identity, ["TF%d" % i] + pkeys, [hkeys(kc, tb)],
                        bias=shift_col(kc), scale=1.0)
                else:
                    ACT(HT[:, kc, tbs(tb)], TF[i][:], AF.Copy, ["TF%d" % i], [hkeys(kc, tb)])

            pcs.append(stats_all)
            for kc in range(KC):
                pcs.append(lambda kc=kc: mod(kc))
            return pcs

        def run_norm(em):
            stats, modulate = em[0], em[1]
            rr = {0: stats(0)}
            for tb in range(NTB):
                if tb + 1 < NTB:
                    rr[tb + 1] = stats(tb + 1)
                modulate(tb, rr[tb])

        def norm_to_HT(scale_col, shift_col, pkeys):
            run_norm(make_norm(scale_col, shift_col, pkeys))

        def norm_spec(l, which):
            if which == "mix":
                return make_norm(lambda kc: modcol(l, 1, kc), lambda kc: modcol(l, 0, kc), mkeys(l, [1, 0]))
            return make_norm(lambda kc: modcol(l, 4, kc), lambda kc: modcol(l, 3, kc), mkeys(l, [4, 3]))

        def acc_bank():
            return nxt("ACC", 2)

        def resid_update(b, l, gvec, dc, tb, mk):
            STT(XT[:, dc, tbs(tb)], PS[b][:], modcol(l, gvec, dc), XT[:, dc, tbs(tb)], ALU.mult, ALU.add,
                ["PS%d" % b, xkeys(dc, tb)] + mkeys(l, [gvec]), [xkeys(dc, tb)])

        def mlp(l, skip_norm=False, next_em=None):
            mk = ["MODC%d" % l]
            later = [x for x in layers_used if x > l and x not in mod_done]
            nl = later[0] if later else None
            if not skip_norm:
                run_norm(norm_spec(l, "mlp"))
            sc.fence(pool=(nl is not None))
            H1 = MIXB[:, 0:4 * S].rearrange("p (f t) -> p f t", f=4)
            MS = [MIXB[:, 8192 + i * 4096:8192 + (i + 1) * 4096] for i in range(2)]

            def ada_chunk(cg):
                i = nxt("MS", 2)
                keys = ["MS%d" % i]
                wv = MS[i].rearrange("p (kc n) -> p kc n", kc=8)
                sc.dma("pool", wv, w_rows(adaw_d[nl], 0, D, cg * 512, 512), sc.dsem("WMS%d" % i), reads=[], writes=keys)
                mod_chunk(nl, cg, wv, keys)

            for g in range(8):
                t1, k1 = wload(WA, "WA", [(lambda t: sview(t, 0, 8, 512), w_rows(w1_d[l], 0, D, g * 512, 512))])
                t2, k2 = wload(WB, "WB", [(lambda t: sview(t, 0, 4, 1024), w_rows(w2_d[l], g * 512, 512, 0, 1024))])
                w1v = sview(t1, 0, 8, 512)
                w2v = sview(t2, 0, 4, 1024)
                cgs = [c_ for c_ in (2 * g, 2 * g + 1) if c_ < 12] if nl is not None else []
                for tb in range(NTB):
                    for f in range(4):
                        b = nxt("ACC4", 4)
                        for kc in range(KC):
                            MM(PS[b][:], w1v[:, kc, f * 128:(f + 1) * 128], HT[:, kc, tbs(tb)], kc == 0, kc == KC - 1,
                               k1 + [hkeys(kc, tb)], ["PS%d" % b])
                        i = nxt("TF", NTF)
                        ACT(TF[i][:], PS[b][:], AF.Relu, ["PS%d" % b], ["TF%d" % i])
                        TT(H1[:, f, tbs(tb)], TF[i][:], TF[i][:], ALU.mult, ["TF%d" % i], ["H1_%d_%d" % (f, tb)])
                    if tb == 1 and cgs:
                        ada_chunk(cgs[0])
                for tb in range(NTB):
                    for dc in range(KC):
                        b = nxt("ACC4", 4)
                        for f in range(4):
                            MM(PS[b][:], w2v[:, f, dc * 128:(dc + 1) * 128], H1[:, f, tbs(tb)], f == 0, f == 3,
                               k2 + ["H1_%d_%d" % (f, tb)], ["PS%d" % b])
                        resid_update(b, l, 5, dc, tb, mk)
                    if tb == 1 and len(cgs) > 1:
                        ada_chunk(cgs[1])
                    if g == 7 and next_em is not None:
                        next_em[1](tb, next_em[0](tb))
                if g == 6 and nl is not None:
                    mod_finish(nl)
                    mod_done.add(nl)

        def mlstm(l, skip_norm=False, next_em=None):
            mk = ["MODC%d" % l]
            if not skip_norm:
                run_norm(norm_spec(l, "mix"))
            sc.fence()
            T_i = MIX[0:4, 0:2048]
            T_l = MIX[0:4, 2048:4096]
            T_F = MIX[0:4, 4096:6144]
            T_m = MIX[0:4, 6144:8192]
            EBt = MIX[:, 0:2048]
            ABt = MIX[:, 2048:4096]
            QH = MIXB[:, 16384:18432]
            KH = MIXB[:, 18432:20480]
            VH = MIXB[:, 20480:24576].rearrange("p (s e) -> p s e", s=16)
            AHt = MIXB[:, 24576:25600].rearrange("p (a t) -> p a t", a=2)
            rk = lambda r, tb: "R%d_%d" % (r, tb)
            allr = lambda r: [rk(r, tb) for tb in range(NTB)]
            tg, kg = wload(WA, "WA", [(lambda t: sview(t, 0, 8, 8), w_rows(awin_d[l], 0, D, 3072, 8))])
            wg = sview(tg, 0, 8, 8)
            for tb in range(NTB):
                b = acc_bank()
                for kc in range(KC):
                    MM(PS[b][0:4, :], wg[:, kc, 0:4], HT[:, kc, tbs(tb)], kc == 0, kc == KC - 1, kg + [hkeys(kc, tb)], ["PS%d" % b])
                ACT(T_i[:, tbs(tb)], PS[b][0:4, :], AF.Identity, ["PS%d" % b, "ABI"], [rk(0, tb)], bias=ABI[:, l:l + 1], scale=1.0)
                b = acc_bank()
                for kc in range(KC):
                    MM(PS[b][0:4, :], wg[:, kc, 4:8], HT[:, kc, tbs(tb)], kc == 0, kc == KC - 1, kg + [hkeys(kc, tb)], ["PS%d" % b])
                ACT(T_l[:, tbs(tb)], PS[b][0:4, :], AF.Exp, ["PS%d" % b, "NABF"], [rk(1, tb)], bias=NABF[:, l:l + 1], scale=-1.0)
            ACT(T_l, T_l, AF.Ln, allr(1) + ["ONEC"], allr(1), bias=ONEC[0:4, :], scale=1.0)
            TS(T_l, T_l, -1.0, ALU.mult, allr(1), allr(1))
            SCAN(T_F, T_l, T_l, ALU.add, ALU.min, allr(1), allr(2))
            SCAN(T_m, T_l, T_i, ALU.add, ALU.max, allr(1) + allr(0), allr(3))
            TT(T_i, T_i, T_F, ALU.subtract, allr(0) + allr(2), allr(0))
            TT(T_F, T_F, T_m, ALU.subtract, allr(2) + allr(3), allr(2))
            TS(T_m, T_m, -1.0, ALU.mult, allr(3), allr(3))
            for J in range(NSB):
                TR(PS[7][:, J * 4:J * 4 + 4], T_i[:, J * 128:(J + 1) * 128], CSTF[0:4, 0:4], [rk(0, J // 4), "CSTF"], ["PS7"])
            ACT(CC[:, 0:64], PS[7][:, 0:64], AF.Copy, ["PS7"], ["CC"])
            X8 = [MIX[:, k * 512:(k + 1) * 512] for k in range(8)]
            ABd, EBd = X8[0:2], X8[2:4]
            N0t, N1t, DAt, RSt = X8[4], X8[5], X8[6], X8[7]
            x8k = ["X8_%d" % k for k in range(8)]
            sc.fence()
            PFm = cfg.get("ml_pf", 3)
            NPTm = 4

            def acc2():
                return (0, 7)[nxt("ACC2", 2)]

            def head_weights(h):
                tA, kA = wload(WA, "WA", [
                    (lambda t: sview(t, 0, 8, 512)[:, :, 0:128], w_rows(awin_d[l], 0, D, h * 128, 128)),
                    (lambda t: sview(t, 0, 8, 512)[:, :, 128:256], w_rows(awin_d[l], 0, D, 512 + h * 128, 128)),
                    (lambda t: sview(t, 0, 8, 512)[:, :, 256:512], w_rows(awin_d[l], 0, D, 1024 + h * 256, 256))])
                tB, kB = wload(WB, "WB", [
                    (lambda t: sview(t, 0, 8, 256), w_rows(awin_d[l], 0, D, 2048 + h * 256, 256)),
                    (lambda t: sview(t, 2048, 2, 1024), w_rows(awout_d[l], h * 256, 256, 0, 1024))])
                return tA, kA, tB, kB

            hw = {0: head_weights(0)}
            bgm = []

            def sq_tmp_m():
                i = nxt("TF", NTF)
                return TF[i].bitcast(BF16)[:, 0:512], "TF%d" % i
            for h in range(A_H):
                tA, kA, tB, kB = hw[h]
                if h + 1 < A_H:
                    hw[h + 1] = head_weights(h + 1)
                wA = sview(tA, 0, 8, 512)
                wo = sview(tB, 0, 8, 256)
                wout = sview(tB, 2048, 2, 1024)
                oh = CSTF[0:4, 384 + h * 128:384 + (h + 1) * 128]

                def gen_ab(tb, h=h, oh=oh):
                    d = tb % 2
                    b = acc2()
                    MM(PS[b][:], oh, T_F[:, tbs(tb)], True, True, ["CSTF", rk(2, tb)], ["PS%d" % b])
                    ACT(ABd[d], PS[b][:], AF.Copy, ["PS%d" % b], [x8k[d]])
                    b = acc2()
                    MM(PS[b][:], oh, T_m[:, tbs(tb)], True, True, ["CSTF", rk(3, tb)], ["PS%d" % b])
                    ACT(EBd[d], PS[b][:], AF.Exp, ["PS%d" % b], [x8k[2 + d]])

                for tb in range(NTB):
                    b = acc2()
                    for kc in range(KC):
                        MM(PS[b][:], wA[:, kc, 0:128], HT[:, kc, tbs(tb)], kc == 0, kc == KC - 1, kA + [hkeys(kc, tb)], ["PS%d" % b])
                    ACT(QH[:, tbs(tb)], PS[b][:], AF.Identity, ["PS%d" % b], ["QH%d" % tb], scale=float(128 ** -0.5))
                    b = acc2()
                    for kc in range(KC):
                        MM(PS[b][:], wA[:, kc, 128:256], HT[:, kc, tbs(tb)], kc == 0, kc == KC - 1, kA + [hkeys(kc, tb)], ["PS%d" % b])
                    VCOPY(KH[:, tbs(tb)], PS[b][:], ["PS%d" % b], ["KH%d" % tb])
                for sbk in range(NSB):
                    b = acc2()
                    for kc in range(KC):
                        MM(PS[b][:, 0:256], HT[:, kc, sbk * 128:(sbk + 1) * 128], wA[:, kc, 256:512], kc == 0, kc == KC - 1,
                           kA + [hkeys(kc, sbk // 4)], ["PS%d" % b])
                    if sbk % 2 == 0:
                        VCOPY(VH[:, sbk, :], PS[b][:, 0:256], ["PS%d" % b], ["VH%d" % sbk])
                    else:
                        ACT(VH[:, sbk, :], PS[b][:, 0:256], AF.Copy, ["PS%d" % b], ["VH%d" % sbk])
                gen_ab(0)

                iters = [(tb, J) for tb in range(NTB) for J in range(4 * tb + 4)]
                state = {}
                pending = []

                def stageA(k, h=h):
                    tb, J = iters[k]
                    for _ in range(2):
                        if bgm:
                            bgm.pop(0)()
                    if J == PFm and tb + 1 < NTB:
                        gen_ab(tb + 1)
                    d = tb % 2
                    n0 = max(0, J - 4 * tb)
                    c0 = n0 * 128
                    st = 1 + nxt("ST", 3)
                    MM(PS[st][:, c0:512], KH[:, J * 128:(J + 1) * 128], QH[:, tb * 512 + c0:(tb + 1) * 512], True, True,
                       ["KH%d" % (J // 4), "QH%d" % tb], ["PS%d" % st])
                    wi = nxt("TF", NTF)
                    ACT(TF[wi][:, c0:512], ABd[d][:, c0:512], AF.Exp, [x8k[d], "CC"], ["TF%d" % wi],
                        bias=CC[:, J * 4 + h:J * 4 + h + 1], scale=1.0)
                    ai = nxt("TB", NPTm)
                    TT(TB[ai][:, c0:512], PS[st][:, c0:512], TF[wi][:, c0:512], ALU.mult, ["PS%d" % st, "TF%d" % wi], ["TB%d" % ai])
                    if J >= 4 * tb:
                        TT(TB[ai][:, c0:c0 + 128], TB[ai][:, c0:c0 + 128], MASKB, ALU.mult, ["TB%d" % ai, "CSTB"], ["TB%d" % ai])
                    state[k] = (ai, c0)

                def post2(tb, h=h, wo=wo, wout=wout, kB=kB):
                    ACT(DAt, DAt, AF.Ln, [x8k[6]], [x8k[6]])
                    ACT(DAt, DAt, AF.Exp, [x8k[6]], [x8k[6]], scale=-1.0)
                    sq = []
                    for e2, Nt, nk in ((0, N0t, x8k[4]), (1, N1t, x8k[5])):
                        sc.op("pool", "tensor_tensor", dict(out=Nt, in0=Nt, in1=DAt, op=ALU.mult), [nk, x8k[6]], [nk])
                        si = nxt("TF", NTF)
                        sc.op("pool", "tensor_tensor", dict(out=TF[si][:], in0=Nt, in1=Nt, op=ALU.mult), [nk], ["TF%d" % si])
                        sq.append(si)
                    gis = []
                    for e2 in range(2):
                        b = acc2()
                        for kc in range(KC):
                            MM(PS[b][:], wo[:, kc, e2 * 128:(e2 + 1) * 128], HT[:, kc, tbs(tb)], kc == 0, kc == KC - 1,
                               kB + [hkeys(kc, tb)], ["PS%d" % b])
                        gi = nxt("TF", NTF)
                        ACT(TF[gi][:], PS[b][:], AF.Sigmoid, ["PS%d" % b], ["TF%d" % gi])
                        gis.append(gi)
                    b = acc2()
                    for e2 in range(2):
                        MM(PS[b][:], ONESF, TF[sq[e2]][:], e2 == 0, e2 == 1, ["CSTF", "TF%d" % sq[e2]], ["PS%d" % b])
                    ACT(RSt, PS[b][:], AF.Ln, ["PS%d" % b, "EPSC"], [x8k[7]], bias=EPSC, scale=1.0 / 256)
                    ACT(RSt, RSt, AF.Exp, [x8k[7]], [x8k[7]], scale=-0.5)
                    for e2, Nt, nk in ((0, N0t, x8k[4]), (1, N1t, x8k[5])):
                        gi = gis[e2]
                        gc = l * 8 + h * 2 + e2
                        STT(Nt, Nt, AHG[:, gc:gc + 1], RSt, ALU.mult, ALU.mult, [nk, "AHG", x8k[7]], [nk])
                        sc.op("pool", "tensor_tensor", dict(out=AHt[:, e2, :], in0=Nt, in1=TF[gi][:], op=ALU.mult),
                              [nk, "TF%d" % gi], ["AH%d" % e2])
                    pending.append([4, ("out", tb)])

                def post3(tb, h=h, wout=wout, kB=kB):
                    for dc in range(KC):
                        b = acc2()
                        MM(PS[b][:], wout[:, 0, dc * 128:(dc + 1) * 128], AHt[:, 0, :], True, False, kB + ["AH0"], ["PS%d" % b])
                        MM(PS[b][:], wout[:, 1, dc * 128:(dc + 1) * 128], AHt[:, 1, :], False, True, kB + ["AH1"], ["PS%d" % b])
                        if dc % 2 == 0 or not cfg.get("split_evac", False):
                            resid_update(b, l, 2, dc, tb, mk)
                        else:
                            ui = nxt("TF", NTF)
                            ACT(TF[ui][:], PS[b][:], AF.Identity, ["PS%d" % b] + mkeys(l, [2]), ["TF%d" % ui], scale=modcol(l, 2, dc))
                            sc.op("pool", "tensor_tensor", dict(out=XT[:, dc, tbs(tb)], in0=XT[:, dc, tbs(tb)], in1=TF[ui][:], op=ALU.add),
                                  ["TF%d" % ui, xkeys(dc, tb)], [xkeys(dc, tb)])
                    if h == A_H - 1 and next_em is not None:
                        bgm.extend(norm_pieces(next_em, tb, acc2, sq_tmp_m))

                def stageB(k):
                    tb, J = iters[k]
                    ai, c0 = state.pop(k)
                    last = 4 * tb + 3
                    MM(PS[4][:, c0:512], VH[:, J, 0:128], TB[ai][:, c0:512], J == 0, J == last, ["VH%d" % J, "TB%d" % ai], ["PS4"])
                    MM(PS[5][:, c0:512], VH[:, J, 128:256], TB[ai][:, c0:512], J == 0, J == last, ["VH%d" % J, "TB%d" % ai], ["PS5"])
                    MM(PS[6][:, c0:512], ONESB, TB[ai][:, c0:512], J == 0, J == last, ["CSTB", "TB%d" % ai], ["PS6"])
                    if J == last:
                        ACT(DAt, PS[6][:], AF.Abs, ["PS6"], [x8k[6]])
                        VCOPY(N0t, PS[4][:], ["PS4"], [x8k[4]])
                        ACT(N1t, PS[5][:], AF.Copy, ["PS5"], [x8k[5]])
                        TT(DAt, DAt, EBd[tb % 2], ALU.max, [x8k[6], x8k[2 + tb % 2]], [x8k[6]])
                        pending.append([3, tb])
                    for p in pending:
                        p[0] -= 1
                    while pending and pending[0][0] <= 0:
                        _, a = pending.pop(0)
                        run_post(a)

                def run_post(a):
                    if isinstance(a, tuple):
                        post3(a[1])
                    else:
                        post2(a)

                n = len(iters)
                for k in range(min(PFm, n)):
                    stageA(k)
                for k in range(n):
                    if k + PFm < n:
                        stageA(k + PFm)
                    stageB(k)
                while pending:
                    _, a = pending.pop(0)
                    run_post(a)
                while bgm:
                    bgm.pop(0)()

        def kv_norm_em():
            return make_norm(lambda kc: KVG[:, kc:kc + 1], None, ["KVG"])

        def fox_kv(skip_norm=False):
            if not skip_norm:
                run_norm(kv_norm_em())
            sc.fence()
            FGr = MIX[0:16, 0:2048]
            Gr = MIX[0:16, 2048:4096]
            KTs = [MIXB[:, 8192:10240], MIXB[:, 10240:12288]]
            tf_, kf = wload(WA, "WA", [(lambda t: sview(t, 0, 8, 16), w_rows(bwkv_d, 0, D, 2048, 16))])
            wf = sview(tf_, 0, 8, 16)
            for tb in range(NTB):
                b = acc_bank()
                for kc in range(KC):
                    MM(PS[b][0:16, :], wf[:, kc, 0:16], HT[:, kc, tbs(tb)], kc == 0, kc == KC - 1, kf + [hkeys(kc, tb)], ["PS%d" % b])
                ACT(FGr[:, tbs(tb)], PS[b][0:16, :], AF.Exp, ["PS%d" % b, "NFGB"], ["FGr"], bias=NFGB, scale=-1.0)
            ACT(FGr, FGr, AF.Ln, ["FGr", "ONEC"], ["FGr"], bias=ONEC[0:16, :], scale=1.0)
            SCAN(Gr, FGr, FGr, ALU.add, ALU.max, ["FGr"], ["Gr"])
            GHt = [MIXB[0:16, 12288 + k * 2048:12288 + (k + 1) * 2048] for k in range(3)]
            NGt = [MIXB[0:16, 18432 + k * 2048:18432 + (k + 1) * 2048] for k in range(3)]
            R1 = FGr
            VCOPY(GHt[0], Gr, ["Gr"], ["GH0"])
            TT(R1, Gr, GHt[0], ALU.subtract, ["Gr", "GH0", "FGr"], ["FGr"])
            VCOPY(GHt[1], R1, ["FGr"], ["GH1"])
            TT(R1, R1, GHt[1], ALU.subtract, ["FGr", "GH1"], ["FGr"])
            VCOPY(GHt[2], R1, ["FGr"], ["GH2"])
            for k in range(3):
                TS(NGt[k], GHt[k], -1.0, ALU.mult, ["GH%d" % k], ["NG%d" % k])
                sc.dma("sp", gk_d[:, k, :], GHt[k], sc.dsem("gh%d" % k), reads=["GH%d" % k], writes=["GKD"])
                sc.dma("sp", gq_d[:, k, :], NGt[k], sc.dsem("ng%d" % k), reads=["NG%d" % k], writes=["GQD"])
            for c in range(KC):
                tk, kk = wload(WA, "WA", [(lambda t: sview(t, 0, 8, 128), w_rows(bwkv_d, 0, D, c * 128, 128))])
                wk = sview(tk, 0, 8, 128)
                kt = KTs[c % 2]
                for tb in range(NTB):
                    b = acc_bank()
                    for kc in range(KC):
                        MM(PS[b][:], wk[:, kc, :], HT[:, kc, tbs(tb)], kc == 0, kc == KC - 1, kk + [hkeys(kc, tb)], ["PS%d" % b])
                    ACT(kt[:, tbs(tb)], PS[b][:], AF.Identity, ["PS%d" % b], ["KTs%d" % (c % 2)], scale=0.125)
                sc.dma("sp", kt_d[:, c * S:(c + 1) * S], kt, sc.dsem("kts%d" % (c % 2)), reads=["KTs%d" % (c % 2)], writes=["KTD%d" % c])
            vdv = v_d.rearrange("p (c s n) -> p c s n", c=8, s=16)
            for half in range(2):
                tv, kv = wload(WB, "WB", [(lambda t: sview(t, 0, 8, 512), w_rows(bwkv_d, 0, D, 1024 + half * 512, 512))])
                wv = sview(tv, 0, 8, 512)
                for sbk in range(NSB):
                    b = acc_bank()
                    for kc in range(KC):
                        MM(PS[b][:], HT[:, kc, sbk * 128:(sbk + 1) * 128], wv[:, kc, :], kc == 0, kc == KC - 1,
                           kv + [hkeys(kc, sbk // 4)], ["PS%d" % b])
                    vi = nxt("TB", NTBT)
                    VCOPY(TB[vi][:], PS[b][:], ["PS%d" % b], ["TB%d" % vi])
                    sc.dma("sp", vdv[:, half * 4:(half + 1) * 4, sbk, :], TB[vi][:].rearrange("p (c n) -> p c n", c=4),
                           sc.dsem("tb%d" % vi), reads=["TB%d" % vi], writes=["VD%d_%d" % (half, sbk)])

        def fox(l, j, skip_norm=False, next_em=None):
            mk = ["MODC%d" % l]
            if not skip_norm:
                run_norm(norm_spec(l, "mix"))
            sc.fence()
            KX = [[MIXB[:, (2 * ks + hh) * 2048:(2 * ks + hh + 1) * 2048] for hh in range(2)] for ks in range(2)]
            VX = [[MIXB[:, 8192 + (2 * ks + hh) * 2048:8192 + (2 * ks + hh + 1) * 2048].rearrange("p (s n) -> p s n", s=16)
                   for hh in range(2)] for ks in range(2)]
            QX = [[MIXB[:, 16384 + (2 * ks + hh) * 512:16384 + (2 * ks + hh + 1) * 512] for hh in range(2)] for ks in range(2)]
            OTS = [MIXB[:, 18432 + c * 512:18432 + (c + 1) * 512] for c in range(KC)]
            NPT = 8
            PT = [TB[k][:] for k in range(4)] + [MIXB[:, 22528 + k * 512:22528 + (k + 1) * 512] for k in range(4)]
            AUG = [64, 0]
            ptk = ["TB%d" % k for k in range(4)] + ["PTm%d" % k for k in range(4)]
            NSF = 5
            SF = [TF[k][:] for k in range(5)]
            LT = TF[5]
            PF = cfg.get("fox_pf", 3)
            vdkeys = ["VD%d_%d" % (hf, s_) for hf in range(2) for s_ in range(NSB)]
            for ks in range(2):
                for hh in range(2):
                    o = 64 * (1 - hh)
                    sc.op("dve", "memset", {}, [], ["VX%d_%d" % (ks, hh)], args=(VX[ks][hh][:, :, o:o + 64], 0.0))
                    sc.op("dve", "memset", {}, [], ["QXz%d_%d" % (ks, hh)], args=(QX[ks][hh][o:o + 64, :], 0.0))
                    sc.op("dve", "memset", {}, [], ["QXz%d_%d" % (ks, hh)], args=(QX[ks][hh][AUG[hh]:AUG[hh] + 6, :], 1.0))
                    sc.op("dve", "memset", {}, [], ["KXz%d_%d" % (ks, hh)], args=(KX[ks][hh][:, :], 0.0))
                    sc.op("dve", "memset", {}, [], ["KXz%d_%d" % (ks, hh)], args=(KX[ks][hh][AUG[hh]:AUG[hh] + 6, :], 1.0))
            wq_s, wo_s = [], []
            for i in range(2):
                t, k = wload(WA, "WA", [(lambda t: sview(t, 0, 8, 512), w_rows(bwq_d[j], 0, D, i * 512, 512))])
                wq_s.append((sview(t, 0, 8, 512), k))
            for i in range(2):
                t, k = wload(WB, "WB", [(lambda t: sview(t, 0, 4, 1024), w_rows(bwout_d[j], i * 512, 512, 0, 1024))])
                wo_s.append((sview(t, 0, 4, 1024), k))

            units = [(tb, c) for tb in range(NTB) for c in range(KC)]

            def loads(u, part="all"):
                tb, c = units[u]
                ks = u % 2
                nJ = 4 * tb + 4
                for hh in (range(2) if part in ("all", "k") else []):
                    h = 2 * c + hh
                    pp = slice(hh * 64, (hh + 1) * 64)
                    sc.dma("sp", KX[ks][hh][pp, 0:nJ * 128], kt_d[pp, c * S:c * S + nJ * 128], sc.dsem("kx%d_%d" % (ks, hh)),
                           reads=["KTD%d" % c, "KXz%d_%d" % (ks, hh)], writes=["KX%d_%d" % (ks, hh)])
                    a0 = AUG[hh]
                    sc.dma("sp", KX[ks][hh][a0:a0 + 3, 0:nJ * 128], gk_d[h, :, 0:nJ * 128], sc.dsem("kxa%d_%d" % (ks, hh)),
                           reads=["GKD", "KXz%d_%d" % (ks, hh)], writes=["KXa%d_%d" % (ks, hh)])
                    sc.dma("sp", QX[ks][hh][a0 + 3:a0 + 6, :], gq_d[h, :, tb * 512:(tb + 1) * 512], sc.dsem("qxa%d_%d" % (ks, hh)),
                           reads=["GQD", "QXz%d_%d" % (ks, hh)], writes=["QXa%d_%d" % (ks, hh)])
                vsrc = v_d[:, c * 2048:(c + 1) * 2048].rearrange("p (s n) -> p s n", s=16)
                for hh in (range(2) if part in ("all", "v") else []):
                    sc.dma("sp", VX[ks][hh][:, 0:nJ, hh * 64:(hh + 1) * 64], vsrc[:, 0:nJ, hh * 64:(hh + 1) * 64],
                           sc.dsem("vx%d_%d" % (ks, hh)), reads=vdkeys, writes=["VX%d_%d" % (ks, hh)])

            def qproj(u):
                tb, c = units[u]
                ks = u % 2
                wq, kq = wq_s[c // 4]
                cc = (c % 4) * 128
                for kc in range(KC):
                    MM(PS[0][:], wq[:, kc, cc:cc + 128], HT[:, kc, tbs(tb)], kc == 0, kc == KC - 1, kq + [hkeys(kc, tb)], ["PS0"])
                ACT(QX[ks][0][0:64, :], PS[0][0:64, :], AF.Copy, ["PS0"], ["QX%d_0" % ks])
                VCOPY(QX[ks][1][64:128, :], PS[0][64:128, :], ["PS0"], ["QX%d_1" % ks])

            iters = []
            for u, (tb, c) in enumerate(units):
                n_u = 2 * (4 * tb + 4)
                li = 0
                for J in range(4 * tb + 4):
                    for hh in range(2):
                        iters.append((u, J, hh, li))
                        li += 1
            state = {}
            pending = []

            bg = []

            def sq_tmp():
                i = nxt("SF", NSF)
                return TF[i].bitcast(BF16)[:, 0:512], "TF%d" % i

            def stageA(k):
                u, J, hh, li = iters[k]
                tb, c = units[u]
                ks = u % 2
                if bg:
                    bg.pop(0)()
                if li == 1 and u + 1 < len(units):
                    qproj(u + 1)
                if li == 0 and u + 1 < len(units):
                    loads(u + 1, "k")
                if li == PF + 2 and u + 1 < len(units):
                    loads(u + 1, "v")
                n0 = max(0, J - 4 * tb)
                c0 = n0 * 128
                nb = 4 - n0
                st = 1 + nxt("ST", 3)
                MM(PS[st][:, c0:512], KX[ks][hh][:, J * 128:(J + 1) * 128], QX[ks][hh][:, c0:512], True, True,
                   ["KX%d_%d" % (ks, hh), "KXa%d_%d" % (ks, hh), "KXz%d_%d" % (ks, hh),
                    "QX%d_%d" % (ks, hh), "QXa%d_%d" % (ks, hh), "QXz%d_%d" % (ks, hh)], ["PS%d" % st])
                pi = nxt("PT", NPT)
                if J >= 4 * tb:
                    sf = nxt("SF", NSF)
                    TT(SF[sf][:, 0:128], PS[st][:, c0:c0 + 128], NEGMASK, ALU.add, ["PS%d" % st, "CSTF"], ["TF%d" % sf])
                    ACT(PT[pi][:, c0:c0 + 128], SF[sf][:, 0:128], AF.Exp, ["TF%d" % sf], [ptk[pi]])
                    if nb > 1:
                        ACT(PT[pi][:, c0 + 128:512], PS[st][:, c0 + 128:512], AF.Exp, ["PS%d" % st], [ptk[pi]])
                else:
                    ACT(PT[pi][:, c0:512], PS[st][:, c0:512], AF.Exp, ["PS%d" % st], [ptk[pi]])
                state[k] = (pi, c0)

            def outproj(tb):
                for dc in range(KC):
                    for c in range(KC):
                        wo, ko = wo_s[c // 4]
                        MM(PS[0][:], wo[:, c % 4, dc * 128:(dc + 1) * 128], OTS[c], c == 0, c == KC - 1, ko + ["OTS%d" % c], ["PS0"])
                    resid_update(0, l, 2, dc, tb, mk)
                if next_em is not None:
                    bg.extend(norm_pieces(next_em, tb, lambda: 1 + nxt("ST", 3), sq_tmp))

            def stageB(k):
                u, J, hh, li = iters[k]
                tb, c = units[u]
                ks = u % 2
                pi, c0 = state.pop(k)
                last = 4 * tb + 3
                first = (J == 0 and hh == 0)
                final = (J == last and hh == 1)
                bo = 4 + 2 * (u % 2)
                MM(PS[bo][:, c0:512], VX[ks][hh][:, J, :], PT[pi][:, c0:512], first, final,
                   ["VX%d_%d" % (ks, hh), ptk[pi]], ["PS%d" % bo])
                MM(PS[bo + 1][:, c0:512], HALFB[hh], PT[pi][:, c0:512], first, final, ["CSTB", ptk[pi]], ["PS%d" % (bo + 1)])
                if final:
                    RECIP(LT[:], PS[bo + 1][:], ["PS%d" % (bo + 1)], ["TF5"])
                    TT(OTS[c], PS[bo][:], LT[:], ALU.mult, ["PS%d" % bo, "TF5"], ["OTS%d" % c])
                    if c == KC - 1:
                        pending.append([3, tb])
                for p in pending:
                    p[0] -= 1
                while pending and pending[0][0] <= 0:
                    _, a = pending.pop(0)
                    outproj(a)

            loads(0)
            qproj(0)
            n = len(iters)
            for k in range(min(PF, n)):
                stageA(k)
            for k in range(n):
                if k + PF < n:
                    stageA(k + PF)
                stageB(k)
            while pending:
                _, a = pending.pop(0)
                outproj(a)
            while bg:
                bg.pop(0)()

        sc.fence()
        phases = []
        kv_done = False
        for (l, do_mixer, do_mlp) in cfg["layers"]:
            if do_mixer:
                if l < N_A:
                    phases.append(("mlstm", l))
                else:
                    if not kv_done:
                        phases.append(("kv", l))
                        kv_done = True
                    phases.append(("fox", l))
            if do_mlp:
                phases.append(("mlp", l))
        if cfg.get("final_norm", True):
            phases.append(("final", None))
        fuse_norm = cfg.get("fuse_norm", True)
        prenormed = False
        for pi_, (kind, l) in enumerate(phases):
            if l is not None and l not in mod_done:
                for cg in range(12):
                    t, keys = wload(WA, "WA", [(lambda t: sview(t, 0, 8, 512), w_rows(adaw_d[l], 0, D, cg * 512, 512))])
                    mod_chunk(l, cg, sview(t, 0, 8, 512), keys)
                mod_finish(l)
                mod_done.add(l)
            skip = prenormed
            prenormed = False
            mix_next = None
            if fuse_norm and cfg.get("fuse_" + kind, kind == "fox") and kind in ("mlstm", "fox") and pi_ + 1 < len(phases):
                nk, nl_ = phases[pi_ + 1]
                if nk == "mlp" and nl_ in mod_done:
                    mix_next = norm_spec(nl_, "mlp")
                elif nk == "final":
                    mix_next = final_norm_emitters()
            if kind == "mlstm":
                mlstm(l, skip_norm=skip, next_em=mix_next)
                prenormed = mix_next is not None
            elif kind == "kv":
                fox_kv(skip_norm=skip)
            elif kind == "fox":
                fox(l, l - N_A, skip_norm=skip, next_em=mix_next)
                prenormed = mix_next is not None
            elif kind == "mlp":
                next_em = None
                if fuse_norm and pi_ + 1 < len(phases):
                    nk, nl_ = phases[pi_ + 1]
                    if nk in ("mlstm", "fox"):
                        next_em = norm_spec(nl_, "mix")
                    elif nk == "kv":
                        next_em = kv_norm_em()
                    elif nk == "mlp":
                        next_em = norm_spec(nl_, "mlp")
                    elif nk == "final":
                        next_em = final_norm_emitters()
                    if nl_ is not None and nl_ not in mod_done and not (nl_ > l and all(x <= l or x >= nl_ for x in layers_used)):
                        next_em = None
                mlp(l, skip_norm=skip, next_em=next_em)
                prenormed = next_em is not None
            elif kind == "final":
                if not skip:
                    run_norm(final_norm_emitters())

        if not cfg.get("final_norm", True):
            for kc in range(KC):
                out_toks.append(sc.dma("sp", yT_d[kc * 128:(kc + 1) * 128, :], XT[:, kc, :], sc.dsem("out"),
                                       reads=[xkeys(kc, tb) for tb in range(NTB)]))
        sc.wait_all("sp", [out_toks[-1]])
        sc.emit()
    return nc


def _prep_inputs(inputs):
    f = lambda a: np.ascontiguousarray(np.asarray(a, dtype=np.float32))
    x = f(inputs["x"])
    c = f(inputs["c"])
    B = x.shape[0]
    shared = {}
    shared["ada_w"] = f(inputs["ada_w"])
    ab = f(inputs["ada_b"])
    shared["ada_bT"] = f(ab.reshape(DEPTH, 48, 128).transpose(2, 0, 1).reshape(128, DEPTH * 48))
    shared["a_w_in"] = f(inputs["a_w_in"])
    shared["a_b_iT"] = f(f(inputs["a_b_i"]).T)
    shared["a_b_fT"] = f(f(inputs["a_b_f"]).T)
    hg = f(inputs["a_head_gain"])
    shared["a_hgT"] = f(hg.reshape(N_A, A_H, 2, 128).transpose(3, 0, 1, 2).reshape(128, N_A * 8))
    shared["a_w_out"] = f(inputs["a_w_out"])
    shared["kv_gT"] = f(f(inputs["kv_gain"]).reshape(KC, 128).T)
    shared["b_w_kv"] = f(inputs["b_w_kv"])
    shared["b_fgbT"] = f(f(inputs["b_fg_bias"]).reshape(B_H, 1))
    shared["b_w_q"] = f(inputs["b_w_q"])
    shared["b_w_out"] = f(inputs["b_w_out"])
    shared["mlp_w1"] = f(inputs["mlp_w1"])
    shared["mlp_w2"] = f(inputs["mlp_w2"])
    shared["fin_gT"] = f(f(inputs["final_gain"]).reshape(KC, 128).T)
    cstf = np.zeros((128, 1024), np.float32)
    cstf[:, 896:1024] = -30000.0 * np.tril(np.ones((128, 128)), -1)
    cstf[:, 0:128] = np.eye(128)
    cstf[127, 128:256] = 1.0
    cstf[:, 256:384] = 1.0
    for h in range(4):
        cstf[h, 384 + h * 128:384 + (h + 1) * 128] = 1.0
    shared["cstf"] = cstf
    cstb = np.zeros((128, 512), np.float32)
    cstb[:, 0:128] = np.triu(np.ones((128, 128)))
    cstb[:, 128:256] = 1.0
    cstb[:, 256:320] = 1.0
    cstb[:, 448:512] = 1.0
    shared["cstb"] = cstb
    in_maps = []
    for b in range(B):
        m = dict(shared)
        m["xT"] = f(x[b].T)
        m["cT"] = f(c[b].reshape(KC, 128).T)
        in_maps.append(m)
    return in_maps


FULL_CFG = dict(layers=[(l, True, True) for l in range(DEPTH)], final_norm=True)


def run_cfg(inputs, cfg, cores=None, trace=False):
    in_maps = _prep_inputs(inputs)
    if cores is not None:
        in_maps = [in_maps[i] for i in cores]
    nc = build_program(cfg)
    res = run_bass_kernel_spmd(nc, in_maps, core_ids=list(range(len(in_maps))), trace=trace)
    outs = [np.ascontiguousarray(r["yT"].T) for r in res.results]
    out = np.stack(outs, axis=0).astype(np.float32)
    if trace:
        return out, res
    return out


def kernel(**inputs):
    return run_cfg(inputs, FULL_CFG)
```

```python
import numpy as np
from contextlib import ExitStack
import concourse.bass as bass
import concourse.mybir as mybir
from concourse.bass_utils import run_bass_kernel_spmd

F32 = mybir.dt.float32
BF16 = mybir.dt.bfloat16
AF = mybir.ActivationFunctionType
ALU = mybir.AluOpType

S = 2048
D = 1024
KC = 8
NTB = 4
NSB = 16
DEPTH = 4
N_A = 2
A_H = 4
B_H = 16
EPS = 1e-6


class DSem:
    def __init__(self, sem, name):
        self.sem = sem
        self.val = 0
        self.name = name


class Sched:
    def __init__(self, nc, es):
        self.nc = nc
        self.es = es
        self.engs = dict(pe=nc.tensor, act=nc.scalar, dve=nc.vector, pool=nc.gpsimd, sp=nc.sync)
        self.prog = {e: [] for e in self.engs}
        self.pos = {e: 0 for e in self.engs}
        self.waited = {}
        self.last_w = {}
        self.readers = {}
        self.milestones = {e: set() for e in self.engs}
        self.esem = {}
        for e in ("pe", "act", "dve", "pool"):
            self.esem[e] = es.enter_context(nc.semaphore("es_" + e))
        self.dsems = {}
        self.final_tokens = []

    def dsem(self, name):
        if name not in self.dsems:
            self.dsems[name] = DSem(self.es.enter_context(self.nc.semaphore("ds_" + name)), name)
        return self.dsems[name]

    def _deps(self, eng, reads, writes, pe_accum):
        deps = []
        for r in reads:
            t = self.last_w.get(r)
            if t is not None:
                deps.append(t)
        for w in writes:
            t = self.last_w.get(w)
            if t is not None:
                if not (pe_accum and t[0] == "E" and t[1] == "pe"):
                    deps.append(t)
            deps.extend(self.readers.get(w, []))
        out = []
        for t in deps:
            key = (eng, t[0], t[1])
            if self.waited.get(key, 0) >= t[2]:
                continue
            self.waited[key] = t[2]
            out.append(t)
            if t[0] == "E":
                self.milestones[t[1]].add(t[2])
        return out

    def _commit(self, tok, reads, writes):
        for r in reads:
            self.readers.setdefault(r, []).append(tok)
        for w in writes:
            self.last_w[w] = tok
            self.readers[w] = []

    def op(self, eng, meth, kw, reads=(), writes=(), pe_accum=False, args=()):
        fn = (meth, args, kw)
        reads = list(reads)
        writes = list(writes)
        deps = self._deps(eng, reads, writes, pe_accum)
        self.pos[eng] += 1
        p = self.pos[eng]
        tok = ("E", eng, p)
        self.prog[eng].append(("op", deps, fn, p))
        self._commit(tok, reads, writes)
        return tok

    def dma(self, q, out, in_, ds, reads=(), writes=()):
        fn = ("dma_start", (), dict(out=out, in_=in_))
        reads = list(reads)
        writes = list(writes)
        deps = self._deps(q, reads, writes, False)
        ds.val += 16
        tok = ("D", ds.name, ds.val)
        self.prog[q].append(("dma", deps, fn, ds))
        self._commit(tok, reads, writes)
        return tok

    def fence(self, pool=False):
        toks = [("E", e, self.pos[e]) for e in ("pe", "act", "dve", "pool") if self.pos[e] > 0 and any(k == "op" for k, _, _, _ in self.prog[e])]
        for name, ds in self.dsems.items():
            if ds.val > 0 and not name.startswith("W"):
                toks.append(("D", name, ds.val))
        for e in (("pe", "act", "dve", "sp", "pool") if pool else ("pe", "act", "dve", "sp")):
            deps = []
            for t in toks:
                if t[0] == "E" and t[1] == e:
                    continue
                key = (e, t[0], t[1])
                if self.waited.get(key, 0) >= t[2]:
                    continue
                self.waited[key] = t[2]
                deps.append(t)
                if t[0] == "E":
                    self.milestones[t[1]].add(t[2])
            self.prog[e].append(("wait", deps, None, None))

    def wait_all(self, eng, toks):
        self.prog[eng].append(("wait", [t for t in toks], None, None))
        for t in toks:
            if t[0] == "E":
                self.milestones[t[1]].add(t[2])

    def emit(self):
        ms_rank = {}
        for e, ms in self.milestones.items():
            for i, p in enumerate(sorted(ms)):
                ms_rank[(e, p)] = i + 1
        engs = self.engs
        esem = self.esem
        dsems = self.dsems
        milestones = self.milestones

        def replay(ename, eobj):
            for kind, deps, fn, extra in self.prog[ename]:
                for t in deps:
                    if t[0] == "E":
                        eobj.wait_ge(esem[t[1]], ms_rank[(t[1], t[2])])
                    else:
                        eobj.wait_ge(dsems[t[1]].sem, t[2])
                if kind == "op":
                    ins = getattr(eobj, fn[0])(*fn[1], **fn[2])
                    if extra in milestones[ename]:
                        ins.then_inc(esem[ename], 1)
                elif kind == "dma":
                    getattr(eobj, fn[0])(*fn[1], **fn[2]).then_inc(extra.sem, 16)

        with self.nc.Block() as block:
            @block.tensor
            def _(e):
                replay("pe", e)

            @block.scalar
            def _(e):
                replay("act", e)

            @block.vector
            def _(e):
                replay("dve", e)

            @block.gpsimd
            def _(e):
                replay("pool", e)

            @block.sync
            def _(e):
                replay("sp", e)


def build_program(cfg):
    nc = bass.Bass("TRN2", target_bir_lowering=False)
    dt = nc.dram_tensor
    xT_d = dt("xT", [D, S], F32, kind="ExternalInput").ap()
    cT_d = dt("cT", [128, KC], F32, kind="ExternalInput").ap()
    adaw_d = dt("ada_w", [DEPTH, D, 6 * D], F32, kind="ExternalInput").ap()
    adab_d = dt("ada_bT", [128, DEPTH * 48], F32, kind="ExternalInput").ap()
    awin_d = dt("a_w_in", [N_A, D, 3080], F32, kind="ExternalInput").ap()
    abi_d = dt("a_b_iT", [A_H, N_A], F32, kind="ExternalInput").ap()
    abf_d = dt("a_b_fT", [A_H, N_A], F32, kind="ExternalInput").ap()
    ahg_d = dt("a_hgT", [128, N_A * 8], F32, kind="ExternalInput").ap()
    awout_d = dt("a_w_out", [N_A, D, D], F32, kind="ExternalInput").ap()
    kvg_d = dt("kv_gT", [128, KC], F32, kind="ExternalInput").ap()
    bwkv_d = dt("b_w_kv", [D, 2064], F32, kind="ExternalInput").ap()
    bfgb_d = dt("b_fgbT", [B_H, 1], F32, kind="ExternalInput").ap()
    bwq_d = dt("b_w_q", [2, D, D], F32, kind="ExternalInput").ap()
    bwout_d = dt("b_w_out", [2, D, D], F32, kind="ExternalInput").ap()
    w1_d = dt("mlp_w1", [DEPTH, D, 4 * D], F32, kind="ExternalInput").ap()
    w2_d = dt("mlp_w2", [DEPTH, 4 * D, D], F32, kind="ExternalInput").ap()
    fg_d = dt("fin_gT", [128, KC], F32, kind="ExternalInput").ap()
    cstf_d = dt("cstf", [128, 1024], F32, kind="ExternalInput").ap()
    cstb_d = dt("cstb", [128, 512], F32, kind="ExternalInput").ap()
    yT_d = dt("yT", [D, S], F32, kind="ExternalOutput").ap()
    kt_d = dt("kt_scr", [128, KC * S], BF16, kind="Internal").ap()
    v_d = dt("v_scr", [128, NSB * D], BF16, kind="Internal").ap()
    gk_d = dt("gk_scr", [B_H, 3, S], BF16, kind="Internal").ap()
    gq_d = dt("gq_scr", [B_H, 3, S], BF16, kind="Internal").ap()

    es = ExitStack()
    with es:
        def sb(name, shape, dtype):
            return es.enter_context(nc.sbuf_tensor(name, shape, dtype))

        XT = sb("XT", [128, KC, S], F32)
        HT = sb("HT", [128, KC, S], BF16)
        CSTF = sb("CSTF", [128, 1024], F32)
        CSTB = sb("CSTB", [128, 512], BF16)
        IDENT = CSTF[:, 0:128]
        E127 = CSTF[:, 128:256]
        ONESF = CSTF[:, 256:384]
        NEGMASK = CSTF[:, 896:1024]
        MASKB = CSTB[:, 0:128]
        ONESB = CSTB[:, 128:256]
        HALFB = [CSTB[:, 256:384], CSTB[:, 384:512]]
        MODC = sb("MODC", [128, DEPTH * 48], F32)
        ADAB = sb("ADAB", [128, DEPTH * 48], F32)
        CONDF = sb("CONDF", [128, KC], F32)
        CONDS = sb("CONDS", [128, KC], F32)
        CONDB = sb("CONDB", [128, KC], BF16)
        SMALL = sb("SMALL", [128, 64], F32)
        ABI = SMALL[0:4, 0:2]
        ABF = SMALL[0:4, 2:4]
        NABF = SMALL[0:4, 4:6]
        AHG = SMALL[:, 8:24]
        KVG = SMALL[:, 24:32]
        FING = SMALL[:, 32:40]
        FGB = SMALL[0:16, 40:41]
        NFGB = SMALL[0:16, 41:42]
        EPSC = SMALL[:, 48:49]
        ONEC = SMALL[:, 49:50]
        WA = [sb("WA%d" % i, [128, 4096], BF16) for i in range(2)]
        WB = [sb("WB%d" % i, [128, 4096], BF16) for i in range(2)]
        MIXW = 12800
        MIX = sb("MIX", [128, MIXW], F32)
        MIXB = MIX.bitcast(BF16)
        RS = [sb("RS%d" % i, [128, 512], F32) for i in range(2)]
        NTF = 6
        TF = [sb("TF%d" % i, [128, 512], F32) for i in range(NTF)]
        NTBT = 4
        TB = [sb("TB%d" % i, [128, 512], BF16) for i in range(NTBT)]
        CC = sb("CC", [128, 64], F32)
        PS = [es.enter_context(nc.psum_tensor("PS%d" % i, [128, 512], F32)) for i in range(8)]

        sc = Sched(nc, es)
        ADD_ENG = cfg.get('add_eng', 'pool')
        rot = {}

        def nxt(name, n):
            rot[name] = (rot.get(name, -1) + 1) % n
            return rot[name]

        def MM(out, lhsT, rhs, start, stop, reads, writes):
            return sc.op("pe", "matmul", dict(out=out, lhsT=lhsT, rhs=rhs, start=start, stop=stop), reads, writes, pe_accum=True)

        def TR(out, in_, ident, reads, writes):
            return sc.op("pe", "transpose", dict(out=out, in_=in_, identity=ident), reads, writes, pe_accum=True)

        def ACT(out, in_, func, reads, writes, bias=None, scale=None):
            kw = dict(out=out, in_=in_, func=func)
            if bias is not None:
                kw["bias"] = bias
            if scale is not None:
                kw["scale"] = scale
            return sc.op("act", "activation", kw, reads, writes)

        def TT(out, in0, in1, op, reads, writes):
            return sc.op("dve", "tensor_tensor", dict(out=out, in0=in0, in1=in1, op=op), reads, writes)

        def STT(out, in0, scalar, in1, op0, op1, reads, writes):
            return sc.op("dve", "scalar_tensor_tensor", dict(out=out, in0=in0, scalar=scalar, in1=in1, op0=op0, op1=op1), reads, writes)

        def TS(out, in0, s1, op0, reads, writes):
            return sc.op("dve", "tensor_scalar", dict(out=out, in0=in0, scalar1=s1, scalar2=None, op0=op0), reads, writes)

        def RECIP(out, in_, reads, writes):
            return sc.op("dve", "reciprocal", dict(out=out, in_=in_), reads, writes)

        def VCOPY(out, in_, reads, writes):
            return sc.op("dve", "tensor_copy", dict(out=out, in_=in_), reads, writes)

        def SCAN(out, d0, d1, op0, op1, reads, writes):
            return sc.op("dve", "tensor_tensor_scan", dict(out=out, data0=d0, data1=d1, initial=0.0, op0=op0, op1=op1), reads, writes)

        def wkeys(name, i):
            return ["%s%d_%d" % (name, i, q) for q in range(4)]

        def wload(slots, name, parts):
            i = nxt(name, len(slots))
            t = slots[i]
            keys = wkeys(name, i)
            ds = sc.dsem("%s%d" % (name, i))
            tok = None
            for n, (dv, src) in enumerate(parts):
                tok = sc.dma("pool", dv(t), src, ds, reads=[], writes=(keys if n == 0 else []))
            for k in keys:
                sc.last_w[k] = tok
            return t, keys

        def w_rows(src2d, r0, nr, c0, ncw):
            return src2d[r0:r0 + nr, c0:c0 + ncw].rearrange("(kc p) n -> p kc n", p=128)

        def sview(t, off, nk, ncw):
            return t[:, off:off + nk * ncw].rearrange("p (kc n) -> p kc n", kc=nk)

        def small_dma(out, in_, name, writes):
            sc.dma("sp", out, in_, sc.dsem(name), writes=writes)

        small_dma(CSTF[:], cstf_d, "cstf", ["CSTF"])
        sc.dma("pool", CSTB[:], cstb_d, sc.dsem("cstb"), writes=["CSTB"])
        small_dma(ADAB[:], adab_d, "adab", ["ADAB"])
        small_dma(CONDF[:], cT_d, "condf", ["CONDF"])
        small_dma(ABI, abi_d, "abi", ["ABI"])
        small_dma(ABF, abf_d, "abf", ["ABF"])
        small_dma(AHG, ahg_d, "ahg", ["AHG"])
        small_dma(KVG, kvg_d, "kvg", ["KVG"])
        small_dma(FING, fg_d, "fing", ["FING"])
        small_dma(FGB, bfgb_d, "fgb", ["FGB"])
        xkeys = lambda kc, tb: "XT%d_%d" % (kc, tb)
        hkeys = lambda kc, tb: "HT%d_%d" % (kc, tb)
        for kc in range(KC):
            sc.dma("sp", XT[:, kc, :], xT_d[kc * 128:(kc + 1) * 128, :], sc.dsem("x%d" % kc),
                   writes=[xkeys(kc, tb) for tb in range(NTB)])
        TS(NABF, ABF, -1.0, ALU.mult, ["ABF"], ["NABF"])
        TS(NFGB, FGB, -1.0, ALU.mult, ["FGB"], ["NFGB"])
        sc.op("dve", "memset", {}, [], ["EPSC"], args=(EPSC, EPS))
        sc.op("dve", "memset", {}, [], ["ONEC"], args=(ONEC, 1.0))
        ACT(CONDS[:], CONDF[:], AF.Sigmoid, ["CONDF"], ["CONDS"])
        TT(CONDB[:], CONDF[:], CONDS[:], ALU.mult, ["CONDF", "CONDS"], ["CONDB"])

        layers_used = sorted(set(l for (l, _, _) in cfg["layers"]))

        def mod_chunk(l, cg, wv, keys):
            for j in range(4):
                col = l * 48 + cg * 4 + j
                for kc in range(KC):
                    MM(PS[7][:, col:col + 1], wv[:, kc, j * 128:(j + 1) * 128], CONDB[:, kc:kc + 1],
                       kc == 0, kc == KC - 1, keys + ["CONDB"], ["PS7"])

        def mkeys(l, vs):
            return ["MODC%dv%d" % (l, v) for v in vs]

        def mod_finish_vec(l, v):
            a0 = l * 48 + v * 8
            TT(MODC[:, a0:a0 + 8], PS[7][:, a0:a0 + 8], ADAB[:, a0:a0 + 8], ALU.add, ["PS7", "ADAB"], mkeys(l, [v]))
            if v in (1, 4):
                TS(MODC[:, a0:a0 + 8], MODC[:, a0:a0 + 8], 1.0, ALU.add, mkeys(l, [v]), mkeys(l, [v]))

        def mod_finish(l):
            allk = mkeys(l, range(6))
            TT(MODC[:, l * 48:(l + 1) * 48], PS[7][:, l * 48:(l + 1) * 48], ADAB[:, l * 48:(l + 1) * 48], ALU.add,
               ["PS7", "ADAB"], allk)
            for v in (1, 4):
                a0 = l * 48 + v * 8
                TS(MODC[:, a0:a0 + 8], MODC[:, a0:a0 + 8], 1.0, ALU.add, mkeys(l, [v]), mkeys(l, [v]))

        mod_done = set()
        overlap_mod = cfg.get("overlap_mod", True)
        for l in (layers_used[:1] if overlap_mod else layers_used):
            for cg in range(12):
                t, keys = wload(WA, "WA", [(lambda t: sview(t, 0, 8, 512), w_rows(adaw_d[l], 0, D, cg * 512, 512))])
                mod_chunk(l, cg, sview(t, 0, 8, 512), keys)
                if cg % 2 == 1:
                    mod_finish_vec(l, cg // 2)
            mod_done.add(l)

        def modcol(l, v, kc):
            c = l * 48 + v * 8 + kc
            return MODC[:, c:c + 1]

        def tbs(tb):
            return slice(tb * 512, (tb + 1) * 512)

        def rstd_block(tb):
            for kc in range(KC):
                i = nxt("TB", NTBT)
                ACT(TB[i][:], XT[:, kc, tbs(tb)], AF.Square, [xkeys(kc, tb)], ["TB%d" % i])
                MM(PS[6][:], ONESB, TB[i][:], kc == 0, kc == KC - 1, ["TB%d" % i, "CSTB"], ["PS6"])
            r = nxt("RS", 2)
            ACT(RS[r][:], PS[6][:], AF.Ln, ["PS6", "EPSC"], ["RS%d" % r], bias=EPSC, scale=1.0 / D)
            ACT(RS[r][:], RS[r][:], AF.Exp, ["RS%d" % r], ["RS%d" % r], scale=-0.5)
            return r

        def make_norm(scale_col, shift_col, pkeys):
            def modulate(tb, r):
                for kc in range(KC):
                    i = nxt("TF", NTF)
                    STT(TF[i][:], XT[:, kc, tbs(tb)], scale_col(kc), RS[r][:], ALU.mult, ALU.mult,
                        [xkeys(kc, tb), "RS%d" % r] + pkeys, ["TF%d" % i])
                    if shift_col is not None:
                        ACT(HT[:, kc, tbs(tb)], TF[i][:], AF.Identity, ["TF%d" % i] + pkeys, [hkeys(kc, tb)],
                            bias=shift_col(kc), scale=1.0)
                    else:
                        ACT(HT[:, kc, tbs(tb)], TF[i][:], AF.Copy, ["TF%d" % i], [hkeys(kc, tb)])
            return rstd_block, modulate, ("std", scale_col, shift_col, pkeys)

        out_toks = []

        def final_norm_emitters():
            def modulate(tb, r):
                for kc in range(KC):
                    STT(XT[:, kc, tbs(tb)], XT[:, kc, tbs(tb)], FING[:, kc:kc + 1], RS[r][:], ALU.mult, ALU.mult,
                        [xkeys(kc, tb), "RS%d" % r, "FING"], [xkeys(kc, tb)])
                    out_toks.append(sc.dma("sp", yT_d[kc * 128:(kc + 1) * 128, tbs(tb)], XT[:, kc, tbs(tb)], sc.dsem("out"),
                                           reads=[xkeys(kc, tb)]))
            return rstd_block, modulate, ("final", None, None, [])

        def norm_pieces(em, tb, bank_fn, tmp_fn):
            kind, scale_col, shift_col, pkeys = em[2]
            st = {}
            pcs = []

            def stats_all():
                b = bank_fn()
                for kc in range(KC):
                    tq, tk_ = tmp_fn()
                    if kc % 2 == 0:
                        ACT(tq, XT[:, kc, tbs(tb)], AF.Square, [xkeys(kc, tb)], [tk_])
                    else:
                        TT(tq, XT[:, kc, tbs(tb)], XT[:, kc, tbs(tb)], ALU.mult, [xkeys(kc, tb)], [tk_])
                    MM(PS[b][:], ONESB, tq, kc == 0, kc == KC - 1, [tk_, "CSTB"], ["PS%d" % b])
                r = nxt("RS", 2)
                st["r"] = r
                ACT(RS[r][:], PS[b][:], AF.Ln, ["PS%d" % b, "EPSC"], ["RS%d" % r], bias=EPSC, scale=1.0 / D)
                ACT(RS[r][:], RS[r][:], AF.Exp, ["RS%d" % r], ["RS%d" % r], scale=-0.5)

            def mod(kc):
                r = st["r"]
                if kind == "final":
                    STT(XT[:, kc, tbs(tb)], XT[:, kc, tbs(tb)], FING[:, kc:kc + 1], RS[r][:], ALU.mult, ALU.mult,
                        [xkeys(kc, tb), "RS%d" % r, "FING"], [xkeys(kc, tb)])
                    out_toks.append(sc.dma("sp", yT_d[kc * 128:(kc + 1) * 128, tbs(tb)], XT[:, kc, tbs(tb)], sc.dsem("out"),
                                           reads=[xkeys(kc, tb)]))
                    return
                i = nxt("TF", NTF)
                STT(TF[i][:], XT[:, kc, tbs(tb)], scale_col(kc), RS[r][:], ALU.mult, ALU.mult,
                    [xkeys(kc, tb), "RS%d" % r] + pkeys, ["TF%d" % i])
                if shift_col is not None:
                    ACT(HT[:, kc, tbs(tb)], TF[i][:], AF.Identity, ["TF%d" % i] + pkeys, [hkeys(kc, tb)],
                        bias=shift_col(kc), scale=1.0)
                else:
                    ACT(HT[:, kc, tbs(tb)], TF[i][:], AF.Copy, ["TF%d" % i], [hkeys(kc, tb)])

            pcs.append(stats_all)
            for kc in range(KC):
                pcs.append(lambda kc=kc: mod(kc))
            return pcs

        def run_norm(em):
            stats, modulate = em[0], em[1]
            rr = {0: stats(0)}
            for tb in range(NTB):
                if tb + 1 < NTB:
                    rr[tb + 1] = stats(tb + 1)
                modulate(tb, rr[tb])

        def norm_to_HT(scale_col, shift_col, pkeys):
            run_norm(make_norm(scale_col, shift_col, pkeys))

        def norm_spec(l, which):
            if which == "mix":
                return make_norm(lambda kc: modcol(l, 1, kc), lambda kc: modcol(l, 0, kc), mkeys(l, [1, 0]))
            return make_norm(lambda kc: modcol(l, 4, kc), lambda kc: modcol(l, 3, kc), mkeys(l, [4, 3]))

        def acc_bank():
            return nxt("ACC", 2)

        def resid_update(b, l, gvec, dc, tb, mk):
            STT(XT[:, dc, tbs(tb)], PS[b][:], modcol(l, gvec, dc), XT[:, dc, tbs(tb)], ALU.mult, ALU.add,
                ["PS%d" % b, xkeys(dc, tb)] + mkeys(l, [gvec]), [xkeys(dc, tb)])

        def mlp(l, skip_norm=False, next_em=None):
            mk = ["MODC%d" % l]
            later = [x for x in layers_used if x > l and x not in mod_done]
            nl = later[0] if later else None
            if not skip_norm:
                run_norm(norm_spec(l, "mlp"))
            sc.fence(pool=(nl is not None))
            H1 = MIXB[:, 0:4 * S].rearrange("p (f t) -> p f t", f=4)
            MS = [MIXB[:, 8192 + i * 4096:8192 + (i + 1) * 4096] for i in range(2)]

            def ada_chunk(cg):
                i = nxt("MS", 2)
                keys = ["MS%d" % i]
                wv = MS[i].rearrange("p (kc n) -> p kc n", kc=8)
                sc.dma("pool", wv, w_rows(adaw_d[nl], 0, D, cg * 512, 512), sc.dsem("WMS%d" % i), reads=[], writes=keys)
                mod_chunk(nl, cg, wv, keys)

            for g in range(8):
                t1, k1 = wload(WA, "WA", [(lambda t: sview(t, 0, 8, 512), w_rows(w1_d[l], 0, D, g * 512, 512))])
                t2, k2 = wload(WB, "WB", [(lambda t: sview(t, 0, 4, 1024), w_rows(w2_d[l], g * 512, 512, 0, 1024))])
                w1v = sview(t1, 0, 8, 512)
                w2v = sview(t2, 0, 4, 1024)
                cgs = [c_ for c_ in (2 * g, 2 * g + 1) if c_ < 12] if nl is not None else []
                for tb in range(NTB):
                    for f in range(4):
                        b = nxt("ACC4", 4)
                        for kc in range(KC):
                            MM(PS[b][:], w1v[:, kc, f * 128:(f + 1) * 128], HT[:, kc, tbs(tb)], kc == 0, kc == KC - 1,
                               k1 + [hkeys(kc, tb)], ["PS%d" % b])
                        i = nxt("TF", NTF)
                        ACT(TF[i][:], PS[b][:], AF.Relu, ["PS%d" % b], ["TF%d" % i])
                        TT(H1[:, f, tbs(tb)], TF[i][:], TF[i][:], ALU.mult, ["TF%d" % i], ["H1_%d_%d" % (f, tb)])
                    if tb == 1 and cgs:
                        ada_chunk(cgs[0])
                for tb in range(NTB):
                    for dc in range(KC):
                        b = nxt("ACC4", 4)
                        for f in range(4):
                            MM(PS[b][:], w2v[:, f, dc * 128:(dc + 1) * 128], H1[:, f, tbs(tb)], f == 0, f == 3,
                               k2 + ["H1_%d_%d" % (f, tb)], ["PS%d" % b])
                        resid_update(b, l, 5, dc, tb, mk)
                    if tb == 1 and len(cgs) > 1:
                        ada_chunk(cgs[1])
                    if g == 7 and next_em is not None:
                        next_em[1](tb, next_em[0](tb))
                if g == 6 and nl is not None:
                    mod_finish(nl)
                    mod_done.add(nl)

        def mlstm(l, skip_norm=False, next_em=None):
            mk = ["MODC%d" % l]
            if not skip_norm:
                run_norm(norm_spec(l, "mix"))
            sc.fence()
            T_i = MIX[0:4, 0:2048]
            T_l = MIX[0:4, 2048:4096]
            T_F = MIX[0:4, 4096:6144]
            T_m = MIX[0:4, 6144:8192]
            EBt = MIX[:, 0:2048]
            ABt = MIX[:, 2048:4096]
            QH = MIXB[:, 16384:18432]
            KH = MIXB[:, 18432:20480]
            VH = MIXB[:, 20480:24576].rearrange("p (s e) -> p s e", s=16)
            AHt = MIXB[:, 24576:25600].rearrange("p (a t) -> p a t", a=2)
            rk = lambda r, tb: "R%d_%d" % (r, tb)
            allr = lambda r: [rk(r, tb) for tb in range(NTB)]
            tg, kg = wload(WA, "WA", [(lambda t: sview(t, 0, 8, 8), w_rows(awin_d[l], 0, D, 3072, 8))])
            wg = sview(tg, 0, 8, 8)
            for tb in range(NTB):
                b = acc_bank()
                for kc in range(KC):
                    MM(PS[b][0:4, :], wg[:, kc, 0:4], HT[:, kc, tbs(tb)], kc == 0, kc == KC - 1, kg + [hkeys(kc, tb)], ["PS%d" % b])
                ACT(T_i[:, tbs(tb)], PS[b][0:4, :], AF.Identity, ["PS%d" % b, "ABI"], [rk(0, tb)], bias=ABI[:, l:l + 1], scale=1.0)
                b = acc_bank()
                for kc in range(KC):
                    MM(PS[b][0:4, :], wg[:, kc, 4:8], HT[:, kc, tbs(tb)], kc == 0, kc == KC - 1, kg + [hkeys(kc, tb)], ["PS%d" % b])
                ACT(T_l[:, tbs(tb)], PS[b][0:4, :], AF.Exp, ["PS%d" % b, "NABF"], [rk(1, tb)], bias=NABF[:, l:l + 1], scale=-1.0)
            ACT(T_l, T_l, AF.Ln, allr(1) + ["ONEC"], allr(1), bias=ONEC[0:4, :], scale=1.0)
            TS(T_l, T_l, -1.0, ALU.mult, allr(1), allr(1))
            SCAN(T_F, T_l, T_l, ALU.add, ALU.min, allr(1), allr(2))
            SCAN(T_m, T_l, T_i, ALU.add, ALU.max, allr(1) + allr(0), allr(3))
            TT(T_i, T_i, T_F, ALU.subtract, allr(0) + allr(2), allr(0))
            TT(T_F, T_F, T_m, ALU.subtract, allr(2) + allr(3), allr(2))
            TS(T_m, T_m, -1.0, ALU.mult, allr(3), allr(3))
            for J in range(NSB):
                TR(PS[7][:, J * 4:J * 4 + 4], T_i[:, J * 128:(J + 1) * 128], CSTF[0:4, 0:4], [rk(0, J // 4), "CSTF"], ["PS7"])
            ACT(CC[:, 0:64], PS[7][:, 0:64], AF.Copy, ["PS7"], ["CC"])
            X8 = [MIX[:, k * 512:(k + 1) * 512] for k in range(8)]
            ABd, EBd = X8[0:2], X8[2:4]
            N0t, N1t, DAt, RSt = X8[4], X8[5], X8[6], X8[7]
            x8k = ["X8_%d" % k for k in range(8)]
            sc.fence()
            PFm = cfg.get("ml_pf", 3)
            NPTm = 4

            def acc2():
                return (0, 7)[nxt("ACC2", 2)]

            def head_weights(h):
                tA, kA = wload(WA, "WA", [
                    (lambda t: sview(t, 0, 8, 512)[:, :, 0:128], w_rows(awin_d[l], 0, D, h * 128, 128)),
                    (lambda t: sview(t, 0, 8, 512)[:, :, 128:256], w_rows(awin_d[l], 0, D, 512 + h * 128, 128)),
                    (lambda t: sview(t, 0, 8, 512)[:, :, 256:512], w_rows(awin_d[l], 0, D, 1024 + h * 256, 256))])
                tB, kB = wload(WB, "WB", [
                    (lambda t: sview(t, 0, 8, 256), w_rows(awin_d[l], 0, D, 2048 + h * 256, 256)),
                    (lambda t: sview(t, 2048, 2, 1024), w_rows(awout_d[l], h * 256, 256, 0, 1024))])
                return tA, kA, tB, kB

            hw = {0: head_weights(0)}
            bgm = []

            def sq_tmp_m():
                i = nxt("TF", NTF)
                return TF[i].bitcast(BF16)[:, 0:512], "TF%d" % i
            for h in range(A_H):
                tA, kA, tB, kB = hw[h]
                if h + 1 < A_H:
                    hw[h + 1] = head_weights(h + 1)
                wA = sview(tA, 0, 8, 512)
                wo = sview(tB, 0, 8, 256)
                wout = sview(tB, 2048, 2, 1024)
                oh = CSTF[0:4, 384 + h * 128:384 + (h + 1) * 128]

                def gen_ab(tb, h=h, oh=oh):
                    d = tb % 2
                    b = acc2()
                    MM(PS[b][:], oh, T_F[:, tbs(tb)], True, True, ["CSTF", rk(2, tb)], ["PS%d" % b])
                    ACT(ABd[d], PS[b][:], AF.Copy, ["PS%d" % b], [x8k[d]])
                    b = acc2()
                    MM(PS[b][:], oh, T_m[:, tbs(tb)], True, True, ["CSTF", rk(3, tb)], ["PS%d" % b])
                    ACT(EBd[d], PS[b][:], AF.Exp, ["PS%d" % b], [x8k[2 + d]])

                for tb in range(NTB):
                    b = acc2()
                    for kc in range(KC):
                        MM(PS[b][:], wA[:, kc, 0:128], HT[:, kc, tbs(tb)], kc == 0, kc == KC - 1, kA + [hkeys(kc, tb)], ["PS%d" % b])
                    ACT(QH[:, tbs(tb)], PS[b][:], AF.Identity, ["PS%d" % b], ["QH%d" % tb], scale=float(128 ** -0.5))
                    b = acc2()
                    for kc in range(KC):
                        MM(PS[b][:], wA[:, kc, 128:256], HT[:, kc, tbs(tb)], kc == 0, kc == KC - 1, kA + [hkeys(kc, tb)], ["PS%d" % b])
                    VCOPY(KH[:, tbs(tb)], PS[b][:], ["PS%d" % b], ["KH%d" % tb])
                for sbk in range(NSB):
                    b = acc2()
                    for kc in range(KC):
                        MM(PS[b][:, 0:256], HT[:, kc, sbk * 128:(sbk + 1) * 128], wA[:, kc, 256:512], kc == 0, kc == KC - 1,
                           kA + [hkeys(kc, sbk // 4)], ["PS%d" % b])
                    if sbk % 2 == 0:
                        VCOPY(VH[:, sbk, :], PS[b][:, 0:256], ["PS%d" % b], ["VH%d" % sbk])
                    else:
                        ACT(VH[:, sbk, :], PS[b][:, 0:256], AF.Copy, ["PS%d" % b], ["VH%d" % sbk])
                gen_ab(0)

                iters = [(tb, J) for tb in range(NTB) for J in range(4 * tb + 4)]
                state = {}
                pending = []

                def stageA(k, h=h):
                    tb, J = iters[k]
                    for _ in range(2):
                        if bgm:
                            bgm.pop(0)()
                    if J == PFm and tb + 1 < NTB:
                        gen_ab(tb + 1)
                    d = tb % 2
                    n0 = max(0, J - 4 * tb)
                    c0 = n0 * 128
                    st = 1 + nxt("ST", 3)
                    MM(PS[st][:, c0:512], KH[:, J * 128:(J + 1) * 128], QH[:, tb * 512 + c0:(tb + 1) * 512], True, True,
                       ["KH%d" % (J // 4), "QH%d" % tb], ["PS%d" % st])
                    wi = nxt("TF", NTF)
                    ACT(TF[wi][:, c0:512], ABd[d][:, c0:512], AF.Exp, [x8k[d], "CC"], ["TF%d" % wi],
                        bias=CC[:, J * 4 + h:J * 4 + h + 1], scale=1.0)
                    ai = nxt("TB", NPTm)
                    TT(TB[ai][:, c0:512], PS[st][:, c0:512], TF[wi][:, c0:512], ALU.mult, ["PS%d" % st, "TF%d" % wi], ["TB%d" % ai])
                    if J >= 4 * tb:
                        TT(TB[ai][:, c0:c0 + 128], TB[ai][:, c0:c0 + 128], MASKB, ALU.mult, ["TB%d" % ai, "CSTB"], ["TB%d" % ai])
                    state[k] = (ai, c0)

                def post2(tb, h=h, wo=wo, wout=wout, kB=kB):
                    ACT(DAt, DAt, AF.Ln, [x8k[6]], [x8k[6]])
                    ACT(DAt, DAt, AF.Exp, [x8k[6]], [x8k[6]], scale=-1.0)
                    sq = []
                    for e2, Nt, nk in ((0, N0t, x8k[4]), (1, N1t, x8k[5])):
                        sc.op("pool", "tensor_tensor", dict(out=Nt, in0=Nt, in1=DAt, op=ALU.mult), [nk, x8k[6]], [nk])
                        si = nxt("TF", NTF)
                        sc.op("pool", "tensor_tensor", dict(out=TF[si][:], in0=Nt, in1=Nt, op=ALU.mult), [nk], ["TF%d" % si])
                        sq.append(si)
                    gis = []
                    for e2 in range(2):
                        b = acc2()
                        for kc in range(KC):
                            MM(PS[b][:], wo[:, kc, e2 * 128:(e2 + 1) * 128], HT[:, kc, tbs(tb)], kc == 0, kc == KC - 1,
                               kB + [hkeys(kc, tb)], ["PS%d" % b])
                        gi = nxt("TF", NTF)
                        ACT(TF[gi][:], PS[b][:], AF.Sigmoid, ["PS%d" % b], ["TF%d" % gi])
                        gis.append(gi)
                    b = acc2()
                    for e2 in range(2):
                        MM(PS[b][:], ONESF, TF[sq[e2]][:], e2 == 0, e2 == 1, ["CSTF", "TF%d" % sq[e2]], ["PS%d" % b])
                    ACT(RSt, PS[b][:], AF.Ln, ["PS%d" % b, "EPSC"], [x8k[7]], bias=EPSC, scale=1.0 / 256)
                    ACT(RSt, RSt, AF.Exp, [x8k[7]], [x8k[7]], scale=-0.5)
                    for e2, Nt, nk in ((0, N0t, x8k[4]), (1, N1t, x8k[5])):
                        gi = gis[e2]
                        gc = l * 8 + h * 2 + e2
                        STT(Nt, Nt, AHG[:, gc:gc + 1], RSt, ALU.mult, ALU.mult, [nk, "AHG", x8k[7]], [nk])
                        sc.op("pool", "tensor_tensor", dict(out=AHt[:, e2, :], in0=Nt, in1=TF[gi][:], op=ALU.mult),
                              [nk, "TF%d" % gi], ["AH%d" % e2])
                    pending.append([7, ("out", tb)])

                def post3(tb, h=h, wout=wout, kB=kB):
                    for dc in range(KC):
                        b = acc2()
                        MM(PS[b][:], wout[:, 0, dc * 128:(dc + 1) * 128], AHt[:, 0, :], True, False, kB + ["AH0"], ["PS%d" % b])
                        MM(PS[b][:], wout[:, 1, dc * 128:(dc + 1) * 128], AHt[:, 1, :], False, True, kB + ["AH1"], ["PS%d" % b])
                        if dc % 2 == 0 or not cfg.get("split_evac", False):
                            resid_update(b, l, 2, dc, tb, mk)
                        else:
                            ui = nxt("TF", NTF)
                            ACT(TF[ui][:], PS[b][:], AF.Identity, ["PS%d" % b] + mkeys(l, [2]), ["TF%d" % ui], scale=modcol(l, 2, dc))
                            sc.op("pool", "tensor_tensor", dict(out=XT[:, dc, tbs(tb)], in0=XT[:, dc, tbs(tb)], in1=TF[ui][:], op=ALU.add),
                                  ["TF%d" % ui, xkeys(dc, tb)], [xkeys(dc, tb)])
                    if h == A_H - 1 and next_em is not None:
                        bgm.extend(norm_pieces(next_em, tb, acc2, sq_tmp_m))

                def stageB(k):
                    tb, J = iters[k]
                    ai, c0 = state.pop(k)
                    last = 4 * tb + 3
                    MM(PS[4][:, c0:512], VH[:, J, 0:128], TB[ai][:, c0:512], J == 0, J == last, ["VH%d" % J, "TB%d" % ai], ["PS4"])
                    MM(PS[5][:, c0:512], VH[:, J, 128:256], TB[ai][:, c0:512], J == 0, J == last, ["VH%d" % J, "TB%d" % ai], ["PS5"])
                    MM(PS[6][:, c0:512], ONESB, TB[ai][:, c0:512], J == 0, J == last, ["CSTB", "TB%d" % ai], ["PS6"])
                    if J == last:
                        ACT(DAt, PS[6][:], AF.Abs, ["PS6"], [x8k[6]])
                        VCOPY(N0t, PS[4][:], ["PS4"], [x8k[4]])
                        ACT(N1t, PS[5][:], AF.Copy, ["PS5"], [x8k[5]])
                        TT(DAt, DAt, EBd[tb % 2], ALU.max, [x8k[6], x8k[2 + tb % 2]], [x8k[6]])
                        pending.append([3, tb])
                    for p in pending:
                        p[0] -= 1
                    while pending and pending[0][0] <= 0:
                        _, a = pending.pop(0)
                        run_post(a)

                def run_post(a):
                    if isinstance(a, tuple):
                        post3(a[1])
                    else:
                        post2(a)

                n = len(iters)
                for k in range(min(PFm, n)):
                    stageA(k)
                for k in range(n):
                    if k + PFm < n:
                        stageA(k + PFm)
                    stageB(k)
                while pending:
                    _, a = pending.pop(0)
                    run_post(a)
                while bgm:
                    bgm.pop(0)()

        def kv_norm_em():
            return make_norm(lambda kc: KVG[:, kc:kc + 1], None, ["KVG"])

        def fox_kv(skip_norm=False):
            if not skip_norm:
                run_norm(kv_norm_em())
            sc.fence()
            FGr = MIX[0:16, 0:2048]
            Gr = MIX[0:16, 2048:4096]
            KTs = [MIXB[:, 8192:10240], MIXB[:, 10240:12288]]
            tf_, kf = wload(WA, "WA", [(lambda t: sview(t, 0, 8, 16), w_rows(bwkv_d, 0, D, 2048, 16))])
            wf = sview(tf_, 0, 8, 16)
            for tb in range(NTB):
                b = acc_bank()
                for kc in range(KC):
                    MM(PS[b][0:16, :], wf[:, kc, 0:16], HT[:, kc, tbs(tb)], kc == 0, kc == KC - 1, kf + [hkeys(kc, tb)], ["PS%d" % b])
                ACT(FGr[:, tbs(tb)], PS[b][0:16, :], AF.Exp, ["PS%d" % b, "NFGB"], ["FGr"], bias=NFGB, scale=-1.0)
            ACT(FGr, FGr, AF.Ln, ["FGr", "ONEC"], ["FGr"], bias=ONEC[0:16, :], scale=1.0)
            SCAN(Gr, FGr, FGr, ALU.add, ALU.max, ["FGr"], ["Gr"])
            GHt = [MIXB[0:16, 12288 + k * 2048:12288 + (k + 1) * 2048] for k in range(3)]
            NGt = [MIXB[0:16, 18432 + k * 2048:18432 + (k + 1) * 2048] for k in range(3)]
            R1 = FGr
            VCOPY(GHt[0], Gr, ["Gr"], ["GH0"])
            TT(R1, Gr, GHt[0], ALU.subtract, ["Gr", "GH0", "FGr"], ["FGr"])
            VCOPY(GHt[1], R1, ["FGr"], ["GH1"])
            TT(R1, R1, GHt[1], ALU.subtract, ["FGr", "GH1"], ["FGr"])
            VCOPY(GHt[2], R1, ["FGr"], ["GH2"])
            for k in range(3):
                TS(NGt[k], GHt[k], -1.0, ALU.mult, ["GH%d" % k], ["NG%d" % k])
                sc.dma("sp", gk_d[:, k, :], GHt[k], sc.dsem("gh%d" % k), reads=["GH%d" % k], writes=["GKD"])
                sc.dma("sp", gq_d[:, k, :], NGt[k], sc.dsem("ng%d" % k), reads=["NG%d" % k], writes=["GQD"])
            for c in range(KC):
                tk, kk = wload(WA, "WA", [(lambda t: sview(t, 0, 8, 128), w_rows(bwkv_d, 0, D, c * 128, 128))])
                wk = sview(tk, 0, 8, 128)
                kt = KTs[c % 2]
                for tb in range(NTB):
                    b = acc_bank()
                    for kc in range(KC):
                        MM(PS[b][:], wk[:, kc, :], HT[:, kc, tbs(tb)], kc == 0, kc == KC - 1, kk + [hkeys(kc, tb)], ["PS%d" % b])
                    ACT(kt[:, tbs(tb)], PS[b][:], AF.Identity, ["PS%d" % b], ["KTs%d" % (c % 2)], scale=0.125)
                sc.dma("sp", kt_d[:, c * S:(c + 1) * S], kt, sc.dsem("kts%d" % (c % 2)), reads=["KTs%d" % (c % 2)], writes=["KTD%d" % c])
            vdv = v_d.rearrange("p (c s n) -> p c s n", c=8, s=16)
            for half in range(2):
                tv, kv = wload(WB, "WB", [(lambda t: sview(t, 0, 8, 512), w_rows(bwkv_d, 0, D, 1024 + half * 512, 512))])
                wv = sview(tv, 0, 8, 512)
                for sbk in range(NSB):
                    b = acc_bank()
                    for kc in range(KC):
                        MM(PS[b][:], HT[:, kc, sbk * 128:(sbk + 1) * 128], wv[:, kc, :], kc == 0, kc == KC - 1,
                           kv + [hkeys(kc, sbk // 4)], ["PS%d" % b])
                    vi = nxt("TB", NTBT)
                    VCOPY(TB[vi][:], PS[b][:], ["PS%d" % b], ["TB%d" % vi])
                    sc.dma("sp", vdv[:, half * 4:(half + 1) * 4, sbk, :], TB[vi][:].rearrange("p (c n) -> p c n", c=4),
                           sc.dsem("tb%d" % vi), reads=["TB%d" % vi], writes=["VD%d_%d" % (half, sbk)])

        def fox(l, j, skip_norm=False, next_em=None):
            mk = ["MODC%d" % l]
            if not skip_norm:
                run_norm(norm_spec(l, "mix"))
            sc.fence()
            KX = [[MIXB[:, (2 * ks + hh) * 2048:(2 * ks + hh + 1) * 2048] for hh in range(2)] for ks in range(2)]
            VX = [[MIXB[:, 8192 + (2 * ks + hh) * 2048:8192 + (2 * ks + hh + 1) * 2048].rearrange("p (s n) -> p s n", s=16)
                   for hh in range(2)] for ks in range(2)]
            QX = [[MIXB[:, 16384 + (2 * ks + hh) * 512:16384 + (2 * ks + hh + 1) * 512] for hh in range(2)] for ks in range(2)]
            OTS = [MIXB[:, 18432 + c * 512:18432 + (c + 1) * 512] for c in range(KC)]
            NPT = 8
            PT = [TB[k][:] for k in range(4)] + [MIXB[:, 22528 + k * 512:22528 + (k + 1) * 512] for k in range(4)]
            AUG = [64, 0]
            ptk = ["TB%d" % k for k in range(4)] + ["PTm%d" % k for k in range(4)]
            NSF = 5
            SF = [TF[k][:] for k in range(5)]
            LT = TF[5]
            PF = cfg.get("fox_pf", 3)
            vdkeys = ["VD%d_%d" % (hf, s_) for hf in range(2) for s_ in range(NSB)]
            for ks in range(2):
                for hh in range(2):
                    o = 64 * (1 - hh)
                    sc.op("dve", "memset", {}, [], ["VX%d_%d" % (ks, hh)], args=(VX[ks][hh][:, :, o:o + 64], 0.0))
                    sc.op("dve", "memset", {}, [], ["QXz%d_%d" % (ks, hh)], args=(QX[ks][hh][o:o + 64, :], 0.0))
                    sc.op("dve", "memset", {}, [], ["QXz%d_%d" % (ks, hh)], args=(QX[ks][hh][AUG[hh]:AUG[hh] + 6, :], 1.0))
                    sc.op("dve", "memset", {}, [], ["KXz%d_%d" % (ks, hh)], args=(KX[ks][hh][:, :], 0.0))
                    sc.op("dve", "memset", {}, [], ["KXz%d_%d" % (ks, hh)], args=(KX[ks][hh][AUG[hh]:AUG[hh] + 6, :], 1.0))
            wq_s, wo_s = [], []
            for i in range(2):
                t, k = wload(WA, "WA", [(lambda t: sview(t, 0, 8, 512), w_rows(bwq_d[j], 0, D, i * 512, 512))])
                wq_s.append((sview(t, 0, 8, 512), k))
            for i in range(2):
                t, k = wload(WB, "WB", [(lambda t: sview(t, 0, 4, 1024), w_rows(bwout_d[j], i * 512, 512, 0, 1024))])
                wo_s.append((sview(t, 0, 4, 1024), k))

            units = [(tb, c) for tb in range(NTB) for c in range(KC)]

            def loads(u, part="all"):
                tb, c = units[u]
                ks = u % 2
                nJ = 4 * tb + 4
                for hh in (range(2) if part in ("all", "k") else []):
                    h = 2 * c + hh
                    pp = slice(hh * 64, (hh + 1) * 64)
                    sc.dma("sp", KX[ks][hh][pp, 0:nJ * 128], kt_d[pp, c * S:c * S + nJ * 128], sc.dsem("kx%d_%d" % (ks, hh)),
                           reads=["KTD%d" % c, "KXz%d_%d" % (ks, hh)], writes=["KX%d_%d" % (ks, hh)])
                    a0 = AUG[hh]
                    sc.dma("sp", KX[ks][hh][a0:a0 + 3, 0:nJ * 128], gk_d[h, :, 0:nJ * 128], sc.dsem("kxa%d_%d" % (ks, hh)),
                           reads=["GKD", "KXz%d_%d" % (ks, hh)], writes=["KXa%d_%d" % (ks, hh)])
                    sc.dma("sp", QX[ks][hh][a0 + 3:a0 + 6, :], gq_d[h, :, tb * 512:(tb + 1) * 512], sc.dsem("qxa%d_%d" % (ks, hh)),
                           reads=["GQD", "QXz%d_%d" % (ks, hh)], writes=["QXa%d_%d" % (ks, hh)])
                vsrc = v_d[:, c * 2048:(c + 1) * 2048].rearrange("p (s n) -> p s n", s=16)
                for hh in (range(2) if part in ("all", "v") else []):
                    sc.dma("sp", VX[ks][hh][:, 0:nJ, hh * 64:(hh + 1) * 64], vsrc[:, 0:nJ, hh * 64:(hh + 1) * 64],
                           sc.dsem("vx%d_%d" % (ks, hh)), reads=vdkeys, writes=["VX%d_%d" % (ks, hh)])

            def qproj(u):
                tb, c = units[u]
                ks = u % 2
                wq, kq = wq_s[c // 4]
                cc = (c % 4) * 128
                for kc in range(KC):
                    MM(PS[0][:], wq[:, kc, cc:cc + 128], HT[:, kc, tbs(tb)], kc == 0, kc == KC - 1, kq + [hkeys(kc, tb)], ["PS0"])
                ACT(QX[ks][0][0:64, :], PS[0][0:64, :], AF.Copy, ["PS0"], ["QX%d_0" % ks])
                VCOPY(QX[ks][1][64:128, :], PS[0][64:128, :], ["PS0"], ["QX%d_1" % ks])

            iters = []
            for u, (tb, c) in enumerate(units):
                n_u = 2 * (4 * tb + 4)
                li = 0
                for J in range(4 * tb + 4):
                    for hh in range(2):
                        iters.append((u, J, hh, li))
                        li += 1
            state = {}
            pending = []

            bg = []

            def sq_tmp():
                i = nxt("SF", NSF)
                return TF[i].bitcast(BF16)[:, 0:512], "TF%d" % i

            def stageA(k):
                u, J, hh, li = iters[k]
                tb, c = units[u]
                ks = u % 2
                if bg:
                    bg.pop(0)()
                if li == 1 and u + 1 < len(units):
                    qproj(u + 1)
                if li == 0 and u + 1 < len(units):
                    loads(u + 1, "k")
                if li == PF + 2 and u + 1 < len(units):
                    loads(u + 1, "v")
                n0 = max(0, J - 4 * tb)
                c0 = n0 * 128
                nb = 4 - n0
                st = 1 + nxt("ST", 3)
                MM(PS[st][:, c0:512], KX[ks][hh][:, J * 128:(J + 1) * 128], QX[ks][hh][:, c0:512], True, True,
                   ["KX%d_%d" % (ks, hh), "KXa%d_%d" % (ks, hh), "KXz%d_%d" % (ks, hh),
                    "QX%d_%d" % (ks, hh), "QXa%d_%d" % (ks, hh), "QXz%d_%d" % (ks, hh)], ["PS%d" % st])
                pi = nxt("PT", NPT)
                if J >= 4 * tb:
                    sf = nxt("SF", NSF)
                    TT(SF[sf][:, 0:128], PS[st][:, c0:c0 + 128], NEGMASK, ALU.add, ["PS%d" % st, "CSTF"], ["TF%d" % sf])
                    ACT(PT[pi][:, c0:c0 + 128], SF[sf][:, 0:128], AF.Exp, ["TF%d" % sf], [ptk[pi]])
                    if nb > 1:
                        ACT(PT[pi][:, c0 + 128:512], PS[st][:, c0 + 128:512], AF.Exp, ["PS%d" % st], [ptk[pi]])
                else:
                    ACT(PT[pi][:, c0:512], PS[st][:, c0:512], AF.Exp, ["PS%d" % st], [ptk[pi]])
                state[k] = (pi, c0)

            def outproj(tb):
                for dc in range(KC):
                    for c in range(KC):
                        wo, ko = wo_s[c // 4]
                        MM(PS[0][:], wo[:, c % 4, dc * 128:(dc + 1) * 128], OTS[c], c == 0, c == KC - 1, ko + ["OTS%d" % c], ["PS0"])
                    resid_update(0, l, 2, dc, tb, mk)
                if next_em is not None:
                    bg.extend(norm_pieces(next_em, tb, lambda: 1 + nxt("ST", 3), sq_tmp))

            def stageB(k):
                u, J, hh, li = iters[k]
                tb, c = units[u]
                ks = u % 2
                pi, c0 = state.pop(k)
                last = 4 * tb + 3
                first = (J == 0 and hh == 0)
                final = (J == last and hh == 1)
                bo = 4 + 2 * (u % 2)
                MM(PS[bo][:, c0:512], VX[ks][hh][:, J, :], PT[pi][:, c0:512], first, final,
                   ["VX%d_%d" % (ks, hh), ptk[pi]], ["PS%d" % bo])
                MM(PS[bo + 1][:, c0:512], HALFB[hh], PT[pi][:, c0:512], first, final, ["CSTB", ptk[pi]], ["PS%d" % (bo + 1)])
                if final:
                    RECIP(LT[:], PS[bo + 1][:], ["PS%d" % (bo + 1)], ["TF5"])
                    TT(OTS[c], PS[bo][:], LT[:], ALU.mult, ["PS%d" % bo, "TF5"], ["OTS%d" % c])
                    if c == KC - 1:
                        pending.append([3, tb])
                for p in pending:
                    p[0] -= 1
                while pending and pending[0][0] <= 0:
                    _, a = pending.pop(0)
                    outproj(a)

            loads(0)
            qproj(0)
            n = len(iters)
            for k in range(min(PF, n)):
                stageA(k)
            for k in range(n):
                if k + PF < n:
                    stageA(k + PF)
                stageB(k)
            while pending:
                _, a = pending.pop(0)
                outproj(a)
            while bg:
                bg.pop(0)()

        sc.fence()
        phases = []
        kv_done = False
        for (l, do_mixer, do_mlp) in cfg["layers"]:
            if do_mixer:
                if l < N_A:
                    phases.append(("mlstm", l))
                else:
                    if not kv_done:
                        phases.append(("kv", l))
                        kv_done = True
                    phases.append(("fox", l))
            if do_mlp:
                phases.append(("mlp", l))
        if cfg.get("final_norm", True):
            phases.append(("final", None))
        fuse_norm = cfg.get("fuse_norm", True)
        prenormed = False
        for pi_, (kind, l) in enumerate(phases):
            if l is not None and l not in mod_done:
                for cg in range(12):
                    t, keys = wload(WA, "WA", [(lambda t: sview(t, 0, 8, 512), w_rows(adaw_d[l], 0, D, cg * 512, 512))])
                    mod_chunk(l, cg, sview(t, 0, 8, 512), keys)
                mod_finish(l)
                mod_done.add(l)
            skip = prenormed
            prenormed = False
            mix_next = None
            if fuse_norm and cfg.get("fuse_" + kind, kind == "fox") and kind in ("mlstm", "fox") and pi_ + 1 < len(phases):
                nk, nl_ = phases[pi_ + 1]
                if nk == "mlp" and nl_ in mod_done:
                    mix_next = norm_spec(nl_, "mlp")
                elif nk == "final":
                    mix_next = final_norm_emitters()
            if kind == "mlstm":
                mlstm(l, skip_norm=skip, next_em=mix_next)
                prenormed = mix_next is not None
            elif kind == "kv":
                fox_kv(skip_norm=skip)
            elif kind == "fox":
                fox(l, l - N_A, skip_norm=skip, next_em=mix_next)
                prenormed = mix_next is not None
            elif kind == "mlp":
                next_em = None
                if fuse_norm and pi_ + 1 < len(phases):
                    nk, nl_ = phases[pi_ + 1]
                    if nk in ("mlstm", "fox"):
                        next_em = norm_spec(nl_, "mix")
                    elif nk == "kv":
                        next_em = kv_norm_em()
                    elif nk == "mlp":
                        next_em = norm_spec(nl_, "mlp")
                    elif nk == "final":
                        next_em = final_norm_emitters()
                    if nl_ is not None and nl_ not in mod_done and not (nl_ > l and all(x <= l or x >= nl_ for x in layers_used)):
                        next_em = None
                mlp(l, skip_norm=skip, next_em=next_em)
                prenormed = next_em is not None
            elif kind == "final":
                if not skip:
                    run_norm(final_norm_emitters())

        if not cfg.get("final_norm", True):
            for kc in range(KC):
                out_toks.append(sc.dma("sp", yT_d[kc * 128:(kc + 1) * 128, :], XT[:, kc, :], sc.dsem("out"),
                                       reads=[xkeys(kc, tb) for tb in range(NTB)]))
        sc.wait_all("sp", [out_toks[-1]])
        sc.emit()
    return nc


def _prep_inputs(inputs):
    f = lambda a: np.ascontiguousarray(np.asarray(a, dtype=np.float32))
    x = f(inputs["x"])
    c = f(inputs["c"])
    B = x.shape[0]
    shared = {}
    shared["ada_w"] = f(inputs["ada_w"])
    ab = f(inputs["ada_b"])
    shared["ada_bT"] = f(ab.reshape(DEPTH, 48, 128).transpose(2, 0, 1).reshape(128, DEPTH * 48))
    shared["a_w_in"] = f(inputs["a_w_in"])
    shared["a_b_iT"] = f(f(inputs["a_b_i"]).T)
    shared["a_b_fT"] = f(f(inputs["a_b_f"]).T)
    hg = f(inputs["a_head_gain"])
    shared["a_hgT"] = f(hg.reshape(N_A, A_H, 2, 128).transpose(3, 0, 1, 2).reshape(128, N_A * 8))
    shared["a_w_out"] = f(inputs["a_w_out"])
    shared["kv_gT"] = f(f(inputs["kv_gain"]).reshape(KC, 128).T)
    shared["b_w_kv"] = f(inputs["b_w_kv"])
    shared["b_fgbT"] = f(f(inputs["b_fg_bias"]).reshape(B_H, 1))
    shared["b_w_q"] = f(inputs["b_w_q"])
    shared["b_w_out"] = f(inputs["b_w_out"])
    shared["mlp_w1"] = f(inputs["mlp_w1"])
    shared["mlp_w2"] = f(inputs["mlp_w2"])
    shared["fin_gT"] = f(f(inputs["final_gain"]).reshape(KC, 128).T)
    cstf = np.zeros((128, 1024), np.float32)
    cstf[:, 896:1024] = -30000.0 * np.tril(np.ones((128, 128)), -1)
    cstf[:, 0:128] = np.eye(128)
    cstf[127, 128:256] = 1.0
    cstf[:, 256:384] = 1.0
    for h in range(4):
        cstf[h, 384 + h * 128:384 + (h + 1) * 128] = 1.0
    shared["cstf"] = cstf
    cstb = np.zeros((128, 512), np.float32)
    cstb[:, 0:128] = np.triu(np.ones((128, 128)))
    cstb[:, 128:256] = 1.0
    cstb[:, 256:320] = 1.0
    cstb[:, 448:512] = 1.0
    shared["cstb"] = cstb
    in_maps = []
    for b in range(B):
        m = dict(shared)
        m["xT"] = f(x[b].T)
        m["cT"] = f(c[b].reshape(KC, 128).T)
        in_maps.append(m)
    return in_maps


FULL_CFG = dict(layers=[(l, True, True) for l in range(DEPTH)], final_norm=True)


def run_cfg(inputs, cfg, cores=None, trace=False):
    in_maps = _prep_inputs(inputs)
    if cores is not None:
        in_maps = [in_maps[i] for i in cores]
    nc = build_program(cfg)
    res = run_bass_kernel_spmd(nc, in_maps, core_ids=list(range(len(in_maps))), trace=trace)
    outs = [np.ascontiguousarray(r["yT"].T) for r in res.results]
    out = np.stack(outs, axis=0).astype(np.float32)
    if trace:
        return out, res
    return out


def kernel(**inputs):
    return run_cfg(inputs, FULL_CFG)
```
